# Optimizing a Trainium2 kernel written in Bass

```python
import functools
import jax, jax.numpy as jnp
from jax import lax
import numpy as np

D_MODEL = 1024
BATCH = 8
SEQ = 2048
DEPTH = 2
DEC_BATCH = 32
DEC_SEQ = 32
PAST_LEN = 4096

CHUNK = 64
D_MIX = D_MODEL
N_GROUPS = 4
G = D_MIX // N_GROUPS
HEAD_DIM = 64
N_HEADS_ATT = G // HEAD_DIM
N_HEADS_MEM = 4
MEM_HEAD_DIM = G // N_HEADS_MEM
N_MEM = 256
CONV_A_WIDTH = 3
CONV_B_WIDTH = 31
FFN_CONV_WIDTH = 3
N_PREV_CHUNKS = 8
BAND_CHUNKS = N_PREV_CHUNKS + 1
ATTN_WINDOW = N_PREV_CHUNKS * CHUNK
REL_CLIP = 128
D_FF = ((8 * D_MODEL // 3 + 127) // 128) * 128
D_IN_PROJ = 9 * G
RMS_EPS = 1e-6
LN_EPS = 1e-5
NEG_INF = -1e30

kernel_name = "hymba_style_streaming_conv_band_attn_encoder_step"


def rmsnorm(x, g):
    xf = x.astype(jnp.float32)
    y = xf * lax.rsqrt(jnp.mean(xf * xf, axis=-1, keepdims=True) + RMS_EPS)
    return (y * g.astype(jnp.float32)).astype(x.dtype)


def layernorm(x, g, b):
    xf = x.astype(jnp.float32)
    mu = jnp.mean(xf, axis=-1, keepdims=True)
    xc = xf - mu
    y = xc * lax.rsqrt(jnp.mean(xc * xc, axis=-1, keepdims=True) + LN_EPS)
    return (y * g.astype(jnp.float32) + b.astype(jnp.float32)).astype(x.dtype)


def causal_dwconv(x, prev, w, b=None):
    width, ch = w.shape
    xp = jnp.concatenate([prev.astype(x.dtype), x], axis=1)
    y = lax.conv_general_dilated(xp, w[:, None, :].astype(x.dtype), window_strides=(1,), padding="VALID",
                                 dimension_numbers=("NWC", "WIO", "NWC"), feature_group_count=ch)
    if b is not None:
        y = y + b.astype(x.dtype)
    return y, xp[:, xp.shape[1] - (width - 1):]


def rel_bias_table(rel_bias, dist):
    idx = jnp.clip(dist, -REL_CLIP, REL_CLIP) + REL_CLIP
    return rel_bias[:, idx].astype(jnp.float32)


def band_attention_prompt(q, k, v, rel_bias):
    bsz, t, h, dh = q.shape
    nc = t // CHUNK
    qc = q.reshape(bsz, nc, CHUNK, h, dh)
    pad = ((0, 0), (N_PREV_CHUNKS, 0), (0, 0), (0, 0), (0, 0))
    kc = jnp.pad(k.reshape(bsz, nc, CHUNK, h, dh), pad)
    vc = jnp.pad(v.reshape(bsz, nc, CHUNK, h, dh), pad)
    kb = jnp.concatenate([kc[:, j:j + nc] for j in range(BAND_CHUNKS)], axis=2)
    vb = jnp.concatenate([vc[:, j:j + nc] for j in range(BAND_CHUNKS)], axis=2)
    s = jnp.einsum("bcqhd,bckhd->bchqk", qc, kb).astype(jnp.float32) * (dh ** -0.5)
    qi = jnp.arange(CHUNK)
    kk = jnp.arange(BAND_CHUNKS * CHUNK)
    dist = (N_PREV_CHUNKS * CHUNK + qi[:, None]) - kk[None, :]
    s = s + rel_bias_table(rel_bias, dist)[None, None]
    valid = (jnp.arange(nc)[:, None] + (kk // CHUNK)[None, :]) >= N_PREV_CHUNKS
    s = jnp.where(valid[None, :, None, None, :], s, NEG_INF)
    p = jax.nn.softmax(s, axis=-1).astype(v.dtype)
    o = jnp.einsum("bchqk,bckhd->bcqhd", p, vb)
    return o.reshape(bsz, t, h, dh)


def band_attention_sample(q, k, v, rel_bias, k_cache, v_cache):
    dh = q.shape[-1]
    t = q.shape[1]
    l = k_cache.shape[1]
    kk = jnp.concatenate([k_cache.astype(k.dtype), k], axis=1)
    vv = jnp.concatenate([v_cache.astype(v.dtype), v], axis=1)
    s = jnp.einsum("bqhd,bkhd->bhqk", q, kk).astype(jnp.float32) * (dh ** -0.5)
    dist = (l + jnp.arange(t)[:, None]) - jnp.arange(l + t)[None, :]
    s = s + rel_bias_table(rel_bias, dist)[None]
    p = jax.nn.softmax(s, axis=-1).astype(v.dtype)
    return jnp.einsum("bhqk,bkhd->bqhd", p, vv)


def mem_attention(q, mk, mv):
    s = jnp.einsum("bqhd,bmhd->bhqm", q, mk.astype(q.dtype)).astype(jnp.float32) * (q.shape[-1] ** -0.5)
    p = jax.nn.softmax(s, axis=-1).astype(q.dtype)
    return jnp.einsum("bhqm,bmhd->bqhd", p, mv.astype(q.dtype))


def layer(x, mk, mv, prev_a, prev_b, prev_f, attend,
          norm1_g, w_in, conv_a_w, conv_b_w, conv_b_bias, ln_b_g, ln_b_b, rel_bias,
          grp_norm_g, w_out, norm2_g, w_up, ffn_conv_w, ffn_conv_b, w_down):
    bsz, t, _ = x.shape
    h = rmsnorm(x, norm1_g)
    z = h @ w_in
    a_b, a_c, a_h, b_u, b_g, c_q, c_k, c_v, m_q = jnp.split(z, 9, axis=-1)
    ya, new_a = causal_dwconv(a_c * a_h, prev_a, conv_a_w)
    ya = a_b * ya
    yb, new_b = causal_dwconv(b_u * jax.nn.sigmoid(b_g), prev_b, conv_b_w, conv_b_bias)
    yb = jax.nn.silu(layernorm(yb, ln_b_g, ln_b_b))
    q = c_q.reshape(bsz, t, N_HEADS_ATT, HEAD_DIM)
    k = c_k.reshape(bsz, t, N_HEADS_ATT, HEAD_DIM)
    v = c_v.reshape(bsz, t, N_HEADS_ATT, HEAD_DIM)
    yc = attend(q, k, v, rel_bias).reshape(bsz, t, G)
    ym = mem_attention(m_q.reshape(bsz, t, N_HEADS_MEM, MEM_HEAD_DIM), mk, mv).reshape(bsz, t, G)
    y = jnp.stack([ya, yb, yc, ym], axis=2)
    y = rmsnorm(y, grp_norm_g.reshape(N_GROUPS, G)).reshape(bsz, t, D_MIX)
    x = x + y @ w_out
    h2 = rmsnorm(x, norm2_g)
    u, new_f = causal_dwconv(h2 @ w_up, prev_f, ffn_conv_w, ffn_conv_b)
    val, gate = jnp.split(u, 2, axis=-1)
    x = x + (jax.nn.silu(gate) * val) @ w_down
    return x, new_a, new_b, new_f, k, v


def setup_inputs(seed: int = 0) -> dict:
    key = jax.random.key(seed)
    ks = jax.random.split(key, 32)
    f32 = jnp.float32
    nrm = lambda i, shape, scale: jax.random.normal(ks[i], shape, f32) * scale
    win = min(ATTN_WINDOW, PAST_LEN)
    return {
        "x_prompt": nrm(0, (BATCH, SEQ, D_MODEL), 1.0),
        "x_sample": nrm(1, (DEC_BATCH, DEC_SEQ, D_MODEL), 1.0),
        "cache_conv_a": nrm(2, (DEPTH, DEC_BATCH, CONV_A_WIDTH - 1, G), 1.0),
        "cache_conv_b": nrm(3, (DEPTH, DEC_BATCH, CONV_B_WIDTH - 1, G), 1.0),
        "cache_ffn_conv": nrm(4, (DEPTH, DEC_BATCH, FFN_CONV_WIDTH - 1, 2 * D_FF), 1.0),
        "cache_attn_k": nrm(5, (DEPTH, DEC_BATCH, win, N_HEADS_ATT, HEAD_DIM), 1.0),
        "cache_attn_v": nrm(6, (DEPTH, DEC_BATCH, win, N_HEADS_ATT, HEAD_DIM), 1.0),
        "cache_mem_k": nrm(7, (DEPTH, DEC_BATCH, N_MEM, N_HEADS_MEM, MEM_HEAD_DIM), 1.0),
        "cache_mem_v": nrm(8, (DEPTH, DEC_BATCH, N_MEM, N_HEADS_MEM, MEM_HEAD_DIM), 1.0),
        "mem_prompt": nrm(9, (BATCH, N_MEM, D_MODEL), 1.0),
        "norm1_g": 1.0 + nrm(10, (DEPTH, D_MODEL), 0.05),
        "w_in": nrm(11, (DEPTH, D_MODEL, D_IN_PROJ), D_MODEL ** -0.5),
        "w_mem_kv": nrm(12, (DEPTH, D_MODEL, 2 * G), D_MODEL ** -0.5),
        "conv_a_w": nrm(13, (DEPTH, CONV_A_WIDTH, G), CONV_A_WIDTH ** -0.5),
        "conv_b_w": nrm(14, (DEPTH, CONV_B_WIDTH, G), CONV_B_WIDTH ** -0.5),
        "conv_b_bias": nrm(15, (DEPTH, G), 0.02),
        "ln_b_g": 1.0 + nrm(16, (DEPTH, G), 0.05),
        "ln_b_b": nrm(17, (DEPTH, G), 0.02),
        "rel_bias": nrm(18, (DEPTH, N_HEADS_ATT, 2 * REL_CLIP + 1), 0.1),
        "grp_norm_g": 1.0 + nrm(19, (DEPTH, D_MIX), 0.05),
        "w_out": nrm(20, (DEPTH, D_MIX, D_MODEL), D_MIX ** -0.5),
        "norm2_g": 1.0 + nrm(21, (DEPTH, D_MODEL), 0.05),
        "w_up": nrm(22, (DEPTH, D_MODEL, 2 * D_FF), D_MODEL ** -0.5),
        "ffn_conv_w": nrm(23, (DEPTH, FFN_CONV_WIDTH, 2 * D_FF), FFN_CONV_WIDTH ** -0.5),
        "ffn_conv_b": nrm(24, (DEPTH, 2 * D_FF), 0.02),
        "w_down": nrm(25, (DEPTH, D_FF, D_MODEL), D_FF ** -0.5),
        "final_g": 1.0 + nrm(26, (D_MODEL,), 0.05),
    }


def reference(x_prompt, x_sample, cache_conv_a, cache_conv_b, cache_ffn_conv, cache_attn_k, cache_attn_v,
              cache_mem_k, cache_mem_v, mem_prompt, norm1_g, w_in, w_mem_kv, conv_a_w, conv_b_w, conv_b_bias,
              ln_b_g, ln_b_b, rel_bias, grp_norm_g, w_out, norm2_g, w_up, ffn_conv_w, ffn_conv_b, w_down, final_g):
    bp, tp, _ = x_prompt.shape
    win_p = min(ATTN_WINDOW, tp)
    xp, xs = x_prompt, x_sample
    pa, pb, pf, pk, pv, pmk, pmv = [], [], [], [], [], [], []
    sa, sb, sf, sk, sv = [], [], [], [], []
    for l in range(DEPTH):
        weights = (norm1_g[l], w_in[l], conv_a_w[l], conv_b_w[l], conv_b_bias[l], ln_b_g[l], ln_b_b[l],
                   rel_bias[l], grp_norm_g[l], w_out[l], norm2_g[l], w_up[l], ffn_conv_w[l], ffn_conv_b[l], w_down[l])
        mkv = mem_prompt.astype(xp.dtype) @ w_mem_kv[l]
        mk_p, mv_p = jnp.split(mkv, 2, axis=-1)
        mk_p = mk_p.reshape(bp, N_MEM, N_HEADS_MEM, MEM_HEAD_DIM)
        mv_p = mv_p.reshape(bp, N_MEM, N_HEADS_MEM, MEM_HEAD_DIM)
        zero_a = jnp.zeros((bp, CONV_A_WIDTH - 1, G), xp.dtype)
        zero_b = jnp.zeros((bp, CONV_B_WIDTH - 1, G), xp.dtype)
        zero_f = jnp.zeros((bp, FFN_CONV_WIDTH - 1, 2 * D_FF), xp.dtype)
        xp, na, nb, nf, k_p, v_p = layer(xp, mk_p, mv_p, zero_a, zero_b, zero_f, band_attention_prompt, *weights)
        pa.append(na); pb.append(nb); pf.append(nf)
        pk.append(k_p[:, tp - win_p:]); pv.append(v_p[:, tp - win_p:])
        pmk.append(mk_p); pmv.append(mv_p)
        attend_s = functools.partial(band_attention_sample, k_cache=cache_attn_k[l], v_cache=cache_attn_v[l])
        xs, na, nb, nf, k_s, v_s = layer(xs, cache_mem_k[l], cache_mem_v[l], cache_conv_a[l], cache_conv_b[l],
                                         cache_ffn_conv[l], attend_s, *weights)
        sa.append(na); sb.append(nb); sf.append(nf); sk.append(k_s); sv.append(v_s)
    y_prompt = rmsnorm(xp, final_g)
    y_sample = rmsnorm(xs, final_g)
    return (y_prompt, y_sample,
            jnp.stack(pa), jnp.stack(pb), jnp.stack(pf), jnp.stack(pk), jnp.stack(pv), jnp.stack(pmk), jnp.stack(pmv),
            jnp.stack(sa), jnp.stack(sb), jnp.stack(sf), jnp.stack(sk), jnp.stack(sv))
```

```python
import contextlib
import numpy as np
import concourse.bass as bass
import concourse.mybir as mybir
from concourse.bass_utils import run_bass_kernel_spmd

F32 = mybir.dt.float32
BF16 = mybir.dt.bfloat16
ALU = mybir.AluOpType
AF = mybir.ActivationFunctionType

L = 2
T = 2176
TP = 2048
TILES = [(0, 512), (512, 512), (1024, 512), (1536, 512), (2048, 128)]
NPR = 640
PL = 274
O_N1, O_N2, O_GN, O_CAW, O_CBW, O_CBB, O_LNG, O_LNB, O_FCW, O_FCB = 0, 8, 16, 24, 30, 92, 94, 96, 98, 230
NSLOT = 7
NDSLOT = 8
DMA_INFLIGHT = {"sp": 3, "pool": 8}
ARENA_W = 11950
FPARTS = [(0, 4), (4, 4), (8, 4), (12, 4), (16, 4), (20, 2)]


class Prog:
    ENGS = ("sp", "pool", "pe", "dve", "act")

    def __init__(self):
        self.ops = []
        self.last_w = {}
        self.readers = {}
        self.pending_barrier = {e: set() for e in self.ENGS}
        self.last_op = {e: None for e in self.ENGS}
        self.dma_ops = {"sp": [], "pool": []}
        self.marks = []

    def add(self, eng, fn, r=(), w=(), dma=False, arena=True):
        idx = len(self.ops)
        raw = set()
        oth = set()
        for k in r:
            if k in self.last_w:
                raw.add(self.last_w[k])
        for k in w:
            if k in self.last_w:
                oth.add(self.last_w[k])
            oth.update(self.readers.get(k, ()))
        if arena:
            hard = self.pending_barrier[eng]
            self.pending_barrier[eng] = set()
        else:
            hard = set()
        for k in r:
            self.readers.setdefault(k, []).append(idx)
        for k in w:
            self.last_w[k] = idx
            self.readers[k] = []
        deps = set()
        for d in raw | oth | hard:
            o = self.ops[d]
            if (not o["dma"]) and (not dma) and o["eng"] == eng and d not in hard:
                if eng == "pe":
                    continue
                if d not in raw:
                    continue
            deps.add(d)
        op = dict(eng=eng, fn=fn, dma=dma, deps=deps, signal=False, seq=None, dslot=None, dtarget=None)
        if dma:
            q = self.dma_ops[eng]
            j = len(q)
            op["dslot"] = j % NDSLOT
            op["dtarget"] = 16 * (j // NDSLOT + 1)
            cap = DMA_INFLIGHT[eng]
            if j >= cap:
                op["deps"].add(q[j - cap])
            q.append(idx)
        self.ops.append(op)
        self.last_op[eng] = idx
        return idx

    def mark(self, name):
        self.marks.append((name, sum(1 for o in self.ops if o["eng"] == "act")))

    def barrier(self):
        s = set()
        for e in self.ENGS:
            if self.last_op[e] is not None:
                s.add(self.last_op[e])
        for q in self.dma_ops.values():
            for i in q[-NDSLOT:]:
                s.add(i)
        for e in self.ENGS:
            self.pending_barrier[e] |= s

    def finalize(self):
        for op in self.ops:
            for d in op["deps"]:
                self.ops[d]["signal"] = True
        cnt = {e: 0 for e in self.ENGS}
        for op in self.ops:
            if not op["dma"] and op["signal"]:
                cnt[op["eng"]] += 1
                op["seq"] = cnt[op["eng"]]

    def emit(self, nc, block, sems, dsems):
        attr = {"sp": "sync", "pool": "gpsimd", "pe": "tensor", "dve": "vector", "act": "scalar"}
        ops = self.ops
        final_waits = []
        for qn, q in self.dma_ops.items():
            for i in q[-NDSLOT:]:
                o = ops[i]
                final_waits.append((("d", qn, o["dslot"]), o["dtarget"]))

        def mk(eng):
            def body(e):
                known = {}

                def wait(key, val):
                    if known.get(key, 0) >= val:
                        return
                    known[key] = val
                    if key[0] == "d":
                        e.wait_ge(dsems[key[1]][key[2]], val)
                    else:
                        e.wait_ge(sems[key[1]], val)

                for op in ops:
                    if op["eng"] != eng:
                        continue
                    for d in sorted(op["deps"]):
                        o = ops[d]
                        if o["dma"]:
                            wait(("d", o["eng"], o["dslot"]), o["dtarget"])
                        else:
                            wait(("c", o["eng"]), o["seq"])
                    ins = op["fn"](e)
                    if op["dma"]:
                        ins.then_inc(dsems[eng][op["dslot"]], 16)
                    elif op["signal"]:
                        ins.then_inc(sems[eng], 1)
                if eng == "sp":
                    for key, val in final_waits:
                        wait(key, val)
            return body

        for eng in self.ENGS:
            getattr(block, attr[eng])(mk(eng))


def build_program():
    nc = bass.Bass("TRN2", target_bir_lowering=False)

    def din(name, shape):
        return nc.dram_tensor(name, list(shape), F32, kind="ExternalInput").ap()

    def dout(name, shape):
        return nc.dram_tensor(name, list(shape), F32, kind="ExternalOutput").ap()

    xin = din("xin", [T, 1024])
    memp = din("memp", [256, 1024])
    cca = din("cca", [L, 8, 256])
    ccb = din("ccb", [L, 120, 256])
    ccf = din("ccf", [L, 8, 5632])
    cak = din("cak", [L, 4, 512, 256])
    cav = din("cav", [L, 4, 512, 256])
    cmk = din("cmk", [L, 4, 256, 256])
    cmv = din("cmv", [L, 4, 256, 256])
    w_in = din("w_in", [L, 1024, 2304])
    w_mkv = din("w_mkv", [L, 1024, 512])
    w_out = din("w_out", [L, 1024, 1024])
    w_up = din("w_up", [L, 1024, 5632])
    w_down = din("w_down", [L, 2816, 1024])
    params = din("params", [NPR, 128])
    biasd = din("biasd", [L, 128, 4, 2, 128])
    cvecd = din("cvecd", [L, 128, 4])
    fgbc = din("fgbc", [128, 1024])

    y_o = dout("y_o", [T, 1024])
    o_ca = dout("o_ca", [L, 5, 2, 256])
    o_cb = dout("o_cb", [L, 5, 30, 256])
    o_cf = dout("o_cf", [L, 5, 2, 5632])
    o_k = dout("o_k", [L, 640, 256])
    o_v = dout("o_v", [L, 640, 256])
    o_mk = dout("o_mk", [L, 256, 256])
    o_mv = dout("o_mv", [L, 256, 256])

    P = Prog()
    es = contextlib.ExitStack()
    with es:
        def sb(name, shape, dt):
            return es.enter_context(nc.sbuf_tensor(name, list(shape), dt))

        x = sb("x", [128, 8, T], F32)
        h = sb("h", [128, 8, T], BF16)
        yg = sb("yg", [128, 4, T], BF16)
        wsl = sb("wsl", [128, NSLOT, 8, 256], BF16)
        ident_f = sb("ident_f", [128, 128], F32)
        ident_b = sb("ident_b", [128, 128], BF16)
        ones_b = sb("ones_b", [128, 128], BF16)
        epsr = sb("epsr", [128, 1], F32)
        epsl = sb("epsl", [128, 1], F32)
        PT = sb("PT", [128, NPR], F32)
        memT = sb("memT", [128, 8, 256], BF16)
        haloA = sb("haloA", [128, L, 2, 8], F32)
        haloB = sb("haloB", [128, L, 2, 120], F32)
        haloF = sb("haloF", [128, L, 44, 8], F32)
        stF = sb("stF", [128, 44, 5, 2], F32)
        arena = sb("arena", [128, ARENA_W], F32)
        ps = es.enter_context(nc.psum_tensor("ps", [128, 8, 512], F32))

        sems = {e: es.enter_context(nc.semaphore("c_" + e)) for e in Prog.ENGS if e != "sp"}
        sems["sp"] = sems["pool"]
        dsems = {q: [es.enter_context(nc.semaphore("d_%s%d" % (q, i))) for i in range(NDSLOT)] for q in ("sp", "pool")}

        ar = {"off": 0}

        def areset():
            ar["off"] = 0

        def af(n):
            o = ar["off"]
            ar["off"] = o + n
            assert ar["off"] <= ARENA_W, ("arena overflow", ar["off"])
            return arena[:, o:o + n]

        def ab(n):
            nw = (n + 1) // 2
            o = ar["off"]
            ar["off"] = o + nw
            assert ar["off"] <= ARENA_W, ("arena overflow", ar["off"])
            return arena[:, o:o + nw].bitcast(BF16)[:, 0:n]

        pst = {"s": 0, "p": 0, "single_banks": list(range(8))}

        def ps1():
            b = pst["single_banks"][pst["s"] % len(pst["single_banks"])]
            pst["s"] += 1
            return b

        def ps2():
            b = (pst["p"] % 2) * 2
            pst["p"] += 1
            return b

        def pk(b):
            return ("ps", b)

        wst = {"n": 0}

        def wload(src_ap, nk=8):
            s = wst["n"] % NSLOT
            wst["n"] += 1
            dst = wsl[:, s, 0:nk, :]
            P.add("pool", lambda e, dst=dst, src=src_ap: e.dma_start(out=dst, in_=src.rearrange("(kc p) n -> p kc n", p=128)),
                  r=(), w=[("ws", s)], dma=True, arena=False)
            return s

        def wslot(s):
            return wsl[:, s, :, :]

        def walloc():
            s = wst["n"] % NSLOT
            wst["n"] += 1
            return s

        def mm_group(out_ap, pairs, r, w, skip=False, arena=True):
            n = len(pairs)

            def fn(e, out_ap=out_ap, pairs=pairs):
                ins = None
                for i, (a, b) in enumerate(pairs):
                    ins = e.matmul(out_ap, a, b, start=(i == 0), stop=(i == n - 1))
                return ins
            P.add("pe", fn, r=r, w=w, arena=arena)

        def act(out, in_, func, r, w, bias=None, scale=None):
            kw = {}
            if bias is not None:
                kw["bias"] = bias
            if scale is not None:
                kw["scale"] = scale
            P.add("act", lambda e: e.activation(out, in_, func, **kw), r=r, w=w)

        def dve_tt(out, a, b, op, r, w, arena=True):
            P.add("dve", lambda e: e.tensor_tensor(out, a, b, op), r=r, w=w, arena=arena)

        def dve_stt(out, in0, scalar, in1, op0, op1, r, w):
            P.add("dve", lambda e: e.scalar_tensor_tensor(out=out, in0=in0, scalar=scalar, in1=in1, op0=op0, op1=op1), r=r, w=w)

        def dve_ts(out, in0, s1, s2, op0, op1, r, w):
            if s2 is None:
                P.add("dve", lambda e: e.tensor_scalar(out, in0, s1, None, op0), r=r, w=w)
            else:
                P.add("dve", lambda e: e.tensor_scalar(out, in0, s1, s2, op0, op1), r=r, w=w)

        def dve_recip(out, in_, r, w):
            act(out, in_, AF.Ln, r=r, w=w)
            act(out, out, AF.Exp, r=list(w), w=w, scale=-1.0)

        def dve_recip_v(out, in_, r, w):
            P.add("dve", lambda e: e.reciprocal(out, in_), r=r, w=w)

        def sp_dma(out, in_, r, w):
            P.add("sp", lambda e: e.dma_start(out=out, in_=in_), r=r, w=w, dma=True)

        def pcol(l, off, n=1):
            c = PL * l + off
            return PT[:, c:c + n]

        def xk(c, ti):
            return ("x", c, ti)

        def xkeys(ti):
            return [("x", c, ti) for c in range(8)]

        def split3(ap, a):
            return ap.rearrange("p (a b) -> p a b", a=a)

        areset()
        NSTG = 5
        ar["off"] = ARENA_W - NSTG * 1024
        stg = [af(1024) for _ in range(NSTG)]
        P.add("pool", lambda e: e.memset(ident_f[:], 0.0), w=["ident_f"])
        P.add("pool", lambda e: e.affine_select(out=ident_f[:], in_=ident_f[:], pattern=[[-1, 128]], compare_op=ALU.not_equal,
                                                 fill=1.0, base=0, channel_multiplier=1), r=["ident_f"], w=["ident_f"])
        P.add("dve", lambda e: e.tensor_copy(ident_b[:], ident_f[:]), r=["ident_f"], w=["ident_b"])
        P.add("dve", lambda e: e.memset(ones_b[:], 1.0), w=["ones_b"])
        P.add("dve", lambda e: e.memset(epsr[:], 1e-6), w=["epsr"])
        P.add("dve", lambda e: e.memset(epsl[:], 1e-5), w=["epsl"])

        stn = {"n": 0}

        def stage():
            k = stn["n"] % NSTG
            stn["n"] += 1
            return k

        for blk in range(NPR // 128):
            k = stage()
            sp_dma(stg[k][:, 0:128], params[blk * 128:(blk + 1) * 128, :], r=(), w=[("stg", k)])
            b = ps1()
            P.add("pe", lambda e, b=b, k=k: e.transpose(ps[:, b, 0:128], stg[k][:, 0:128], ident_f[:]),
                  r=[("stg", k), "ident_f"], w=[pk(b)])
            act(PT[:, blk * 128:(blk + 1) * 128], ps[:, b, 0:128], AF.Copy, r=[pk(b)], w=["PT"])
        for l in range(L):
            k = stage()
            sp_dma(stg[k][0:8, 0:256], cca[l], r=(), w=[("stg", k)])
            b = ps1()
            for j in range(2):
                P.add("pe", lambda e, b=b, k=k, j=j: e.transpose(ps[:, b, 8 * j:8 * j + 8], stg[k][0:8, 128 * j:128 * j + 128], ident_f[0:8, 0:8]),
                      r=[("stg", k), "ident_f"], w=[pk(b)])
            act(haloA[:, l, :, :], split3(ps[:, b, 0:16], 2), AF.Copy, r=[pk(b)], w=["haloA"])
            k = stage()
            sp_dma(stg[k][0:120, 0:256], ccb[l], r=(), w=[("stg", k)])
            b = ps1()
            for j in range(2):
                P.add("pe", lambda e, b=b, k=k, j=j: e.transpose(ps[:, b, 120 * j:120 * j + 120], stg[k][0:120, 128 * j:128 * j + 128], ident_f[0:120, 0:120]),
                      r=[("stg", k), "ident_f"], w=[pk(b)])
            act(haloB[:, l, :, :], split3(ps[:, b, 0:240], 2), AF.Copy, r=[pk(b)], w=["haloB"])
            for c0 in range(0, 44, 8):
                ncn = min(8, 44 - c0)
                k = stage()
                sp_dma(stg[k][0:8, 0:128 * ncn], ccf[l, :, 128 * c0:128 * (c0 + ncn)], r=(), w=[("stg", k)])
                b = ps1()
                for ci in range(ncn):
                    P.add("pe", lambda e, b=b, k=k, ci=ci: e.transpose(ps[:, b, 8 * ci:8 * ci + 8], stg[k][0:8, 128 * ci:128 * ci + 128], ident_f[0:8, 0:8]),
                          r=[("stg", k), "ident_f"], w=[pk(b)])
                act(haloF[:, l, c0:c0 + ncn, :], split3(ps[:, b, 0:8 * ncn], ncn), AF.Copy, r=[pk(b)], w=["haloF"])
        for blk in range(2):
            k = stage()
            sp_dma(stg[k][:, :], memp[blk * 128:(blk + 1) * 128, :], r=(), w=[("stg", k)])
            for half in range(2):
                b = ps1()
                for ci in range(4):
                    c = half * 4 + ci
                    P.add("pe", lambda e, b=b, k=k, c=c, ci=ci: e.transpose(ps[:, b, 128 * ci:128 * ci + 128], stg[k][:, 128 * c:128 * c + 128], ident_f[:]),
                          r=[("stg", k), "ident_f"], w=[pk(b)])
                act(memT[:, half * 4:half * 4 + 4, blk * 128:(blk + 1) * 128], split3(ps[:, b, :], 4), AF.Copy, r=[pk(b)], w=["memT"])
        for blk in range(17):
            k = stage()
            ti = min(blk // 4, 4)
            sp_dma(stg[k][:, :], xin[blk * 128:(blk + 1) * 128, :], r=(), w=[("stg", k)])
            for half in range(2):
                b = ps1()
                for ci in range(4):
                    c = half * 4 + ci
                    P.add("pe", lambda e, b=b, k=k, c=c, ci=ci: e.transpose(ps[:, b, 128 * ci:128 * ci + 128], stg[k][:, 128 * c:128 * c + 128], ident_f[:]),
                          r=[("stg", k), "ident_f"], w=[pk(b)])
                wk = [xk(half * 4 + ci, ti) for ci in range(4)]
                if half == 0:
                    act(x[:, 0:4, blk * 128:(blk + 1) * 128], split3(ps[:, b, :], 4), AF.Copy, r=[pk(b)], w=wk)
                else:
                    P.add("dve", lambda e, b=b, blk=blk: e.tensor_copy(x[:, 4:8, blk * 128:(blk + 1) * 128], split3(ps[:, b, :], 4)), r=[pk(b)], w=wk)

        def rms_rstd(ss_bank, n, inv_n, eps_t, sd, rstd, keys_w):
            act(sd[:, 0:n], ps[:, ss_bank, 0:n], AF.Ln, r=[pk(ss_bank), "epsr", "epsl"], w=[keys_w + "_sd"], bias=eps_t[:, 0:1], scale=inv_n)
            act(rstd[:, 0:n], sd[:, 0:n], AF.Exp, r=[keys_w + "_sd"], w=[keys_w], scale=-0.5)

        def do_norm(l, goff):
            P.mark("norm L%d" % l)
            if not (l == 0 and goff == O_N1):
                P.barrier()
            areset()
            sqs = [split3(ab(8 * 512), 8), split3(ab(8 * 512), 8)]
            assert ar["off"] + 2048 <= ARENA_W - 5 * 1024
            sds = [af(512), af(512)]
            rss = [af(512), af(512)]
            pipe = Pipe()
            for ti, (t0, n) in enumerate(TILES):
                k = ti % 2
                b = ps1()

                def s0(ti=ti, t0=t0, n=n, k=k):
                    act(sqs[k][:, :, 0:n], x[:, :, t0:t0 + n], AF.Square, r=xkeys(ti), w=[("nsq", k)])

                def s1(n=n, k=k, b=b):
                    mm_group(ps[:, b, 0:n], [(ones_b[:], sqs[k][:, c, 0:n]) for c in range(8)], r=[("nsq", k), "ones_b"], w=[pk(b)])

                def s2(n=n, k=k, b=b):
                    rms_rstd(b, n, 1.0 / 1024, epsr, sds[k], rss[k], "nrs%d" % k)

                def s3(ti=ti, t0=t0, n=n, k=k):
                    for c in range(8):
                        dve_stt(h[:, c, t0:t0 + n], x[:, c, t0:t0 + n], pcol(l, goff + c), rss[k][:, 0:n], ALU.mult, ALU.mult,
                                r=[xk(c, ti), "nrs%d" % k, "PT"], w=[("h", ti)])
                pipe.item([s0, s1, s2, s3])
            pipe.flush()

        def gn_square(ygrp_k, kk, n, sq2):
            act(sq2[:, :, 0:n], ygrp_k[:, :, 0:n], AF.Square, r=[("ygrp", kk)], w=["gsq"])

        def gn_rest(l, gi, ygrp_k, kk, ti, t0, n, sq2, sd, rstd, b):
            mm_group(ps[:, b, 0:n], [(ones_b[:], sq2[:, j, 0:n]) for j in range(2)], r=["gsq", "ones_b"], w=[pk(b)])
            rms_rstd(b, n, 1.0 / 256, epsr, sd, rstd, "grs")
            for j in range(2):
                dve_stt(yg[:, (gi % 2) * 2 + j, t0:t0 + n], ygrp_k[:, j, 0:n], pcol(l, O_GN + 2 * gi + j), rstd[:, 0:n], ALU.mult, ALU.mult,
                        r=[("ygrp", kk), "grs", "PT"], w=[("yg", ti)])

        def group_norm(l, gi, ygrp_k, kk, ti, t0, n, sq2, sd, rstd):
            gn_square(ygrp_k, kk, n, sq2)
            gn_rest(l, gi, ygrp_k, kk, ti, t0, n, sq2, sd, rstd, ps1())

        def h_mm(slot, c0, ti, t0, n, b):
            ws_ = wslot(slot)
            mm_group(ps[:, b, 0:n], [(ws_[:, kc, c0:c0 + 128], h[:, kc, t0:t0 + n]) for kc in range(8)],
                     r=[("ws", slot), ("h", ti)], w=[pk(b)], arena=False)

        def state_out(l, dst, srcs, nr, stgs, keys_r):
            nch = len(srcs[0])
            cmax = stgs[0].shape[1] // 128
            q = 0
            for s in range(5):
                for c0 in range(0, nch, cmax):
                    cn = min(cmax, nch - c0)
                    b = ps1()
                    for ci in range(cn):
                        P.add("pe", lambda e, b=b, ci=ci, src=srcs[s][c0 + ci]: e.transpose(ps[0:nr, b, 128 * ci:128 * ci + 128], src, ident_f[:]),
                              r=keys_r + ["ident_f"], w=[pk(b)])
                    k = q % 2
                    q += 1
                    kname = ("sostg", k)
                    act(stgs[k][0:nr, 0:128 * cn], ps[0:nr, b, 0:128 * cn], AF.Copy, r=[pk(b)], w=[kname])
                    sp_dma(dst[l, s, :, 128 * c0:128 * (c0 + cn)], stgs[k][0:nr, 0:128 * cn], r=[kname], w=())

        def wout_load(l, pi):
            return [wload(w_out[l, 512 * pi:512 * pi + 512, 256 * dp:256 * dp + 256], nk=4) for dp in range(4)]

        def wout_tile(l, pi, slots, ti):
            t0, n = TILES[ti]
            bank_list = pst["single_banks"]
            for dc in range(8):
                ws_ = wslot(slots[dc // 2])
                dsub = dc % 2
                b = ps1()
                mm_group(ps[:, b, 0:n], [(ws_[:, kc, 128 * dsub:128 * dsub + 128], yg[:, kc, t0:t0 + n]) for kc in range(4)],
                         r=[("ws", slots[dc // 2]), ("yg", ti)], w=[pk(b)], arena=False)
                dve_tt(x[:, dc, t0:t0 + n], x[:, dc, t0:t0 + n], ps[:, b, 0:n], ALU.add, r=[pk(b), xk(dc, ti)], w=[xk(dc, ti)], arena=False)

        def wout_pair(l, pi):
            slots = wout_load(l, pi)
            for ti in range(5):
                wout_tile(l, pi, slots, ti)

        class Pipe:
            def __init__(self):
                self.items = []

            def _advance(self):
                nI = len(self.items)
                for idx, it in enumerate(self.items):
                    k = nI - idx
                    if k < len(it):
                        it[k]()

            def item(self, stages):
                self._advance()
                stages[0]()
                self.items.append(stages)
                self.items = self.items[-9:]

            def flush(self):
                for _ in range(9):
                    self._advance()
                    self.items.append([lambda: None])
                    self.items = self.items[-9:]
                self.items = []

        def ffn_item(l, c, cl, sub, sv_, sg_, ti, t0, n, ku, kprev, u, U, tv, tg, sgf):
            bv, bg = ps1(), ps1()
            Uk = U[ku]
            if ti < 4:
                dv, dgt = Uk[:, 0, 2:2 + n], Uk[:, 1, 2:2 + n]
                pv_, pg_ = ps[:, bv, 0:n], ps[:, bg, 0:n]
                shv = [Uk[:, 0, s:s + n] for s in range(3)]
                shg = [Uk[:, 1, s:s + n] for s in range(3)]
                tvo, tgo, sgo = tv[u][:, 0:n], tg[u][:, 0:n], sgf[u % 2][:, 0:n]
                go = yg[:, cl, t0:t0 + n]
                Us = None
            else:
                Us = Uk[:, :, 0:136].rearrange("p a (b c) -> p a b c", b=4)
                dv, dgt = Us[:, 0, :, 2:34], Us[:, 1, :, 2:34]
                pv_, pg_ = split3(ps[:, bv, 0:n], 4), split3(ps[:, bg, 0:n], 4)
                shv = [Us[:, 0, :, s:s + 32] for s in range(3)]
                shg = [Us[:, 1, :, s:s + 32] for s in range(3)]
                tvo, tgo, sgo = split3(tv[u][:, 0:n], 4), split3(tg[u][:, 0:n], 4), split3(sgf[u % 2][:, 0:n], 4)
                go = split3(yg[:, cl, t0:t0 + n], 4)

            def s1():
                h_mm(sv_, 128 * sub, ti, t0, n, bv)
                h_mm(sg_, 128 * sub, ti, t0, n, bg)
                if ti == 0:
                    P.add("dve", lambda e, d=Uk[:, :, 0:2]: e.memset(d, 0.0), w=[("U", ku)])
                elif ti < 4:
                    P.add("act", lambda e, d=Uk[:, :, 0:2], s_=U[kprev][:, :, 512:514]: e.activation(d, s_, AF.Copy),
                          r=[("U", kprev)], w=[("U", ku)])
                else:
                    P.add("act", lambda e, d=Us[:, 0, :, 0:2], s_=haloF[:, l, c, :].rearrange("p (b c) -> p b c", b=4): e.activation(d, s_, AF.Copy),
                          r=["haloF"], w=[("U", ku)])
                    P.add("act", lambda e, d=Us[:, 1, :, 0:2], s_=haloF[:, l, 22 + c, :].rearrange("p (b c) -> p b c", b=4): e.activation(d, s_, AF.Copy),
                          r=["haloF"], w=[("U", ku)])
                act(dv, pv_, AF.Copy, r=[pk(bv)], w=[("U", ku)])
                act(dgt, pg_, AF.Copy, r=[pk(bg)], w=[("U", ku)])
                act(tvo, pv_, AF.Identity, r=[pk(bv), "PT"], w=[("tv", u)], bias=pcol(l, O_FCB + c), scale=pcol(l, O_FCW + 3 * c + 2))
                act(tgo, pg_, AF.Identity, r=[pk(bg), "PT"], w=[("tg", u)], bias=pcol(l, O_FCB + 22 + c), scale=pcol(l, O_FCW + 3 * (22 + c) + 2))
                if ti == 3:
                    P.add("act", lambda e, d=stF[:, c, 0, :], s_=Uk[:, 0, 512:514]: e.activation(d, s_, AF.Copy), r=[("U", ku)], w=[("stF", c)])
                    P.add("act", lambda e, d=stF[:, 22 + c, 0, :], s_=Uk[:, 1, 512:514]: e.activation(d, s_, AF.Copy), r=[("U", ku)], w=[("stF", 22 + c)])
                if ti == 4:
                    P.add("act", lambda e, d=stF[:, c, 1:5, :], s_=Us[:, 0, :, 32:34]: e.activation(d, s_, AF.Copy), r=[("U", ku)], w=[("stF", c)])
                    P.add("act", lambda e, d=stF[:, 22 + c, 1:5, :], s_=Us[:, 1, :, 32:34]: e.activation(d, s_, AF.Copy), r=[("U", ku)], w=[("stF", 22 + c)])

            def s2():
                for (to, sh_, cc_) in ((tvo, shv, c), (tgo, shg, 22 + c)):
                    key = ("tv", u) if cc_ == c else ("tg", u)
                    dve_stt(to, sh_[1], pcol(l, O_FCW + 3 * cc_ + 1), to, ALU.mult, ALU.add, r=[("U", ku), key, "PT"], w=[key])
                    dve_stt(to, sh_[0], pcol(l, O_FCW + 3 * cc_ + 0), to, ALU.mult, ALU.add, r=[("U", ku), key, "PT"], w=[key])

            def s3():
                act(sgo, tgo, AF.Silu, r=[("tg", u)], w=[("sgf", u % 2)])

            def s4():
                P.add("pool", lambda e: e.tensor_tensor(go, tvo, sgo, ALU.mult), r=[("tv", u), ("sgf", u % 2)], w=[("yg", ti)])
            return [s1, s2, s3, s4]

        for l in range(L):
            do_norm(l, O_N1)

            P.mark("A L%d" % l)
            P.barrier()
            areset()
            s_ab = wload(w_in[l, :, 0:256])
            s_ac = wload(w_in[l, :, 256:512])
            s_ah = wload(w_in[l, :, 512:768])
            s_bu = wload(w_in[l, :, 768:1024])
            s_bg = wload(w_in[l, :, 1024:1280])
            pA = split3(af(2 * 2050), 2)
            pAs = af(2 * 4 * 34).rearrange("p (j b c) -> p j b c", j=2, b=4)
            hh = [af(512), af(512)]
            acc = [af(512), af(512)]
            ygrp = [split3(af(1024), 2), split3(af(1024), 2)]
            sq2 = split3(ab(1024), 2)
            sd = af(512)
            rstd = af(512)
            sostg = [af(256), af(256)]
            P.add("dve", lambda e, d=pA[:, :, 0:2]: e.memset(d, 0.0), w=[("pA", 0), ("pA", 1)])
            P.add("dve", lambda e, d=pAs[:, :, :, 0:2], s_=haloA[:, l, :, :].rearrange("p j (b c) -> p j b c", b=4): e.tensor_copy(d, s_),
                  r=["haloA"], w=[("pA", 0), ("pA", 1)])
            cnt = 0
            pipe = Pipe()
            for ti, (t0, n) in enumerate(TILES):
                kk = ti % 2
                for j in range(2):
                    u = cnt % 2
                    cnt += 1
                    bb, bc, bh = ps1(), ps1(), ps1()
                    if ti < 4:
                        data = pA[:, j, 2 + t0:2 + t0 + n]
                        sh = [pA[:, j, t0 + s:t0 + s + n] for s in range(3)]
                        pin, hin, ao, pbin, yo = ps[:, bc, 0:n], hh[u][:, 0:n], acc[u][:, 0:n], ps[:, bb, 0:n], ygrp[kk][:, j, 0:n]
                    else:
                        data = pAs[:, j, :, 2:34]
                        sh = [pAs[:, j, :, s:s + 32] for s in range(3)]
                        pin, hin, ao = split3(ps[:, bc, 0:n], 4), split3(hh[u][:, 0:n], 4), split3(acc[u][:, 0:n], 4)
                        pbin, yo = split3(ps[:, bb, 0:n], 4), split3(ygrp[kk][:, j, 0:n], 4)

                    def s0(j=j, ti=ti, t0=t0, n=n, bb=bb, bc=bc, bh=bh, u=u):
                        h_mm(s_ab, 128 * j, ti, t0, n, bb)
                        h_mm(s_ac, 128 * j, ti, t0, n, bc)
                        h_mm(s_ah, 128 * j, ti, t0, n, bh)
                        act(hh[u][:, 0:n], ps[:, bh, 0:n], AF.Copy, r=[pk(bh)], w=[("hh", u)])

                    def s1(j=j, n=n, u=u, kk=kk, bb=bb, bc=bc, data=data, sh=sh, pin=pin, hin=hin, ao=ao, pbin=pbin, yo=yo, l=l, sq2=sq2, ygrp=ygrp):
                        dve_tt(data, pin, hin, ALU.mult, r=[pk(bc), ("hh", u)], w=[("pA", j)])
                        dve_ts(ao, sh[2], pcol(l, O_CAW + 3 * j + 2), None, ALU.mult, None, r=[("pA", j), "PT"], w=[("acc", u)])
                        dve_stt(ao, sh[1], pcol(l, O_CAW + 3 * j + 1), ao, ALU.mult, ALU.add, r=[("pA", j), ("acc", u), "PT"], w=[("acc", u)])
                        dve_stt(ao, sh[0], pcol(l, O_CAW + 3 * j + 0), ao, ALU.mult, ALU.add, r=[("pA", j), ("acc", u), "PT"], w=[("acc", u)])
                        dve_tt(yo, pbin, ao, ALU.mult, r=[pk(bb), ("acc", u)], w=[("ygrp", kk)])
                        if j == 1:
                            gn_square(ygrp[kk], kk, n, sq2)
                    stages = [s0, s1]
                    if j == 1:
                        bgn = ps1()

                        def s2(l=l, kk=kk, ti=ti, t0=t0, n=n, bgn=bgn, ygrp=ygrp, sq2=sq2, sd=sd, rstd=rstd):
                            gn_rest(l, 0, ygrp[kk], kk, ti, t0, n, sq2, sd, rstd, bgn)
                        stages.append(s2)
                    pipe.item(stages)
            pipe.flush()
            state_out(l, o_ca, [[pA[:, j, 2048:2050] for j in range(2)]] + [[pAs[:, j, b, 32:34] for j in range(2)] for b in range(4)],
                      2, sostg, [("pA", 0), ("pA", 1)])

            P.mark("B L%d" % l)
            P.barrier()
            areset()
            pBl = split3(af(2 * 30), 2)
            pBs = af(2 * 4 * 62).rearrange("p (j b c) -> p j b c", j=2, b=4)
            pBb = split3(ab(2 * 2078), 2)
            pBbs = ab(2 * 4 * 62).rearrange("p (j b c) -> p j b c", j=2, b=4)
            sg = [af(512), af(512)]
            ygrp = [split3(af(1024), 2), split3(af(1024), 2)]
            ycb = split3(ab(1024), 2)
            sqb = split3(ab(1024), 2)
            sq2 = split3(ab(1024), 2)
            mt = af(512)
            msq = af(512)
            lsd = af(512)
            lrs = af(512)
            sd = af(512)
            rstd = af(512)
            dtmp = [af(512), msq]
            sostg = [af(256), af(256)]
            dslots = [(walloc(), walloc()) for j in range(2)]

            def dg(j, k, dslots=dslots):
                s = dslots[j][k // 16]
                return wsl[:, s, :, :].rearrange("p a b -> p (a b)")[:, 128 * (k % 16):128 * (k % 16) + 128]
            for j in range(2):
                for k in range(31):
                    P.add("dve", lambda e, d=dg(j, k), sc=pcol(l, O_CBW + 31 * j + k): e.tensor_scalar(d, ident_b[:], sc, None, ALU.mult),
                          r=["ident_b", "PT"], w=[("ws", dslots[j][k // 16])])
            P.add("dve", lambda e, d=pBb[:, :, 0:30]: e.memset(d, 0.0), w=[("pBb", 0), ("pBb", 1)])
            hB = haloB[:, l, :, :].rearrange("p j (b c) -> p j b c", b=4)
            P.add("dve", lambda e, d=pBs[:, :, :, 0:30], s_=hB: e.tensor_copy(d, s_), r=["haloB"], w=[("pB", 0), ("pB", 1)])
            P.add("dve", lambda e, d=pBbs[:, :, :, 0:30], s_=hB: e.tensor_copy(d, s_), r=["haloB"], w=[("pBb", 0), ("pBb", 1)])
            cnt = 0
            pipe = Pipe()
            for ti, (t0, n) in enumerate(TILES):
                kk = ti % 2
                for j in range(2):
                    u = cnt % 2
                    cnt += 1
                    bu, bg, bcv = ps1(), ps1(), ps1()
                    if ti < 4:
                        taps = [(dg(j, k), pBb[:, j, t0 + k:t0 + k + n]) for k in range(31)]
                        cout = ps[:, bcv, 0:n]
                    else:
                        taps = [(dg(j, k), pBbs[:, j, :, k:k + 32]) for k in range(31)]
                        cout = split3(ps[:, bcv, 0:n], 4)

                    def s0(j=j, ti=ti, t0=t0, n=n, bu=bu, bg=bg, u=u, sg=sg):
                        h_mm(s_bu, 128 * j, ti, t0, n, bu)
                        h_mm(s_bg, 128 * j, ti, t0, n, bg)
                        act(sg[u][:, 0:n], ps[:, bg, 0:n], AF.Sigmoid, r=[pk(bg)], w=[("sg", u)])

                    def s1(j=j, ti=ti, t0=t0, n=n, bu=bu, u=u, sg=sg, pBb=pBb, pBl=pBl, pBs=pBs, pBbs=pBbs):
                        if ti < 4:
                            dve_tt(pBb[:, j, 30 + t0:30 + t0 + n], ps[:, bu, 0:n], sg[u][:, 0:n], ALU.mult, r=[pk(bu), ("sg", u)], w=[("pBb", j)])
                            if ti == 3:
                                dve_tt(pBl[:, j, :], ps[:, bu, n - 30:n], sg[u][:, n - 30:n], ALU.mult, r=[pk(bu), ("sg", u)], w=[("pB", j)])
                        else:
                            dve_tt(pBs[:, j, :, 30:62], split3(ps[:, bu, 0:n], 4), split3(sg[u][:, 0:n], 4), ALU.mult, r=[pk(bu), ("sg", u)], w=[("pB", j)])
                            act(pBbs[:, j, :, 30:62], pBs[:, j, :, 30:62], AF.Copy, r=[("pB", j)], w=[("pBb", j)])

                    def s2(j=j, n=n, kk=kk, bcv=bcv, taps=taps, cout=cout, dslots=dslots, ygrp=ygrp, l=l, ycb=ycb, sqb=sqb):
                        mm_group(cout, taps, r=[("pBb", j), ("ws", dslots[j][0]), ("ws", dslots[j][1])], w=[pk(bcv)])
                        act(ygrp[kk][:, j, 0:n], ps[:, bcv, 0:n], AF.Identity, r=[pk(bcv), "PT"], w=[("ygrp", kk)], bias=pcol(l, O_CBB + j))
                        if j == 1:
                            act(ycb[:, :, 0:n], ygrp[kk][:, :, 0:n], AF.Copy, r=[("ygrp", kk)], w=["ycb"])
                            act(sqb[:, :, 0:n], ygrp[kk][:, :, 0:n], AF.Square, r=[("ygrp", kk)], w=["sqb"])
                    stages = [s0, s1, s2]
                    if j == 1:
                        b1, b2, bgn = ps1(), ps1(), ps1()

                        def s3(n=n, b1=b1, b2=b2, ycb=ycb, sqb=sqb, mt=mt, msq=msq, lsd=lsd, lrs=lrs):
                            mm_group(ps[:, b1, 0:n], [(ones_b[:], ycb[:, jj, 0:n]) for jj in range(2)], r=["ycb", "ones_b"], w=[pk(b1)])
                            mm_group(ps[:, b2, 0:n], [(ones_b[:], sqb[:, jj, 0:n]) for jj in range(2)], r=["sqb", "ones_b"], w=[pk(b2)])
                            act(mt[:, 0:n], ps[:, b1, 0:n], AF.Copy, r=[pk(b1)], w=["mt"], scale=1.0 / 256)
                            dve_tt(msq[:, 0:n], mt[:, 0:n], mt[:, 0:n], ALU.mult, r=["mt"], w=["msq"])
                            dve_stt(msq[:, 0:n], ps[:, b2, 0:n], 1.0 / 256, msq[:, 0:n], ALU.mult, ALU.subtract, r=[pk(b2), "msq"], w=["msq"])
                            act(lsd[:, 0:n], msq[:, 0:n], AF.Ln, r=["msq", "epsl"], w=["lsd"], bias=epsl[:, 0:1], scale=1.0)
                            act(lrs[:, 0:n], lsd[:, 0:n], AF.Exp, r=["lsd"], w=["lrs"], scale=-0.5)

                        def s4(n=n, kk=kk, ygrp=ygrp, mt=mt, lrs=lrs, dtmp=dtmp, l=l, sq2=sq2):
                            yk = ygrp[kk]
                            dk = ["dtmp0", "msq"]
                            for jj in range(2):
                                dve_tt(dtmp[jj][:, 0:n], yk[:, jj, 0:n], mt[:, 0:n], ALU.subtract, r=[("ygrp", kk), "mt"], w=[dk[jj]])
                                dve_tt(dtmp[jj][:, 0:n], dtmp[jj][:, 0:n], lrs[:, 0:n], ALU.mult, r=[dk[jj], "lrs"], w=[dk[jj]])
                            for jj in range(2):
                                act(yk[:, jj, 0:n], dtmp[jj][:, 0:n], AF.Silu, r=[dk[jj], "PT"], w=[("ygrp", kk)],
                                    bias=pcol(l, O_LNB + jj), scale=pcol(l, O_LNG + jj))
                            gn_square(yk, kk, n, sq2)

                        def s5(l=l, kk=kk, ti=ti, t0=t0, n=n, bgn=bgn, ygrp=ygrp, sq2=sq2, sd=sd, rstd=rstd):
                            gn_rest(l, 1, ygrp[kk], kk, ti, t0, n, sq2, sd, rstd, bgn)
                        stages += [s3, s4, s5]
                    pipe.item(stages)
            pipe.flush()
            state_out(l, o_cb, [[pBl[:, j, :] for j in range(2)]] + [[pBs[:, j, b, 32:62] for j in range(2)] for b in range(4)],
                      30, sostg, [("pB", 0), ("pB", 1)])

            P.mark("woutAB L%d" % l)
            wout_pair(l, 0)

            P.mark("C L%d" % l)
            P.barrier()
            areset()
            s_q = wload(w_in[l, :, 1280:1536])
            s_k = wload(w_in[l, :, 1536:1792])
            s_v = wload(w_in[l, :, 1792:2048])
            bT = af(4 * 2 * 128).rearrange("p (h r c) -> p h r c", h=4, r=2)
            cv = af(4)
            rec = [af(512), af(512)]
            ygrp = [split3(af(1024), 2), split3(af(1024), 2)]
            kvo = [af(512), af(512)]
            sq2 = split3(ab(1024), 2)
            sd = af(512)
            rstd = af(512)
            cmark = ar["off"]
            Q = [split3(ab(1024), 2), split3(ab(1024), 2)]
            Kr = split3(ab(2048), 2)
            Vr = ab(10 * 4 * 128).rearrange("p (s h c) -> p s h c", s=10, h=4)
            PTs = [ab(640), ab(640)]
            P.add("dve", lambda e, d=Vr[:, :, :, 64:128]: e.memset(d, 1.0), w=["Vr"])
            sp_dma(bT.rearrange("p h r c -> p (h r c)"), biasd[l].rearrange("p h r c -> p (h r c)"), r=(), w=["bT"])
            sp_dma(cv, cvecd[l], r=(), w=["cv"])
            for hd in range(4):
                dve_ts(bT[:, hd, :, :], bT[:, hd, :, :], cv[:, hd:hd + 1], None, ALU.subtract, None, r=["bT", "cv"], w=["bT"])
            pst["single_banks"] = [6, 7]
            ocnt = 0
            for ti in range(4):
                t0, n = TILES[ti]
                kk = ti % 2
                for j in range(2):
                    b = ps1()
                    h_mm(s_q, 128 * j, ti, t0, n, b)
                    act(Q[kk][:, j, :], ps[:, b, :], AF.Copy, r=[pk(b)], w=[("Q", kk)], scale=0.125)
                for j in range(2):
                    b = ps1()
                    h_mm(s_k, 128 * j, ti, t0, n, b)
                    c0 = ((4 * ti) % 8) * 128
                    act(Kr[:, j, c0:c0 + 512], ps[:, b, :], AF.Copy, r=[pk(b)], w=["Kr"])
                for blk in range(4):
                    tb = 4 * ti + blk
                    b = ps1()
                    vsl = wslot(s_v)
                    ksl = wslot(s_k)
                    if ti == 3:
                        mm_group(ps[:, b, 0:256], [(h[:, kc, tb * 128:tb * 128 + 128], ksl[:, kc, :]) for kc in range(8)],
                                 r=[("ws", s_k), ("h", ti)], w=[pk(b)])
                    mm_group(ps[:, b, 256:512], [(h[:, kc, tb * 128:tb * 128 + 128], vsl[:, kc, :]) for kc in range(8)],
                             r=[("ws", s_v), ("h", ti)], w=[pk(b)])
                    act(Vr[:, tb % 10, :, 0:64], split3(ps[:, b, 256:512], 4), AF.Copy, r=[pk(b)], w=["Vr"])
                    if ti == 3:
                        ko = ocnt % 2
                        ocnt += 1
                        act(kvo[ko][:, :], ps[:, b, :], AF.Copy, r=[pk(b)], w=[("kvo", ko)])
                        sp_dma(o_k[l, blk * 128:blk * 128 + 128, :], kvo[ko][:, 0:256], r=[("kvo", ko)], w=())
                        sp_dma(o_v[l, blk * 128:blk * 128 + 128, :], kvo[ko][:, 256:512], r=[("kvo", ko)], w=())
                if ti == 0:
                    pipe = Pipe()
                for qb in range(4):
                    i = 4 * ti + qb
                    r0 = max(0, 4 - i)
                    bo = 4 + (qb % 2)
                    ro = qb % 2
                    for hd in range(4):
                        j, pb = hd // 2, 64 * (hd % 2)
                        b2 = ps2()
                        S = ps[:, b2:b2 + 2, :].rearrange("p a b -> p (a b)")
                        qv = Q[kk][pb:pb + 64, j, qb * 128:qb * 128 + 128]
                        pu = hd % 2
                        O = ps[:, bo, hd * 128:hd * 128 + 128]
                        skeys = [pk(b2), pk(b2 + 1)]

                        def s0(S=S, qv=qv, i=i, r0=r0, j=j, pb=pb, kk=kk, skeys=skeys, Kr=Kr):
                            for r_ in range(r0, 5):
                                c0 = ((i - 4 + r_) % 8) * 128
                                mm_group(S[:, r_ * 128:r_ * 128 + 128], [(Kr[pb:pb + 64, j, c0:c0 + 128], qv)], r=["Kr", ("Q", kk)], w=skeys)

                        def s1(S=S, r0=r0, hd=hd, skeys=skeys, bT=bT):
                            r3 = max(r0, 3)
                            dve_tt(S[:, r3 * 128:640], S[:, r3 * 128:640], bT[:, hd, r3 - 3:2, :].rearrange("p r c -> p (r c)"), ALU.add,
                                   r=skeys + ["bT"], w=skeys)
                            if r0 == 0:
                                dve_ts(S[0:64, 64:128], S[0:64, 64:128], -1e30, None, ALU.add, None, r=skeys, w=skeys)
                            dve_ts(S[64:128, 512:576], S[64:128, 512:576], -1e30, None, ALU.add, None, r=skeys, w=skeys)

                        def s2(S=S, r0=r0, hd=hd, pu=pu, skeys=skeys, PTs=PTs, cv=cv):
                            act(PTs[pu][:, r0 * 128:640], S[:, r0 * 128:640], AF.Exp, r=skeys, w=[("PTs", pu)])

                        def s3(O=O, i=i, r0=r0, hd=hd, pu=pu, bo=bo, Vr=Vr, PTs=PTs):
                            mm_group(O, [(Vr[:, (i - 4 + r_) % 10, hd, :], PTs[pu][:, r_ * 128:r_ * 128 + 128]) for r_ in range(r0, 5)],
                                     r=["Vr", ("PTs", pu)], w=[pk(bo)])
                        stages = [s0, s1, s2, s3]
                        if hd == 3:
                            def s4(bo=bo, ro=ro, rec=rec):
                                dve_recip(rec[ro][64:128, :], ps[64:128, bo, :], r=[pk(bo)], w=[("rec", ro)])

                            def s5(bo=bo, ro=ro, rec=rec, kk=kk, qb=qb, ygrp=ygrp):
                                oin = ps[0:64, bo, :].rearrange("p (h c) -> p h c", h=4)
                                rin = rec[ro][64:128, :].rearrange("p (h c) -> p h c", h=4)
                                for par in range(2):
                                    for jj in range(2):
                                        hh_ = 2 * jj + par
                                        dve_tt(ygrp[kk][64 * par:64 * par + 64, jj, qb * 128:qb * 128 + 128], oin[:, hh_, :], rin[:, hh_, :], ALU.mult,
                                               r=[pk(bo), ("rec", ro)], w=[("ygrp", kk)])
                            stages += [s4, s5]
                            if qb == 3:
                                bgn = ps1()

                                def s6(kk=kk, n=n, ygrp=ygrp, sq2=sq2):
                                    gn_square(ygrp[kk], kk, n, sq2)

                                def s7(l=l, kk=kk, ti=ti, t0=t0, n=n, ygrp=ygrp, sq2=sq2, sd=sd, rstd=rstd, bgn=bgn):
                                    gn_rest(l, 2, ygrp[kk], kk, ti, t0, n, sq2, sd, rstd, bgn)
                                stages += [s6, s7]
                        pipe.item(stages)
            pipe.flush()
            P.mark("Csample L%d" % l)
            pst["single_banks"] = [4, 5, 6, 7]
            P.barrier()
            ar["off"] = cmark
            stk = af(1024)
            stv = af(1024)
            kcT = split3(ab(1024), 2)
            Vc = ab(4 * 4 * 128).rearrange("p (s h c) -> p s h c", s=4, h=4)
            Vs = ab(4 * 128).rearrange("p (h c) -> p h c", h=4)
            Ks = split3(ab(256), 2)
            Qs = split3(ab(256), 2)
            PTa = ab(640).rearrange("p (h r c) -> p h r c", h=4, r=5)
            P.add("dve", lambda e, d=Vc[:, :, :, 64:128]: e.memset(d, 1.0), w=["Vc"])
            P.add("dve", lambda e, d=Vs[:, :, 64:128]: e.memset(d, 1.0), w=["Vs"])
            ti = 4
            t0, n = TILES[4]
            kk = 0
            for j in range(2):
                b = ps1()
                h_mm(s_q, 128 * j, ti, t0, n, b)
                act(Qs[:, j, :], ps[:, b, 0:128], AF.Copy, r=[pk(b)], w=["Qs"], scale=0.125)
                b = ps1()
                h_mm(s_k, 128 * j, ti, t0, n, b)
                act(Ks[:, j, :], ps[:, b, 0:128], AF.Copy, r=[pk(b)], w=["Ks"])
            pipe = Pipe()
            for bi in range(4):
                tb0 = TP + 32 * bi
                bkv = ps1()
                bt0, bt1 = ps1(), ps1()
                b2 = ps2()
                bo = 4 + bi % 2 if False else ps1()
                ro = bi % 2
                ko = bi % 2

                def t0_(bi=bi, stk=stk, stv=stv):
                    sp_dma(split3(stk, 4), cak[l, bi].rearrange("(s p) c -> p s c", p=128), r=(), w=["stk"])
                    sp_dma(split3(stv, 4), cav[l, bi].rearrange("(s p) c -> p s c", p=128), r=(), w=["stv"])

                def t1_(bi=bi, tb0=tb0, bkv=bkv, bt=(bt0, bt1), ko=ko, stk=stk, stv=stv, kcT=kcT, Vc=Vc, Vs=Vs, kvo=kvo):
                    mm_group(ps[0:32, bkv, 0:256], [(h[:, kc, tb0:tb0 + 32], wslot(s_k)[:, kc, :]) for kc in range(8)], r=[("ws", s_k), ("h", 4)], w=[pk(bkv)])
                    mm_group(ps[0:32, bkv, 256:512], [(h[:, kc, tb0:tb0 + 32], wslot(s_v)[:, kc, :]) for kc in range(8)], r=[("ws", s_v), ("h", 4)], w=[pk(bkv)])
                    act(Vs[0:32, :, 0:64], split3(ps[0:32, bkv, 256:512], 4), AF.Copy, r=[pk(bkv)], w=["Vs"])
                    act(kvo[ko][0:32, :], ps[0:32, bkv, :], AF.Copy, r=[pk(bkv)], w=[("kvo", ko)])
                    sp_dma(o_k[l, 512 + 32 * bi:512 + 32 * bi + 32, :], kvo[ko][0:32, 0:256], r=[("kvo", ko)], w=())
                    sp_dma(o_v[l, 512 + 32 * bi:512 + 32 * bi + 32, :], kvo[ko][0:32, 256:512], r=[("kvo", ko)], w=())
                    for j in range(2):
                        b = bt[j]
                        for s_ in range(4):
                            P.add("pe", lambda e, b=b, s_=s_, j=j, stk=stk: e.transpose(ps[:, b, 128 * s_:128 * s_ + 128], stk[:, 256 * s_ + 128 * j:256 * s_ + 128 * j + 128], ident_f[:]),
                                  r=["stk", "ident_f"], w=[pk(b)])
                        act(kcT[:, j, :], ps[:, b, :], AF.Copy, r=[pk(b)], w=["kcT"])
                    P.add("dve", lambda e, d=Vc.rearrange("p s h c -> p (s h) c")[:, :, 0:64], s_=stv.rearrange("p (a c) -> p a c", c=64): e.tensor_copy(d, s_),
                          r=["stv"], w=["Vc"])

                def t2_(bi=bi, b2=b2, bo=bo, ro=ro, kcT=kcT, Ks=Ks, Qs=Qs, Vc=Vc, Vs=Vs, PTa=PTa, bT=bT, rec=rec, ygrp=ygrp):
                    S = ps[:, b2:b2 + 2, :].rearrange("p a b -> p (a b)")[:, 0:640].rearrange("p (h r c) -> p h r c", h=4, r=5)
                    skeys = [pk(b2), pk(b2 + 1)]
                    for hd in range(4):
                        j, pb = hd // 2, 64 * (hd % 2)
                        qv = Qs[pb:pb + 64, j, 32 * bi:32 * bi + 32]
                        for r_ in range(4):
                            mm_group(S[:, hd, r_, :], [(kcT[pb:pb + 64, j, r_ * 128:r_ * 128 + 128], qv)], r=["kcT", "Qs"], w=skeys)
                        mm_group(S[0:32, hd, 4, :], [(Ks[pb:pb + 64, j, 32 * bi:32 * bi + 32], qv)], r=["Ks", "Qs"], w=skeys)
                    dve_tt(S[:, :, 3, :], S[:, :, 3, :], bT[:, :, 0, 0:32], ALU.add, r=skeys + ["bT"], w=skeys)
                    dve_tt(S[0:32, :, 4, :], S[0:32, :, 4, :], bT[0:32, :, 1, 0:32], ALU.add, r=skeys + ["bT"], w=skeys)
                    Sx = ps[:, b2:b2 + 2, :].rearrange("p a b -> p (a b)")[:, 0:640].rearrange("p (h x) -> p h x", h=4)
                    Px = PTa.rearrange("p h r c -> p h (r c)")
                    act(Px[:, :, 0:128], Sx[:, :, 0:128], AF.Exp, r=skeys, w=["PTa"])
                    act(Px[0:32, :, 128:160], Sx[0:32, :, 128:160], AF.Exp, r=skeys, w=["PTa"])
                    for hd in range(4):
                        O = ps[:, bo, hd * 128:hd * 128 + 32]
                        pairs = [(Vc[:, r_, hd, :], PTa[:, hd, r_, :]) for r_ in range(4)] + [(Vs[0:32, hd, :], PTa[0:32, hd, 4, :])]
                        mm_group(O, pairs, r=["Vc", "Vs", "PTa"], w=[pk(bo)])
                    o4 = ps[:, bo, :].rearrange("p (h c) -> p h c", h=4)
                    r4 = rec[ro].rearrange("p (h c) -> p h c", h=4)
                    dve_recip(r4[64:128, :, 0:32], o4[64:128, :, 0:32], r=[pk(bo)], w=[("rec", ro)])
                    for par in range(2):
                        for j in range(2):
                            hd = 2 * j + par
                            dve_tt(ygrp[0][64 * par:64 * par + 64, j, 32 * bi:32 * bi + 32], o4[0:64, hd, 0:32], r4[64:128, hd, 0:32], ALU.mult,
                                   r=[pk(bo), ("rec", ro)], w=[("ygrp", 0)])
                pipe.item([t0_, t1_, t2_])
            pipe.flush()
            group_norm(l, 2, ygrp[kk], kk, ti, t0, n, sq2, sd, rstd)
            pst["single_banks"] = list(range(8))

            P.mark("M L%d" % l)
            P.barrier()
            areset()
            s_m = wload(w_in[l, :, 2048:2304])
            s_mk = wload(w_mkv[l, :, 0:256])
            s_mv = wload(w_mkv[l, :, 256:512])
            wo_cm = wout_load(l, 1)
            Q = [split3(ab(1024), 2), split3(ab(1024), 2)]
            mkT = split3(ab(512), 2)
            mva = ab(2 * 4 * 128).rearrange("p (s h c) -> p s h c", s=2, h=4)
            PTs = [split3(ab(1024), 2), split3(ab(1024), 2)]
            rec = [af(512), af(512)]
            ygrp = [split3(af(1024), 2), split3(af(1024), 2)]
            mo = [af(512), af(512)]
            sq2 = split3(ab(1024), 2)
            sd = af(512)
            rstd = af(512)
            stk = af(512)
            stv = af(512)
            mkTs = split3(ab(512), 2)
            mvs = ab(2 * 4 * 128).rearrange("p (s h c) -> p s h c", s=2, h=4)
            Qs = split3(ab(256), 2)
            PTa = ab(64)
            P.add("dve", lambda e, d=mva[:, :, :, 64:128]: e.memset(d, 1.0), w=["mva"])
            P.add("dve", lambda e, d=mvs[:, :, :, 64:128]: e.memset(d, 1.0), w=["mvs"])
            pst["single_banks"] = [4, 5, 6, 7]
            ocnt = 0
            for blk in range(2):
                b = ps1()
                mm_group(ps[:, b, 0:256], [(memT[:, kc, blk * 128:blk * 128 + 128], wslot(s_mk)[:, kc, :]) for kc in range(8)], r=[("ws", s_mk), "memT"], w=[pk(b)])
                mm_group(ps[:, b, 256:512], [(memT[:, kc, blk * 128:blk * 128 + 128], wslot(s_mv)[:, kc, :]) for kc in range(8)], r=[("ws", s_mv), "memT"], w=[pk(b)])
                act(mva[:, blk, :, 0:64], split3(ps[:, b, 256:512], 4), AF.Copy, r=[pk(b)], w=["mva"])
                ko = ocnt % 2
                ocnt += 1
                act(mo[ko][:, :], ps[:, b, :], AF.Copy, r=[pk(b)], w=[("mo", ko)])
                sp_dma(o_mk[l, blk * 128:blk * 128 + 128, :], mo[ko][:, 0:256], r=[("mo", ko)], w=())
                sp_dma(o_mv[l, blk * 128:blk * 128 + 128, :], mo[ko][:, 256:512], r=[("mo", ko)], w=())
            for j in range(2):
                b = ps1()
                mm_group(ps[:, b, 0:256], [(wslot(s_mk)[:, kc, 128 * j:128 * j + 128], memT[:, kc, :]) for kc in range(8)], r=[("ws", s_mk), "memT"], w=[pk(b)])
                act(mkT[:, j, :], ps[:, b, 0:256], AF.Copy, r=[pk(b)], w=["mkT"])
            pipe = Pipe()
            obank = 0
            for ti in range(4):
                t0, n = TILES[ti]
                kk = ti % 2
                for hd in range(4):
                    j, pb = hd // 2, 64 * (hd % 2)
                    b2 = ps2()
                    pu = hd % 2
                    bo = 4 + (obank % 3)
                    obank += 1
                    ro = hd % 2

                    def s0(hd=hd, j=j, pb=pb, b2=b2, kk=kk, ti=ti, t0=t0, n=n, Q=Q, mkT=mkT):
                        if hd == 0:
                            for jj in range(2):
                                h_mm(s_m, 128 * jj, ti, t0, n, 7)
                                act(Q[kk][:, jj, :], ps[:, 7, :], AF.Copy, r=[pk(7)], w=[("Q", kk)], scale=0.125)
                        for r_ in range(2):
                            mm_group(ps[:, b2 + r_, :], [(mkT[pb:pb + 64, j, r_ * 128:r_ * 128 + 128], Q[kk][pb:pb + 64, j, :])],
                                     r=["mkT", ("Q", kk)], w=[pk(b2 + r_)])

                    def s1(b2=b2, pu=pu, PTs=PTs):
                        act(PTs[pu][:, :, :], ps[:, b2:b2 + 2, :], AF.Exp, r=[pk(b2), pk(b2 + 1)], w=[("PTs", pu)])

                    def s2(bo=bo, pu=pu, hd=hd, mva=mva, PTs=PTs):
                        mm_group(ps[:, bo, :], [(mva[:, r_, hd, :], PTs[pu][:, r_, :]) for r_ in range(2)], r=["mva", ("PTs", pu)], w=[pk(bo)])

                    def s3(bo=bo, ro=ro, rec=rec):
                        dve_recip_v(rec[ro][64:128, :], ps[64:128, bo, :], r=[pk(bo)], w=[("rec", ro)])

                    def s4(bo=bo, ro=ro, rec=rec, kk=kk, j=j, pb=pb, ygrp=ygrp, hd=hd, n=n, sq2=sq2):
                        dve_tt(ygrp[kk][pb:pb + 64, j, :], ps[0:64, bo, :], rec[ro][64:128, :], ALU.mult, r=[pk(bo), ("rec", ro)], w=[("ygrp", kk)])
                        if hd == 3:
                            gn_square(ygrp[kk], kk, n, sq2)
                    stages = [s0, s1, s2, s3, s4]
                    if hd == 3:
                        def s5(l=l, kk=kk, ti=ti, t0=t0, n=n, ygrp=ygrp, sq2=sq2, sd=sd, rstd=rstd):
                            gn_rest(l, 3, ygrp[kk], kk, ti, t0, n, sq2, sd, rstd, 7)
                        stages.append(s5)
                    pipe.item(stages)
            pipe.flush()
            P.mark("Msample L%d" % l)
            ti = 4
            t0, n = TILES[4]
            kk = 0
            for j in range(2):
                b = ps1()
                h_mm(s_m, 128 * j, ti, t0, n, b)
                act(Qs[:, j, :], ps[:, b, 0:128], AF.Copy, r=[pk(b)], w=["Qs"], scale=0.125)
            for bi in range(4):
                sp_dma(split3(stk, 2), cmk[l, bi].rearrange("(s p) c -> p s c", p=128), r=(), w=["stk"])
                sp_dma(split3(stv, 2), cmv[l, bi].rearrange("(s p) c -> p s c", p=128), r=(), w=["stv"])
                b = ps1()
                for j in range(2):
                    for s_ in range(2):
                        P.add("pe", lambda e, b=b, s_=s_, j=j, stk=stk: e.transpose(ps[:, b, 256 * j + 128 * s_:256 * j + 128 * s_ + 128],
                                                                            stk[:, 256 * s_ + 128 * j:256 * s_ + 128 * j + 128], ident_f[:]),
                              r=["stk", "ident_f"], w=[pk(b)])
                act(mkTs[:, :, :], split3(ps[:, b, :], 2), AF.Copy, r=[pk(b)], w=["mkTs"])
                P.add("dve", lambda e, d=mvs.rearrange("p s h c -> p (s h) c")[:, :, 0:64], s_=stv.rearrange("p (a c) -> p a c", c=64): e.tensor_copy(d, s_),
                      r=["stv"], w=["mvs"])
                bo = ps1()
                ro = bi % 2
                for hd in range(4):
                    j, pb = hd // 2, 64 * (hd % 2)
                    b2 = ps2()
                    S = ps[:, b2, 0:64].rearrange("p (r c) -> p r c", r=2)
                    qv = Qs[pb:pb + 64, j, 32 * bi:32 * bi + 32]
                    for r_ in range(2):
                        mm_group(S[:, r_, :], [(mkTs[pb:pb + 64, j, r_ * 128:r_ * 128 + 128], qv)], r=["mkTs", "Qs"], w=[pk(b2)])
                    act(split3(PTa, 2), S, AF.Exp, r=[pk(b2)], w=["PTa"])
                    O = ps[:, bo, hd * 128:hd * 128 + 32]
                    mm_group(O, [(mvs[:, r_, hd, :], PTa[:, 32 * r_:32 * r_ + 32]) for r_ in range(2)], r=["mvs", "PTa"], w=[pk(bo)])
                o4 = ps[:, bo, :].rearrange("p (h c) -> p h c", h=4)
                r4 = rec[ro].rearrange("p (h c) -> p h c", h=4)
                dve_recip(r4[64:128, :, 0:32], o4[64:128, :, 0:32], r=[pk(bo)], w=[("rec", ro)])
                for par in range(2):
                    for j in range(2):
                        hd = 2 * j + par
                        dve_tt(ygrp[kk][64 * par:64 * par + 64, j, 32 * bi:32 * bi + 32], o4[0:64, hd, 0:32], r4[64:128, hd, 0:32], ALU.mult,
                               r=[pk(bo), ("rec", ro)], w=[("ygrp", kk)])
                wout_tile(l, 1, wo_cm, bi)
            group_norm(l, 3, ygrp[kk], kk, ti, t0, n, sq2, sd, rstd)
            pst["single_banks"] = list(range(8))

            P.mark("woutCM L%d" % l)
            wout_tile(l, 1, wo_cm, 4)

            P.mark("FFN L%d" % l)
            do_norm(l, O_N2)
            P.barrier()
            areset()
            U = [af(2 * 514).rearrange("p (a b) -> p a b", a=2) for _ in range(3)]
            fstg = [af(512), af(512)]
            tv = [af(512) for _ in range(4)]
            tg = [af(512) for _ in range(4)]
            sgf = [af(512), af(512)]
            ucnt = 0
            vcnt = 0
            fq = 0
            fpipe = Pipe()
            for (c_start, c_n) in FPARTS:
                npair = c_n // 2
                pslots = []

                def ldp(pi, c_start=c_start):
                    cc = c_start // 2 + pi
                    pslots.append((wload(w_up[l, :, 256 * cc:256 * cc + 256]), wload(w_up[l, :, 2816 + 256 * cc:2816 + 256 * cc + 256])))
                ldp(0)
                dsl = []

                def ldd(dp, c_start=c_start, c_n=c_n, dsl=dsl):
                    dsl.append(wload(w_down[l, 128 * c_start:128 * (c_start + c_n), 256 * dp:256 * dp + 256], nk=c_n))
                for pi in range(npair):
                    if pi + 1 < npair:
                        ldp(pi + 1)
                    if pi == 0:
                        for dp in range(3):
                            ldd(dp)
                    if pi == 1:
                        ldd(3)
                    sv_, sg_ = pslots[pi]
                    for sub in range(2):
                        c = c_start + 2 * pi + sub
                        cl = c - c_start
                        for ti, (t0, n) in enumerate(TILES):
                            ku = ucnt % 3
                            kprev = (ucnt - 1) % 3
                            ucnt += 1
                            u = vcnt % 4
                            vcnt += 1
                            fpipe.item(ffn_item(l, c, cl, sub, sv_, sg_, ti, t0, n, ku, kprev, u, U, tv, tg, sgf))
                if npair == 1:
                    ldd(3)
                fpipe.flush()
                for ti, (t0, n) in enumerate(TILES):
                    for dc in range(8):
                        ws_ = wslot(dsl[dc // 2])
                        dsub = dc % 2
                        b = ps1()
                        mm_group(ps[:, b, 0:n], [(ws_[:, kc, 128 * dsub:128 * dsub + 128], yg[:, kc, t0:t0 + n]) for kc in range(c_n)],
                                 r=[("ws", dsl[dc // 2]), ("yg", ti)], w=[pk(b)], arena=False)
                        dve_tt(x[:, dc, t0:t0 + n], x[:, dc, t0:t0 + n], ps[:, b, 0:n], ALU.add, r=[pk(b), xk(dc, ti)], w=[xk(dc, ti)], arena=False)
                ocf2 = o_cf[l].rearrange("s r c -> (s r) c")
                for half in range(2):
                    b = ps1()
                    for ci in range(c_n):
                        cc_ = 22 * half + c_start + ci
                        P.add("pe", lambda e, b=b, ci=ci, src=stF[:, cc_, :, :].rearrange("p s r -> p (s r)"): e.transpose(ps[0:10, b, 128 * ci:128 * ci + 128], src, ident_f[:]),
                              r=[("stF", cc_), "ident_f"], w=[pk(b)])
                    k = fq % 2
                    fq += 1
                    act(fstg[k][0:10, 0:128 * c_n], ps[0:10, b, 0:128 * c_n], AF.Copy, r=[pk(b)], w=[("sostg", k)])
                    sp_dma(ocf2[:, 2816 * half + 128 * c_start:2816 * half + 128 * (c_start + c_n)], fstg[k][0:10, 0:128 * c_n], r=[("sostg", k)], w=())

        P.mark("epilogue")
        P.barrier()
        areset()
        fg = af(1024)
        ost = [af(1024), af(1024)]
        xsq = af(512)
        ssum = [af(2), af(2)]
        sdv = [af(1), af(1)]
        rsv = [af(1), af(1)]
        sp_dma(fg, fgbc, r=(), w=["fg"])
        pipe = Pipe()
        for blk in range(17):
            ti = min(blk // 4, 4)
            k = blk % 2
            banks = [ps1(), ps1()]

            def s0(blk=blk, ti=ti, k=k, banks=banks):
                P.add("dve", lambda e, d=ssum[k]: e.memset(d, 0.0), w=[("ssum", k)])
                for half in range(2):
                    b = banks[half]
                    for ci in range(4):
                        c = half * 4 + ci
                        P.add("pe", lambda e, b=b, c=c, ci=ci, blk=blk: e.transpose(ps[:, b, 128 * ci:128 * ci + 128], x[:, c, blk * 128:blk * 128 + 128], ident_f[:]),
                              r=[xk(c, ti), "ident_f"], w=[pk(b)])
                    P.add("act", lambda e, b=b, half=half, k=k: e.activation(xsq[:, :], ps[:, b, :], AF.Square, accum_out=ssum[k][:, half:half + 1]),
                          r=[pk(b)], w=["xsq", ("ssum", k)])

            def s1(k=k):
                dve_tt(ssum[k][:, 0:1], ssum[k][:, 0:1], ssum[k][:, 1:2], ALU.add, r=[("ssum", k)], w=[("ssum", k)])
                act(sdv[k], ssum[k][:, 0:1], AF.Ln, r=[("ssum", k), "epsr"], w=[("sdv", k)], bias=epsr[:, 0:1], scale=1.0 / 1024)
                act(rsv[k], sdv[k], AF.Exp, r=[("sdv", k)], w=[("rsv", k)], scale=-0.5)

            def s2(blk=blk, k=k, banks=banks):
                for half in range(2):
                    b = banks[half]
                    dve_stt(ost[k][:, 512 * half:512 * half + 512], ps[:, b, :], rsv[k][:, 0:1], fg[:, 512 * half:512 * half + 512], ALU.mult, ALU.mult,
                            r=[pk(b), ("rsv", k), "fg"], w=[("ost", k)])
                sp_dma(y_o[blk * 128:blk * 128 + 128, :], ost[k], r=[("ost", k)], w=())
            pipe.item([s0, s1, s2])
        pipe.flush()

        P.finalize()
        _CACHE['prog'] = P
        with nc.Block() as block:
            P.emit(nc, block, sems, dsems)
    return nc


_CACHE = {}


def _prep_inputs(inp):
    f = lambda a: np.ascontiguousarray(np.asarray(a, dtype=np.float32))
    x_prompt, x_sample = f(inp["x_prompt"]), f(inp["x_sample"])
    rows = []
    for l in range(L):
        rows.append(f(inp["norm1_g"])[l].reshape(8, 128))
        rows.append(f(inp["norm2_g"])[l].reshape(8, 128))
        rows.append(f(inp["grp_norm_g"])[l].reshape(8, 128))
        caw = f(inp["conv_a_w"])[l].reshape(3, 2, 128).transpose(1, 0, 2).reshape(6, 128)
        rows.append(caw)
        cbw = f(inp["conv_b_w"])[l].reshape(31, 2, 128).transpose(1, 0, 2).reshape(62, 128)
        rows.append(cbw)
        rows.append(f(inp["conv_b_bias"])[l].reshape(2, 128))
        rows.append(f(inp["ln_b_g"])[l].reshape(2, 128))
        rows.append(f(inp["ln_b_b"])[l].reshape(2, 128))
        fcw = f(inp["ffn_conv_w"])[l].reshape(3, 44, 128).transpose(1, 0, 2).reshape(132, 128)
        rows.append(fcw)
        rows.append(f(inp["ffn_conv_b"])[l].reshape(44, 128))
    params = np.concatenate(rows, axis=0)
    assert params.shape[0] == 2 * PL
    params = np.concatenate([params, np.zeros((NPR - params.shape[0], 128), np.float32)], axis=0)
    rb = f(inp["rel_bias"])
    kk = np.arange(128)[:, None, None]
    r = np.arange(2)[None, :, None] + 3
    q = np.arange(128)[None, None, :]
    idx = np.clip(q + 128 * (4 - r) - kk, -128, 128) + 128
    biasd = np.ascontiguousarray(rb[:, :, idx].transpose(0, 2, 1, 3, 4))
    cvecd = np.ascontiguousarray(np.broadcast_to(rb[:, None, :, 256], (L, 128, 4)))
    fgbc = np.ascontiguousarray(np.broadcast_to(f(inp["final_g"])[None, :], (128, 1024)))
    shared = dict(w_in=f(inp["w_in"]), w_mkv=f(inp["w_mem_kv"]), w_out=f(inp["w_out"]), w_up=f(inp["w_up"]), w_down=f(inp["w_down"]),
                  params=params, biasd=biasd, cvecd=cvecd, fgbc=fgbc)
    cca, ccb, ccf = f(inp["cache_conv_a"]), f(inp["cache_conv_b"]), f(inp["cache_ffn_conv"])
    cak, cav = f(inp["cache_attn_k"]), f(inp["cache_attn_v"])
    cmk, cmv = f(inp["cache_mem_k"]), f(inp["cache_mem_v"])
    memp = f(inp["mem_prompt"])
    maps = []
    for i in range(8):
        sl = slice(4 * i, 4 * i + 4)
        m = dict(shared)
        m["xin"] = np.ascontiguousarray(np.concatenate([x_prompt[i], x_sample[sl].reshape(128, 1024)], axis=0))
        m["memp"] = np.ascontiguousarray(memp[i])
        m["cca"] = np.ascontiguousarray(cca[:, sl].reshape(L, 8, 256))
        m["ccb"] = np.ascontiguousarray(ccb[:, sl].reshape(L, 120, 256))
        m["ccf"] = np.ascontiguousarray(ccf[:, sl].reshape(L, 8, 5632))
        m["cak"] = np.ascontiguousarray(cak[:, sl].reshape(L, 4, 512, 256))
        m["cav"] = np.ascontiguousarray(cav[:, sl].reshape(L, 4, 512, 256))
        m["cmk"] = np.ascontiguousarray(cmk[:, sl].reshape(L, 4, 256, 256))
        m["cmv"] = np.ascontiguousarray(cmv[:, sl].reshape(L, 4, 256, 256))
        maps.append(m)
    return maps


def kernel(**inputs):
    if "nc" not in _CACHE:
        _CACHE["nc"] = build_program()
    nc = _CACHE["nc"]
    maps = _prep_inputs(inputs)
    res = run_bass_kernel_spmd(nc, maps, core_ids=list(range(8)))
    R = res.results
    y_prompt = np.stack([R[i]["y_o"][0:2048] for i in range(8)], 0)
    y_sample = np.concatenate([R[i]["y_o"][2048:].reshape(4, 32, 1024) for i in range(8)], 0)

    def gather(name, shape_tail):
        pr = np.stack([R[i][name][:, 0] for i in range(8)], 1)
        sa = np.concatenate([R[i][name][:, 1:5] for i in range(8)], 1)
        return pr, sa
    p_ca, s_ca = gather("o_ca", None)
    p_cb, s_cb = gather("o_cb", None)
    p_cf, s_cf = gather("o_cf", None)
    p_k = np.stack([R[i]["o_k"][:, 0:512].reshape(L, 512, 4, 64) for i in range(8)], 1)
    p_v = np.stack([R[i]["o_v"][:, 0:512].reshape(L, 512, 4, 64) for i in range(8)], 1)
    s_k = np.concatenate([R[i]["o_k"][:, 512:].reshape(L, 4, 32, 4, 64) for i in range(8)], 1)
    s_v = np.concatenate([R[i]["o_v"][:, 512:].reshape(L, 4, 32, 4, 64) for i in range(8)], 1)
    p_mk = np.stack([R[i]["o_mk"].reshape(L, 256, 4, 64) for i in range(8)], 1)
    p_mv = np.stack([R[i]["o_mv"].reshape(L, 256, 4, 64) for i in range(8)], 1)
    outs = (y_prompt, y_sample, p_ca, p_cb, p_cf, p_k, p_v, p_mk, p_mv, s_ca, s_cb, s_cf, s_k, s_v)
    return tuple(np.ascontiguousarray(o, dtype=np.float32) for o in outs)
```

```python
import contextlib
import numpy as np
import concourse.bass as bass
import concourse.mybir as mybir
from concourse.bass_utils import run_bass_kernel_spmd

F32 = mybir.dt.float32
BF16 = mybir.dt.bfloat16
ALU = mybir.AluOpType
AF = mybir.ActivationFunctionType

L = 2
T = 2176
TP = 2048
TILES = [(0, 512), (512, 512), (1024, 512), (1536, 512), (2048, 128)]
NPR = 640
PL = 274
O_N1, O_N2, O_GN, O_CAW, O_CBW, O_CBB, O_LNG, O_LNB, O_FCW, O_FCB = 0, 8, 16, 24, 30, 92, 94, 96, 98, 230
NSLOT = 7
NDSLOT = 8
DMA_INFLIGHT = {"sp": 3, "pool": 8}
ARENA_W = 11950
FPARTS = [(0, 4), (4, 4), (8, 4), (12, 4), (16, 4), (20, 2)]


class Prog:
    ENGS = ("sp", "pool", "pe", "dve", "act")

    def __init__(self):
        self.ops = []
        self.last_w = {}
        self.readers = {}
        self.pending_barrier = {e: set() for e in self.ENGS}
        self.last_op = {e: None for e in self.ENGS}
        self.dma_ops = {"sp": [], "pool": []}
        self.marks = []

    def add(self, eng, fn, r=(), w=(), dma=False, arena=True):
        idx = len(self.ops)
        raw = set()
        oth = set()
        for k in r:
            if k in self.last_w:
                raw.add(self.last_w[k])
        for k in w:
            if k in self.last_w:
                oth.add(self.last_w[k])
            oth.update(self.readers.get(k, ()))
        if arena:
            hard = self.pending_barrier[eng]
            self.pending_barrier[eng] = set()
        else:
            hard = set()
        for k in r:
            self.readers.setdefault(k, []).append(idx)
        for k in w:
            self.last_w[k] = idx
            self.readers[k] = []
        deps = set()
        for d in raw | oth | hard:
            o = self.ops[d]
            if (not o["dma"]) and (not dma) and o["eng"] == eng and d not in hard:
                if eng == "pe":
                    continue
                if d not in raw:
                    continue
            deps.add(d)
        op = dict(eng=eng, fn=fn, dma=dma, deps=deps, signal=False, seq=None, dslot=None, dtarget=None)
        if dma:
            q = self.dma_ops[eng]
            j = len(q)
            op["dslot"] = j % NDSLOT
            op["dtarget"] = 16 * (j // NDSLOT + 1)
            cap = DMA_INFLIGHT[eng]
            if j >= cap:
                op["deps"].add(q[j - cap])
            q.append(idx)
        self.ops.append(op)
        self.last_op[eng] = idx
        return idx

    def mark(self, name):
        self.marks.append((name, sum(1 for o in self.ops if o["eng"] == "act")))

    def barrier(self):
        s = set()
        for e in self.ENGS:
            if self.last_op[e] is not None:
                s.add(self.last_op[e])
        for q in self.dma_ops.values():
            for i in q[-NDSLOT:]:
                s.add(i)
        for e in self.ENGS:
            self.pending_barrier[e] |= s

    def finalize(self):
        for op in self.ops:
            for d in op["deps"]:
                self.ops[d]["signal"] = True
        cnt = {e: 0 for e in self.ENGS}
        for op in self.ops:
            if not op["dma"] and op["signal"]:
                cnt[op["eng"]] += 1
                op["seq"] = cnt[op["eng"]]

    def emit(self, nc, block, sems, dsems):
        attr = {"sp": "sync", "pool": "gpsimd", "pe": "tensor", "dve": "vector", "act": "scalar"}
        ops = self.ops
        final_waits = []
        for qn, q in self.dma_ops.items():
            for i in q[-NDSLOT:]:
                o = ops[i]
                final_waits.append((("d", qn, o["dslot"]), o["dtarget"]))

        def mk(eng):
            def body(e):
                known = {}

                def wait(key, val):
                    if known.get(key, 0) >= val:
                        return
                    known[key] = val
                    if key[0] == "d":
                        e.wait_ge(dsems[key[1]][key[2]], val)
                    else:
                        e.wait_ge(sems[key[1]], val)

                for op in ops:
                    if op["eng"] != eng:
                        continue
                    for d in sorted(op["deps"]):
                        o = ops[d]
                        if o["dma"]:
                            wait(("d", o["eng"], o["dslot"]), o["dtarget"])
                        else:
                            wait(("c", o["eng"]), o["seq"])
                    ins = op["fn"](e)
                    if op["dma"]:
                        ins.then_inc(dsems[eng][op["dslot"]], 16)
                    elif op["signal"]:
                        ins.then_inc(sems[eng], 1)
                if eng == "sp":
                    for key, val in final_waits:
                        wait(key, val)
            return body

        for eng in self.ENGS:
            getattr(block, attr[eng])(mk(eng))


def build_program():
    nc = bass.Bass("TRN2", target_bir_lowering=False)

    def din(name, shape):
        return nc.dram_tensor(name, list(shape), F32, kind="ExternalInput").ap()

    def dout(name, shape):
        return nc.dram_tensor(name, list(shape), F32, kind="ExternalOutput").ap()

    xin = din("xin", [T, 1024])
    memp = din("memp", [256, 1024])
    cca = din("cca", [L, 8, 256])
    ccb = din("ccb", [L, 120, 256])
    ccf = din("ccf", [L, 8, 5632])
    cak = din("cak", [L, 4, 512, 256])
    cav = din("cav", [L, 4, 512, 256])
    cmk = din("cmk", [L, 4, 256, 256])
    cmv = din("cmv", [L, 4, 256, 256])
    w_in = din("w_in", [L, 1024, 2304])
    w_mkv = din("w_mkv", [L, 1024, 512])
    w_out = din("w_out", [L, 1024, 1024])
    w_up = din("w_up", [L, 1024, 5632])
    w_down = din("w_down", [L, 2816, 1024])
    params = din("params", [NPR, 128])
    biasd = din("biasd", [L, 128, 4, 2, 128])
    cvecd = din("cvecd", [L, 128, 4])
    fgbc = din("fgbc", [128, 1024])

    y_o = dout("y_o", [T, 1024])
    o_ca = dout("o_ca", [L, 5, 2, 256])
    o_cb = dout("o_cb", [L, 5, 30, 256])
    o_cf = dout("o_cf", [L, 5, 2, 5632])
    o_k = dout("o_k", [L, 640, 256])
    o_v = dout("o_v", [L, 640, 256])
    o_mk = dout("o_mk", [L, 256, 256])
    o_mv = dout("o_mv", [L, 256, 256])

    P = Prog()
    es = contextlib.ExitStack()
    with es:
        def sb(name, shape, dt):
            return es.enter_context(nc.sbuf_tensor(name, list(shape), dt))

        x = sb("x", [128, 8, T], F32)
        h = sb("h", [128, 8, T], BF16)
        yg = sb("yg", [128, 4, T], BF16)
        wsl = sb("wsl", [128, NSLOT, 8, 256], BF16)
        ident_f = sb("ident_f", [128, 128], F32)
        ident_b = sb("ident_b", [128, 128], BF16)
        ones_b = sb("ones_b", [128, 128], BF16)
        epsr = sb("epsr", [128, 1], F32)
        epsl = sb("epsl", [128, 1], F32)
        PT = sb("PT", [128, NPR], F32)
        memT = sb("memT", [128, 8, 256], BF16)
        haloA = sb("haloA", [128, L, 2, 8], F32)
        haloB = sb("haloB", [128, L, 2, 120], F32)
        haloF = sb("haloF", [128, L, 44, 8], F32)
        stF = sb("stF", [128, 44, 5, 2], F32)
        arena = sb("arena", [128, ARENA_W], F32)
        ps = es.enter_context(nc.psum_tensor("ps", [128, 8, 512], F32))

        sems = {e: es.enter_context(nc.semaphore("c_" + e)) for e in Prog.ENGS if e != "sp"}
        sems["sp"] = sems["pool"]
        dsems = {q: [es.enter_context(nc.semaphore("d_%s%d" % (q, i))) for i in range(NDSLOT)] for q in ("sp", "pool")}

        ar = {"off": 0}

        def areset():
            ar["off"] = 0

        def af(n):
            o = ar["off"]
            ar["off"] = o + n
            assert ar["off"] <= ARENA_W, ("arena overflow", ar["off"])
            return arena[:, o:o + n]

        def ab(n):
            nw = (n + 1) // 2
            o = ar["off"]
            ar["off"] = o + nw
            assert ar["off"] <= ARENA_W, ("arena overflow", ar["off"])
            return arena[:, o:o + nw].bitcast(BF16)[:, 0:n]

        pst = {"s": 0, "p": 0, "single_banks": list(range(8))}

        def ps1():
            b = pst["single_banks"][pst["s"] % len(pst["single_banks"])]
            pst["s"] += 1
            return b

        def ps2():
            b = (pst["p"] % 2) * 2
            pst["p"] += 1
            return b

        def pk(b):
            return ("ps", b)

        wst = {"n": 0}

        def wload(src_ap, nk=8):
            s = wst["n"] % NSLOT
            wst["n"] += 1
            dst = wsl[:, s, 0:nk, :]
            P.add("pool", lambda e, dst=dst, src=src_ap: e.dma_start(out=dst, in_=src.rearrange("(kc p) n -> p kc n", p=128)),
                  r=(), w=[("ws", s)], dma=True, arena=False)
            return s

        def wslot(s):
            return wsl[:, s, :, :]

        def walloc():
            s = wst["n"] % NSLOT
            wst["n"] += 1
            return s

        def mm_group(out_ap, pairs, r, w, skip=False, arena=True):
            n = len(pairs)

            def fn(e, out_ap=out_ap, pairs=pairs):
                ins = None
                for i, (a, b) in enumerate(pairs):
                    ins = e.matmul(out_ap, a, b, start=(i == 0), stop=(i == n - 1))
                return ins
            P.add("pe", fn, r=r, w=w, arena=arena)

        def act(out, in_, func, r, w, bias=None, scale=None):
            kw = {}
            if bias is not None:
                kw["bias"] = bias
            if scale is not None:
                kw["scale"] = scale
            P.add("act", lambda e: e.activation(out, in_, func, **kw), r=r, w=w)

        def dve_tt(out, a, b, op, r, w, arena=True):
            P.add("dve", lambda e: e.tensor_tensor(out, a, b, op), r=r, w=w, arena=arena)

        def dve_stt(out, in0, scalar, in1, op0, op1, r, w):
            P.add("dve", lambda e: e.scalar_tensor_tensor(out=out, in0=in0, scalar=scalar, in1=in1, op0=op0, op1=op1), r=r, w=w)

        def dve_ts(out, in0, s1, s2, op0, op1, r, w):
            if s2 is None:
                P.add("dve", lambda e: e.tensor_scalar(out, in0, s1, None, op0), r=r, w=w)
            else:
                P.add("dve", lambda e: e.tensor_scalar(out, in0, s1, s2, op0, op1), r=r, w=w)

        def dve_recip(out, in_, r, w):
            act(out, in_, AF.Ln, r=r, w=w)
            act(out, out, AF.Exp, r=list(w), w=w, scale=-1.0)

        def sp_dma(out, in_, r, w):
            P.add("sp", lambda e: e.dma_start(out=out, in_=in_), r=r, w=w, dma=True)

        def pcol(l, off, n=1):
            c = PL * l + off
            return PT[:, c:c + n]

        def xk(c, ti):
            return ("x", c, ti)

        def xkeys(ti):
            return [("x", c, ti) for c in range(8)]

        def split3(ap, a):
            return ap.rearrange("p (a b) -> p a b", a=a)

        areset()
        NSTG = 5
        ar["off"] = ARENA_W - NSTG * 1024
        stg = [af(1024) for _ in range(NSTG)]
        P.add("pool", lambda e: e.memset(ident_f[:], 0.0), w=["ident_f"])
        P.add("pool", lambda e: e.affine_select(out=ident_f[:], in_=ident_f[:], pattern=[[-1, 128]], compare_op=ALU.not_equal,
                                                 fill=1.0, base=0, channel_multiplier=1), r=["ident_f"], w=["ident_f"])
        P.add("dve", lambda e: e.tensor_copy(ident_b[:], ident_f[:]), r=["ident_f"], w=["ident_b"])
        P.add("dve", lambda e: e.memset(ones_b[:], 1.0), w=["ones_b"])
        P.add("dve", lambda e: e.memset(epsr[:], 1e-6), w=["epsr"])
        P.add("dve", lambda e: e.memset(epsl[:], 1e-5), w=["epsl"])

        stn = {"n": 0}

        def stage():
            k = stn["n"] % NSTG
            stn["n"] += 1
            return k

        for blk in range(NPR // 128):
            k = stage()
            sp_dma(stg[k][:, 0:128], params[blk * 128:(blk + 1) * 128, :], r=(), w=[("stg", k)])
            b = ps1()
            P.add("pe", lambda e, b=b, k=k: e.transpose(ps[:, b, 0:128], stg[k][:, 0:128], ident_f[:]),
                  r=[("stg", k), "ident_f"], w=[pk(b)])
            act(PT[:, blk * 128:(blk + 1) * 128], ps[:, b, 0:128], AF.Copy, r=[pk(b)], w=["PT"])
        for l in range(L):
            k = stage()
            sp_dma(stg[k][0:8, 0:256], cca[l], r=(), w=[("stg", k)])
            b = ps1()
            for j in range(2):
                P.add("pe", lambda e, b=b, k=k, j=j: e.transpose(ps[:, b, 8 * j:8 * j + 8], stg[k][0:8, 128 * j:128 * j + 128], ident_f[0:8, 0:8]),
                      r=[("stg", k), "ident_f"], w=[pk(b)])
            act(haloA[:, l, :, :], split3(ps[:, b, 0:16], 2), AF.Copy, r=[pk(b)], w=["haloA"])
            k = stage()
            sp_dma(stg[k][0:120, 0:256], ccb[l], r=(), w=[("stg", k)])
            b = ps1()
            for j in range(2):
                P.add("pe", lambda e, b=b, k=k, j=j: e.transpose(ps[:, b, 120 * j:120 * j + 120], stg[k][0:120, 128 * j:128 * j + 128], ident_f[0:120, 0:120]),
                      r=[("stg", k), "ident_f"], w=[pk(b)])
            act(haloB[:, l, :, :], split3(ps[:, b, 0:240], 2), AF.Copy, r=[pk(b)], w=["haloB"])
            for c0 in range(0, 44, 8):
                ncn = min(8, 44 - c0)
                k = stage()
                sp_dma(stg[k][0:8, 0:128 * ncn], ccf[l, :, 128 * c0:128 * (c0 + ncn)], r=(), w=[("stg", k)])
                b = ps1()
                for ci in range(ncn):
                    P.add("pe", lambda e, b=b, k=k, ci=ci: e.transpose(ps[:, b, 8 * ci:8 * ci + 8], stg[k][0:8, 128 * ci:128 * ci + 128], ident_f[0:8, 0:8]),
                          r=[("stg", k), "ident_f"], w=[pk(b)])
                act(haloF[:, l, c0:c0 + ncn, :], split3(ps[:, b, 0:8 * ncn], ncn), AF.Copy, r=[pk(b)], w=["haloF"])
        for blk in range(2):
            k = stage()
            sp_dma(stg[k][:, :], memp[blk * 128:(blk + 1) * 128, :], r=(), w=[("stg", k)])
            for half in range(2):
                b = ps1()
                for ci in range(4):
                    c = half * 4 + ci
                    P.add("pe", lambda e, b=b, k=k, c=c, ci=ci: e.transpose(ps[:, b, 128 * ci:128 * ci + 128], stg[k][:, 128 * c:128 * c + 128], ident_f[:]),
                          r=[("stg", k), "ident_f"], w=[pk(b)])
                act(memT[:, half * 4:half * 4 + 4, blk * 128:(blk + 1) * 128], split3(ps[:, b, :], 4), AF.Copy, r=[pk(b)], w=["memT"])
        for blk in range(17):
            k = stage()
            ti = min(blk // 4, 4)
            sp_dma(stg[k][:, :], xin[blk * 128:(blk + 1) * 128, :], r=(), w=[("stg", k)])
            for half in range(2):
                b = ps1()
                for ci in range(4):
                    c = half * 4 + ci
                    P.add("pe", lambda e, b=b, k=k, c=c, ci=ci: e.transpose(ps[:, b, 128 * ci:128 * ci + 128], stg[k][:, 128 * c:128 * c + 128], ident_f[:]),
                          r=[("stg", k), "ident_f"], w=[pk(b)])
                wk = [xk(half * 4 + ci, ti) for ci in range(4)]
                if half == 0:
                    act(x[:, 0:4, blk * 128:(blk + 1) * 128], split3(ps[:, b, :], 4), AF.Copy, r=[pk(b)], w=wk)
                else:
                    P.add("dve", lambda e, b=b, blk=blk: e.tensor_copy(x[:, 4:8, blk * 128:(blk + 1) * 128], split3(ps[:, b, :], 4)), r=[pk(b)], w=wk)

        def rms_rstd(ss_bank, n, inv_n, eps_t, sd, rstd, keys_w):
            act(sd[:, 0:n], ps[:, ss_bank, 0:n], AF.Ln, r=[pk(ss_bank), "epsr", "epsl"], w=[keys_w + "_sd"], bias=eps_t[:, 0:1], scale=inv_n)
            act(rstd[:, 0:n], sd[:, 0:n], AF.Exp, r=[keys_w + "_sd"], w=[keys_w], scale=-0.5)

        def do_norm(l, goff):
            P.mark("norm L%d" % l)
            if not (l == 0 and goff == O_N1):
                P.barrier()
            areset()
            sqs = [split3(ab(8 * 512), 8), split3(ab(8 * 512), 8)]
            assert ar["off"] + 2048 <= ARENA_W - 5 * 1024
            sds = [af(512), af(512)]
            rss = [af(512), af(512)]
            pipe = Pipe()
            for ti, (t0, n) in enumerate(TILES):
                k = ti % 2
                b = ps1()

                def s0(ti=ti, t0=t0, n=n, k=k):
                    act(sqs[k][:, :, 0:n], x[:, :, t0:t0 + n], AF.Square, r=xkeys(ti), w=[("nsq", k)])

                def s1(n=n, k=k, b=b):
                    mm_group(ps[:, b, 0:n], [(ones_b[:], sqs[k][:, c, 0:n]) for c in range(8)], r=[("nsq", k), "ones_b"], w=[pk(b)])

                def s2(n=n, k=k, b=b):
                    rms_rstd(b, n, 1.0 / 1024, epsr, sds[k], rss[k], "nrs%d" % k)

                def s3(ti=ti, t0=t0, n=n, k=k):
                    for c in range(8):
                        dve_stt(h[:, c, t0:t0 + n], x[:, c, t0:t0 + n], pcol(l, goff + c), rss[k][:, 0:n], ALU.mult, ALU.mult,
                                r=[xk(c, ti), "nrs%d" % k, "PT"], w=[("h", ti)])
                pipe.item([s0, s1, s2, s3])
            pipe.flush()

        def gn_square(ygrp_k, kk, n, sq2):
            act(sq2[:, :, 0:n], ygrp_k[:, :, 0:n], AF.Square, r=[("ygrp", kk)], w=["gsq"])

        def gn_rest(l, gi, ygrp_k, kk, ti, t0, n, sq2, sd, rstd, b):
            mm_group(ps[:, b, 0:n], [(ones_b[:], sq2[:, j, 0:n]) for j in range(2)], r=["gsq", "ones_b"], w=[pk(b)])
            rms_rstd(b, n, 1.0 / 256, epsr, sd, rstd, "grs")
            for j in range(2):
                dve_stt(yg[:, (gi % 2) * 2 + j, t0:t0 + n], ygrp_k[:, j, 0:n], pcol(l, O_GN + 2 * gi + j), rstd[:, 0:n], ALU.mult, ALU.mult,
                        r=[("ygrp", kk), "grs", "PT"], w=[("yg", ti)])

        def group_norm(l, gi, ygrp_k, kk, ti, t0, n, sq2, sd, rstd):
            gn_square(ygrp_k, kk, n, sq2)
            gn_rest(l, gi, ygrp_k, kk, ti, t0, n, sq2, sd, rstd, ps1())

        def h_mm(slot, c0, ti, t0, n, b):
            ws_ = wslot(slot)
            mm_group(ps[:, b, 0:n], [(ws_[:, kc, c0:c0 + 128], h[:, kc, t0:t0 + n]) for kc in range(8)],
                     r=[("ws", slot), ("h", ti)], w=[pk(b)], arena=False)

        def state_out(l, dst, srcs, nr, stgs, keys_r):
            nch = len(srcs[0])
            cmax = stgs[0].shape[1] // 128
            q = 0
            for s in range(5):
                for c0 in range(0, nch, cmax):
                    cn = min(cmax, nch - c0)
                    b = ps1()
                    for ci in range(cn):
                        P.add("pe", lambda e, b=b, ci=ci, src=srcs[s][c0 + ci]: e.transpose(ps[0:nr, b, 128 * ci:128 * ci + 128], src, ident_f[:]),
                              r=keys_r + ["ident_f"], w=[pk(b)])
                    k = q % 2
                    q += 1
                    kname = ("sostg", k)
                    act(stgs[k][0:nr, 0:128 * cn], ps[0:nr, b, 0:128 * cn], AF.Copy, r=[pk(b)], w=[kname])
                    sp_dma(dst[l, s, :, 128 * c0:128 * (c0 + cn)], stgs[k][0:nr, 0:128 * cn], r=[kname], w=())

        def state_out_merged(l, dst, srcs, nr, stg_ap, keys_r, groups):
            for grp in groups:
                ns = len(grp)
                for p0 in range(0, ns, 2):
                    pair = grp[p0:p0 + 2]
                    b = ps1()
                    for si, sq_ in enumerate(pair):
                        for j in range(2):
                            P.add("pe", lambda e, b=b, col=256 * si + 128 * j, src=srcs[sq_][j]: e.transpose(ps[0:nr, b, col:col + 128], src, ident_f[:]),
                                  r=keys_r + ["ident_f"], w=[pk(b)])
                    act(stg_ap[0:nr, 256 * p0:256 * (p0 + len(pair))], ps[0:nr, b, 0:256 * len(pair)], AF.Copy, r=[pk(b)], w=["sostg_m"])
                sp_dma(dst[l, grp[0]:grp[0] + ns].rearrange("s r c -> r s c"),
                       stg_ap[0:nr, 0:256 * ns].rearrange("r (s c) -> r s c", s=ns), r=["sostg_m"], w=())

        def wout_load(l, pi):
            return [wload(w_out[l, 512 * pi:512 * pi + 512, 256 * dp:256 * dp + 256], nk=4) for dp in range(4)]

        def wout_tile(l, pi, slots, ti):
            t0, n = TILES[ti]
            bank_list = pst["single_banks"]
            for dc in range(8):
                ws_ = wslot(slots[dc // 2])
                dsub = dc % 2
                b = ps1()
                mm_group(ps[:, b, 0:n], [(ws_[:, kc, 128 * dsub:128 * dsub + 128], yg[:, kc, t0:t0 + n]) for kc in range(4)],
                         r=[("ws", slots[dc // 2]), ("yg", ti)], w=[pk(b)], arena=False)
                dve_tt(x[:, dc, t0:t0 + n], x[:, dc, t0:t0 + n], ps[:, b, 0:n], ALU.add, r=[pk(b), xk(dc, ti)], w=[xk(dc, ti)], arena=False)

        def wout_pair(l, pi):
            slots = wout_load(l, pi)
            for ti in range(5):
                wout_tile(l, pi, slots, ti)

        class Pipe:
            def __init__(self):
                self.items = []

            def _advance(self):
                nI = len(self.items)
                for idx, it in enumerate(self.items):
                    k = nI - idx
                    if k < len(it):
                        it[k]()

            def item(self, stages):
                self._advance()
                stages[0]()
                self.items.append(stages)
                self.items = self.items[-9:]

            def flush(self):
                for _ in range(9):
                    self._advance()
                    self.items.append([lambda: None])
                    self.items = self.items[-9:]
                self.items = []

        def ffn_item(l, c, cl, sub, sv_, sg_, ti, t0, n, ku, kprev, u, U, tv, tg, sgf):
            bv, bg = ps1(), ps1()
            Uk = U[ku]
            if ti < 4:
                dv, dgt = Uk[:, 0, 2:2 + n], Uk[:, 1, 2:2 + n]
                pv_, pg_ = ps[:, bv, 0:n], ps[:, bg, 0:n]
                shv = [Uk[:, 0, s:s + n] for s in range(3)]
                shg = [Uk[:, 1, s:s + n] for s in range(3)]
                tvo, tgo, sgo = tv[u][:, 0:n], tg[u][:, 0:n], sgf[u % 2][:, 0:n]
                go = yg[:, cl, t0:t0 + n]
                Us = None
            else:
                Us = Uk[:, :, 0:136].rearrange("p a (b c) -> p a b c", b=4)
                dv, dgt = Us[:, 0, :, 2:34], Us[:, 1, :, 2:34]
                pv_, pg_ = split3(ps[:, bv, 0:n], 4), split3(ps[:, bg, 0:n], 4)
                shv = [Us[:, 0, :, s:s + 32] for s in range(3)]
                shg = [Us[:, 1, :, s:s + 32] for s in range(3)]
                tvo, tgo, sgo = split3(tv[u][:, 0:n], 4), split3(tg[u][:, 0:n], 4), split3(sgf[u % 2][:, 0:n], 4)
                go = split3(yg[:, cl, t0:t0 + n], 4)

            def s1():
                h_mm(sv_, 128 * sub, ti, t0, n, bv)
                h_mm(sg_, 128 * sub, ti, t0, n, bg)
                if ti == 0:
                    P.add("dve", lambda e, d=Uk[:, :, 0:2]: e.memset(d, 0.0), w=[("U", ku)])
                elif ti < 4:
                    P.add("act", lambda e, d=Uk[:, :, 0:2], s_=U[kprev][:, :, 512:514]: e.activation(d, s_, AF.Copy),
                          r=[("U", kprev)], w=[("U", ku)])
                else:
                    P.add("act", lambda e, d=Us[:, 0, :, 0:2], s_=haloF[:, l, c, :].rearrange("p (b c) -> p b c", b=4): e.activation(d, s_, AF.Copy),
                          r=["haloF"], w=[("U", ku)])
                    P.add("act", lambda e, d=Us[:, 1, :, 0:2], s_=haloF[:, l, 22 + c, :].rearrange("p (b c) -> p b c", b=4): e.activation(d, s_, AF.Copy),
                          r=["haloF"], w=[("U", ku)])
                act(dv, pv_, AF.Copy, r=[pk(bv)], w=[("U", ku)])
                act(dgt, pg_, AF.Copy, r=[pk(bg)], w=[("U", ku)])
                act(tvo, pv_, AF.Identity, r=[pk(bv), "PT"], w=[("tv", u)], bias=pcol(l, O_FCB + c), scale=pcol(l, O_FCW + 3 * c + 2))
                act(tgo, pg_, AF.Identity, r=[pk(bg), "PT"], w=[("tg", u)], bias=pcol(l, O_FCB + 22 + c), scale=pcol(l, O_FCW + 3 * (22 + c) + 2))
                if ti == 3:
                    P.add("act", lambda e, d=stF[:, c, 0, :], s_=Uk[:, 0, 512:514]: e.activation(d, s_, AF.Copy), r=[("U", ku)], w=[("stF", c)])
                    P.add("act", lambda e, d=stF[:, 22 + c, 0, :], s_=Uk[:, 1, 512:514]: e.activation(d, s_, AF.Copy), r=[("U", ku)], w=[("stF", 22 + c)])
                if ti == 4:
                    P.add("act", lambda e, d=stF[:, c, 1:5, :], s_=Us[:, 0, :, 32:34]: e.activation(d, s_, AF.Copy), r=[("U", ku)], w=[("stF", c)])
                    P.add("act", lambda e, d=stF[:, 22 + c, 1:5, :], s_=Us[:, 1, :, 32:34]: e.activation(d, s_, AF.Copy), r=[("U", ku)], w=[("stF", 22 + c)])

            def s2():
                for (to, sh_, cc_) in ((tvo, shv, c), (tgo, shg, 22 + c)):
                    key = ("tv", u) if cc_ == c else ("tg", u)
                    dve_stt(to, sh_[1], pcol(l, O_FCW + 3 * cc_ + 1), to, ALU.mult, ALU.add, r=[("U", ku), key, "PT"], w=[key])
                    dve_stt(to, sh_[0], pcol(l, O_FCW + 3 * cc_ + 0), to, ALU.mult, ALU.add, r=[("U", ku), key, "PT"], w=[key])

            def s3():
                act(sgo, tgo, AF.Silu, r=[("tg", u)], w=[("sgf", u % 2)])

            def s4():
                P.add("pool", lambda e: e.tensor_tensor(go, tvo, sgo, ALU.mult), r=[("tv", u), ("sgf", u % 2)], w=[("yg", ti)])
            return [s1, s2, s3, s4]

        for l in range(L):
            do_norm(l, O_N1)

            P.mark("A L%d" % l)
            P.barrier()
            areset()
            s_ab = wload(w_in[l, :, 0:256])
            s_ac = wload(w_in[l, :, 256:512])
            s_ah = wload(w_in[l, :, 512:768])
            s_bu = wload(w_in[l, :, 768:1024])
            s_bg = wload(w_in[l, :, 1024:1280])
            pA = split3(af(2 * 2050), 2)
            pAs = af(2 * 4 * 34).rearrange("p (j b c) -> p j b c", j=2, b=4)
            hh = [af(512), af(512)]
            acc = [af(512), af(512)]
            ygrp = [split3(af(1024), 2), split3(af(1024), 2)]
            sq2 = split3(ab(1024), 2)
            sd = af(512)
            rstd = af(512)
            sostg = af(1280)
            P.add("dve", lambda e, d=pA[:, :, 0:2]: e.memset(d, 0.0), w=[("pA", 0), ("pA", 1)])
            P.add("dve", lambda e, d=pAs[:, :, :, 0:2], s_=haloA[:, l, :, :].rearrange("p j (b c) -> p j b c", b=4): e.tensor_copy(d, s_),
                  r=["haloA"], w=[("pA", 0), ("pA", 1)])
            cnt = 0
            pipe = Pipe()
            for ti, (t0, n) in enumerate(TILES):
                kk = ti % 2
                for j in range(2):
                    u = cnt % 2
                    cnt += 1
                    bb, bc, bh = ps1(), ps1(), ps1()
                    if ti < 4:
                        data = pA[:, j, 2 + t0:2 + t0 + n]
                        sh = [pA[:, j, t0 + s:t0 + s + n] for s in range(3)]
                        pin, hin, ao, pbin, yo = ps[:, bc, 0:n], hh[u][:, 0:n], acc[u][:, 0:n], ps[:, bb, 0:n], ygrp[kk][:, j, 0:n]
                    else:
                        data = pAs[:, j, :, 2:34]
                        sh = [pAs[:, j, :, s:s + 32] for s in range(3)]
                        pin, hin, ao = split3(ps[:, bc, 0:n], 4), split3(hh[u][:, 0:n], 4), split3(acc[u][:, 0:n], 4)
                        pbin, yo = split3(ps[:, bb, 0:n], 4), split3(ygrp[kk][:, j, 0:n], 4)

                    def s0(j=j, ti=ti, t0=t0, n=n, bb=bb, bc=bc, bh=bh, u=u):
                        h_mm(s_ab, 128 * j, ti, t0, n, bb)
                        h_mm(s_ac, 128 * j, ti, t0, n, bc)
                        h_mm(s_ah, 128 * j, ti, t0, n, bh)
                        act(hh[u][:, 0:n], ps[:, bh, 0:n], AF.Copy, r=[pk(bh)], w=[("hh", u)])

                    def s1(j=j, n=n, u=u, kk=kk, bb=bb, bc=bc, data=data, sh=sh, pin=pin, hin=hin, ao=ao, pbin=pbin, yo=yo, l=l, sq2=sq2, ygrp=ygrp):
                        dve_tt(data, pin, hin, ALU.mult, r=[pk(bc), ("hh", u)], w=[("pA", j)])
                        dve_ts(ao, sh[2], pcol(l, O_CAW + 3 * j + 2), None, ALU.mult, None, r=[("pA", j), "PT"], w=[("acc", u)])
                        dve_stt(ao, sh[1], pcol(l, O_CAW + 3 * j + 1), ao, ALU.mult, ALU.add, r=[("pA", j), ("acc", u), "PT"], w=[("acc", u)])
                        dve_stt(ao, sh[0], pcol(l, O_CAW + 3 * j + 0), ao, ALU.mult, ALU.add, r=[("pA", j), ("acc", u), "PT"], w=[("acc", u)])
                        dve_tt(yo, pbin, ao, ALU.mult, r=[pk(bb), ("acc", u)], w=[("ygrp", kk)])
                        if j == 1:
                            gn_square(ygrp[kk], kk, n, sq2)
                    stages = [s0, s1]
                    if j == 1:
                        bgn = ps1()

                        def s2(l=l, kk=kk, ti=ti, t0=t0, n=n, bgn=bgn, ygrp=ygrp, sq2=sq2, sd=sd, rstd=rstd):
                            gn_rest(l, 0, ygrp[kk], kk, ti, t0, n, sq2, sd, rstd, bgn)
                        stages.append(s2)
                    pipe.item(stages)
            pipe.flush()
            state_out_merged(l, o_ca, [[pA[:, j, 2048:2050] for j in range(2)]] + [[pAs[:, j, b, 32:34] for j in range(2)] for b in range(4)],
                             2, sostg, [("pA", 0), ("pA", 1)], [[0, 1, 2, 3, 4]])

            P.mark("B L%d" % l)
            P.barrier()
            areset()
            pBl = split3(af(2 * 30), 2)
            pBs = af(2 * 4 * 62).rearrange("p (j b c) -> p j b c", j=2, b=4)
            pBb = split3(ab(2 * 2078), 2)
            pBbs = ab(2 * 4 * 62).rearrange("p (j b c) -> p j b c", j=2, b=4)
            sg = [af(512), af(512)]
            ygrp = [split3(af(1024), 2), split3(af(1024), 2)]
            ycb = split3(ab(1024), 2)
            sqb = split3(ab(1024), 2)
            sq2 = split3(ab(1024), 2)
            mt = af(512)
            msq = af(512)
            lsd = af(512)
            lrs = af(512)
            sd = af(512)
            rstd = af(512)
            dtmp = [af(512), msq]
            sostg = af(768)
            dslots = [(walloc(), walloc()) for j in range(2)]

            def dg(j, k, dslots=dslots):
                s = dslots[j][k // 16]
                return wsl[:, s, :, :].rearrange("p a b -> p (a b)")[:, 128 * (k % 16):128 * (k % 16) + 128]
            for j in range(2):
                for k in range(31):
                    P.add("dve", lambda e, d=dg(j, k), sc=pcol(l, O_CBW + 31 * j + k): e.tensor_scalar(d, ident_b[:], sc, None, ALU.mult),
                          r=["ident_b", "PT"], w=[("ws", dslots[j][k // 16])])
            P.add("dve", lambda e, d=pBb[:, :, 0:30]: e.memset(d, 0.0), w=[("pBb", 0), ("pBb", 1)])
            hB = haloB[:, l, :, :].rearrange("p j (b c) -> p j b c", b=4)
            P.add("dve", lambda e, d=pBs[:, :, :, 0:30], s_=hB: e.tensor_copy(d, s_), r=["haloB"], w=[("pB", 0), ("pB", 1)])
            P.add("dve", lambda e, d=pBbs[:, :, :, 0:30], s_=hB: e.tensor_copy(d, s_), r=["haloB"], w=[("pBb", 0), ("pBb", 1)])
            cnt = 0
            pipe = Pipe()
            for ti, (t0, n) in enumerate(TILES):
                kk = ti % 2
                for j in range(2):
                    u = cnt % 2
                    cnt += 1
                    bu, bg, bcv = ps1(), ps1(), ps1()
                    if ti < 4:
                        taps = [(dg(j, k), pBb[:, j, t0 + k:t0 + k + n]) for k in range(31)]
                        cout = ps[:, bcv, 0:n]
                    else:
                        taps = [(dg(j, k), pBbs[:, j, :, k:k + 32]) for k in range(31)]
                        cout = split3(ps[:, bcv, 0:n], 4)

                    def s0(j=j, ti=ti, t0=t0, n=n, bu=bu, bg=bg, u=u, sg=sg):
                        h_mm(s_bu, 128 * j, ti, t0, n, bu)
                        h_mm(s_bg, 128 * j, ti, t0, n, bg)
                        act(sg[u][:, 0:n], ps[:, bg, 0:n], AF.Sigmoid, r=[pk(bg)], w=[("sg", u)])

                    def s1(j=j, ti=ti, t0=t0, n=n, bu=bu, u=u, sg=sg, pBb=pBb, pBl=pBl, pBs=pBs, pBbs=pBbs):
                        if ti < 4:
                            dve_tt(pBb[:, j, 30 + t0:30 + t0 + n], ps[:, bu, 0:n], sg[u][:, 0:n], ALU.mult, r=[pk(bu), ("sg", u)], w=[("pBb", j)])
                            if ti == 3:
                                dve_tt(pBl[:, j, :], ps[:, bu, n - 30:n], sg[u][:, n - 30:n], ALU.mult, r=[pk(bu), ("sg", u)], w=[("pB", j)])
                        else:
                            dve_tt(pBs[:, j, :, 30:62], split3(ps[:, bu, 0:n], 4), split3(sg[u][:, 0:n], 4), ALU.mult, r=[pk(bu), ("sg", u)], w=[("pB", j)])
                            act(pBbs[:, j, :, 30:62], pBs[:, j, :, 30:62], AF.Copy, r=[("pB", j)], w=[("pBb", j)])

                    def s2(j=j, n=n, kk=kk, bcv=bcv, taps=taps, cout=cout, dslots=dslots, ygrp=ygrp, l=l, ycb=ycb, sqb=sqb):
                        mm_group(cout, taps, r=[("pBb", j), ("ws", dslots[j][0]), ("ws", dslots[j][1])], w=[pk(bcv)])
                        act(ygrp[kk][:, j, 0:n], ps[:, bcv, 0:n], AF.Identity, r=[pk(bcv), "PT"], w=[("ygrp", kk)], bias=pcol(l, O_CBB + j))
                        if j == 1:
                            act(ycb[:, :, 0:n], ygrp[kk][:, :, 0:n], AF.Copy, r=[("ygrp", kk)], w=["ycb"])
                            act(sqb[:, :, 0:n], ygrp[kk][:, :, 0:n], AF.Square, r=[("ygrp", kk)], w=["sqb"])
                    stages = [s0, s1, s2]
                    if j == 1:
                        b1, b2, bgn = ps1(), ps1(), ps1()

                        def s3(n=n, b1=b1, b2=b2, ycb=ycb, sqb=sqb, mt=mt, msq=msq, lsd=lsd, lrs=lrs):
                            mm_group(ps[:, b1, 0:n], [(ones_b[:], ycb[:, jj, 0:n]) for jj in range(2)], r=["ycb", "ones_b"], w=[pk(b1)])
                            mm_group(ps[:, b2, 0:n], [(ones_b[:], sqb[:, jj, 0:n]) for jj in range(2)], r=["sqb", "ones_b"], w=[pk(b2)])
                            act(mt[:, 0:n], ps[:, b1, 0:n], AF.Copy, r=[pk(b1)], w=["mt"], scale=1.0 / 256)
                            dve_tt(msq[:, 0:n], mt[:, 0:n], mt[:, 0:n], ALU.mult, r=["mt"], w=["msq"])
                            dve_stt(msq[:, 0:n], ps[:, b2, 0:n], 1.0 / 256, msq[:, 0:n], ALU.mult, ALU.subtract, r=[pk(b2), "msq"], w=["msq"])
                            act(lsd[:, 0:n], msq[:, 0:n], AF.Ln, r=["msq", "epsl"], w=["lsd"], bias=epsl[:, 0:1], scale=1.0)
                            act(lrs[:, 0:n], lsd[:, 0:n], AF.Exp, r=["lsd"], w=["lrs"], scale=-0.5)

                        def s4(n=n, kk=kk, ygrp=ygrp, mt=mt, lrs=lrs, dtmp=dtmp, l=l, sq2=sq2):
                            yk = ygrp[kk]
                            dk = ["dtmp0", "msq"]
                            for jj in range(2):
                                dve_tt(dtmp[jj][:, 0:n], yk[:, jj, 0:n], mt[:, 0:n], ALU.subtract, r=[("ygrp", kk), "mt"], w=[dk[jj]])
                                dve_tt(dtmp[jj][:, 0:n], dtmp[jj][:, 0:n], lrs[:, 0:n], ALU.mult, r=[dk[jj], "lrs"], w=[dk[jj]])
                            for jj in range(2):
                                act(yk[:, jj, 0:n], dtmp[jj][:, 0:n], AF.Silu, r=[dk[jj], "PT"], w=[("ygrp", kk)],
                                    bias=pcol(l, O_LNB + jj), scale=pcol(l, O_LNG + jj))
                            gn_square(yk, kk, n, sq2)

                        def s5(l=l, kk=kk, ti=ti, t0=t0, n=n, bgn=bgn, ygrp=ygrp, sq2=sq2, sd=sd, rstd=rstd):
                            gn_rest(l, 1, ygrp[kk], kk, ti, t0, n, sq2, sd, rstd, bgn)
                        stages += [s3, s4, s5]
                    pipe.item(stages)
            pipe.flush()
            state_out_merged(l, o_cb, [[pBl[:, j, :] for j in range(2)]] + [[pBs[:, j, b, 32:62] for j in range(2)] for b in range(4)],
                             30, sostg, [("pB", 0), ("pB", 1)], [[0, 1, 2], [3, 4]])

            P.mark("woutAB L%d" % l)
            wout_pair(l, 0)

            P.mark("C L%d" % l)
            P.barrier()
            areset()
            s_q = wload(w_in[l, :, 1280:1536])
            s_k = wload(w_in[l, :, 1536:1792])
            s_v = wload(w_in[l, :, 1792:2048])
            bT = af(4 * 2 * 128).rearrange("p (h r c) -> p h r c", h=4, r=2)
            cv = af(4)
            rec = [af(512), af(512)]
            ygrp = [split3(af(1024), 2), split3(af(1024), 2)]
            kvo = [af(512), af(512)]
            sq2 = split3(ab(1024), 2)
            sd = af(512)
            rstd = af(512)
            cmark = ar["off"]
            Q = [split3(ab(1024), 2), split3(ab(1024), 2)]
            Kr = split3(ab(2048), 2)
            Vr = ab(10 * 4 * 128).rearrange("p (s h c) -> p s h c", s=10, h=4)
            PTs = [ab(640), ab(640)]
            P.add("dve", lambda e, d=Vr[:, :, :, 64:128]: e.memset(d, 1.0), w=["Vr"])
            sp_dma(bT.rearrange("p h r c -> p (h r c)"), biasd[l].rearrange("p h r c -> p (h r c)"), r=(), w=["bT"])
            sp_dma(cv, cvecd[l], r=(), w=["cv"])
            for hd in range(4):
                dve_ts(bT[:, hd, :, :], bT[:, hd, :, :], cv[:, hd:hd + 1], None, ALU.subtract, None, r=["bT", "cv"], w=["bT"])
            pst["single_banks"] = [6, 7]
            ocnt = 0
            for ti in range(4):
                t0, n = TILES[ti]
                kk = ti % 2
                for j in range(2):
                    b = ps1()
                    h_mm(s_q, 128 * j, ti, t0, n, b)
                    act(Q[kk][:, j, :], ps[:, b, :], AF.Copy, r=[pk(b)], w=[("Q", kk)], scale=0.125)
                for j in range(2):
                    b = ps1()
                    h_mm(s_k, 128 * j, ti, t0, n, b)
                    c0 = ((4 * ti) % 8) * 128
                    act(Kr[:, j, c0:c0 + 512], ps[:, b, :], AF.Copy, r=[pk(b)], w=["Kr"])
                for blk in range(4):
                    tb = 4 * ti + blk
                    b = ps1()
                    vsl = wslot(s_v)
                    ksl = wslot(s_k)
                    if ti == 3:
                        mm_group(ps[:, b, 0:256], [(h[:, kc, tb * 128:tb * 128 + 128], ksl[:, kc, :]) for kc in range(8)],
                                 r=[("ws", s_k), ("h", ti)], w=[pk(b)])
                    mm_group(ps[:, b, 256:512], [(h[:, kc, tb * 128:tb * 128 + 128], vsl[:, kc, :]) for kc in range(8)],
                             r=[("ws", s_v), ("h", ti)], w=[pk(b)])
                    act(Vr[:, tb % 10, :, 0:64], split3(ps[:, b, 256:512], 4), AF.Copy, r=[pk(b)], w=["Vr"])
                    if ti == 3:
                        ko = ocnt % 2
                        ocnt += 1
                        act(kvo[ko][:, :], ps[:, b, :], AF.Copy, r=[pk(b)], w=[("kvo", ko)])
                        sp_dma(o_k[l, blk * 128:blk * 128 + 128, :], kvo[ko][:, 0:256], r=[("kvo", ko)], w=())
                        sp_dma(o_v[l, blk * 128:blk * 128 + 128, :], kvo[ko][:, 256:512], r=[("kvo", ko)], w=())
                if ti == 0:
                    pipe = Pipe()
                for qb in range(4):
                    i = 4 * ti + qb
                    r0 = max(0, 4 - i)
                    bo = 4 + (qb % 2)
                    ro = qb % 2
                    for hd in range(4):
                        j, pb = hd // 2, 64 * (hd % 2)
                        b2 = ps2()
                        S = ps[:, b2:b2 + 2, :].rearrange("p a b -> p (a b)")
                        qv = Q[kk][pb:pb + 64, j, qb * 128:qb * 128 + 128]
                        pu = hd % 2
                        O = ps[:, bo, hd * 128:hd * 128 + 128]
                        skeys = [pk(b2), pk(b2 + 1)]

                        def s0(S=S, qv=qv, i=i, r0=r0, j=j, pb=pb, kk=kk, skeys=skeys, Kr=Kr):
                            for r_ in range(r0, 5):
                                c0 = ((i - 4 + r_) % 8) * 128
                                mm_group(S[:, r_ * 128:r_ * 128 + 128], [(Kr[pb:pb + 64, j, c0:c0 + 128], qv)], r=["Kr", ("Q", kk)], w=skeys)

                        def s1(S=S, r0=r0, hd=hd, skeys=skeys, bT=bT):
                            r3 = max(r0, 3)
                            dve_tt(S[:, r3 * 128:640], S[:, r3 * 128:640], bT[:, hd, r3 - 3:2, :].rearrange("p r c -> p (r c)"), ALU.add,
                                   r=skeys + ["bT"], w=skeys)
                            if r0 == 0:
                                dve_ts(S[0:64, 64:128], S[0:64, 64:128], -1e30, None, ALU.add, None, r=skeys, w=skeys)
                            dve_ts(S[64:128, 512:576], S[64:128, 512:576], -1e30, None, ALU.add, None, r=skeys, w=skeys)

                        def s2(S=S, r0=r0, hd=hd, pu=pu, skeys=skeys, PTs=PTs, cv=cv):
                            act(PTs[pu][:, r0 * 128:640], S[:, r0 * 128:640], AF.Exp, r=skeys, w=[("PTs", pu)])

                        def s3(O=O, i=i, r0=r0, hd=hd, pu=pu, bo=bo, Vr=Vr, PTs=PTs):
                            mm_group(O, [(Vr[:, (i - 4 + r_) % 10, hd, :], PTs[pu][:, r_ * 128:r_ * 128 + 128]) for r_ in range(r0, 5)],
                                     r=["Vr", ("PTs", pu)], w=[pk(bo)])
                        stages = [s0, s1, s2, s3]
                        if hd == 3:
                            def s4(bo=bo, ro=ro, rec=rec):
                                dve_recip(rec[ro][64:128, :], ps[64:128, bo, :], r=[pk(bo)], w=[("rec", ro)])

                            def s5(bo=bo, ro=ro, rec=rec, kk=kk, qb=qb, ygrp=ygrp):
                                oin = ps[0:64, bo, :].rearrange("p (h c) -> p h c", h=4)
                                rin = rec[ro][64:128, :].rearrange("p (h c) -> p h c", h=4)
                                for par in range(2):
                                    for jj in range(2):
                                        hh_ = 2 * jj + par
                                        dve_tt(ygrp[kk][64 * par:64 * par + 64, jj, qb * 128:qb * 128 + 128], oin[:, hh_, :], rin[:, hh_, :], ALU.mult,
                                               r=[pk(bo), ("rec", ro)], w=[("ygrp", kk)])
                            stages += [s4, s5]
                            if qb == 3:
                                bgn = ps1()

                                def s6(kk=kk, n=n, ygrp=ygrp, sq2=sq2):
                                    gn_square(ygrp[kk], kk, n, sq2)

                                def s7(l=l, kk=kk, ti=ti, t0=t0, n=n, ygrp=ygrp, sq2=sq2, sd=sd, rstd=rstd, bgn=bgn):
                                    gn_rest(l, 2, ygrp[kk], kk, ti, t0, n, sq2, sd, rstd, bgn)
                                stages += [s6, s7]
                        pipe.item(stages)
            pipe.flush()
            P.mark("Csample L%d" % l)
            pst["single_banks"] = [4, 5, 6, 7]
            P.barrier()
            ar["off"] = cmark
            stk = af(1024)
            stv = af(1024)
            kcT = split3(ab(1024), 2)
            Vc = ab(4 * 4 * 128).rearrange("p (s h c) -> p s h c", s=4, h=4)
            Vs = ab(4 * 128).rearrange("p (h c) -> p h c", h=4)
            Ks = split3(ab(256), 2)
            Qs = split3(ab(256), 2)
            PTa = ab(640).rearrange("p (h r c) -> p h r c", h=4, r=5)
            P.add("dve", lambda e, d=Vc[:, :, :, 64:128]: e.memset(d, 1.0), w=["Vc"])
            P.add("dve", lambda e, d=Vs[:, :, 64:128]: e.memset(d, 1.0), w=["Vs"])
            ti = 4
            t0, n = TILES[4]
            kk = 0
            for j in range(2):
                b = ps1()
                h_mm(s_q, 128 * j, ti, t0, n, b)
                act(Qs[:, j, :], ps[:, b, 0:128], AF.Copy, r=[pk(b)], w=["Qs"], scale=0.125)
                b = ps1()
                h_mm(s_k, 128 * j, ti, t0, n, b)
                act(Ks[:, j, :], ps[:, b, 0:128], AF.Copy, r=[pk(b)], w=["Ks"])
            pipe = Pipe()
            for bi in range(4):
                tb0 = TP + 32 * bi
                bkv = ps1()
                bt0, bt1 = ps1(), ps1()
                b2 = ps2()
                bo = 4 + bi % 2 if False else ps1()
                ro = bi % 2
                ko = bi % 2

                def t0_(bi=bi, stk=stk, stv=stv):
                    sp_dma(split3(stk, 4), cak[l, bi].rearrange("(s p) c -> p s c", p=128), r=(), w=["stk"])
                    sp_dma(split3(stv, 4), cav[l, bi].rearrange("(s p) c -> p s c", p=128), r=(), w=["stv"])

                def t1_(bi=bi, tb0=tb0, bkv=bkv, bt=(bt0, bt1), ko=ko, stk=stk, stv=stv, kcT=kcT, Vc=Vc, Vs=Vs, kvo=kvo):
                    mm_group(ps[0:32, bkv, 0:256], [(h[:, kc, tb0:tb0 + 32], wslot(s_k)[:, kc, :]) for kc in range(8)], r=[("ws", s_k), ("h", 4)], w=[pk(bkv)])
                    mm_group(ps[0:32, bkv, 256:512], [(h[:, kc, tb0:tb0 + 32], wslot(s_v)[:, kc, :]) for kc in range(8)], r=[("ws", s_v), ("h", 4)], w=[pk(bkv)])
                    act(Vs[0:32, :, 0:64], split3(ps[0:32, bkv, 256:512], 4), AF.Copy, r=[pk(bkv)], w=["Vs"])
                    act(kvo[ko][0:32, :], ps[0:32, bkv, :], AF.Copy, r=[pk(bkv)], w=[("kvo", ko)])
                    sp_dma(o_k[l, 512 + 32 * bi:512 + 32 * bi + 32, :], kvo[ko][0:32, 0:256], r=[("kvo", ko)], w=())
                    sp_dma(o_v[l, 512 + 32 * bi:512 + 32 * bi + 32, :], kvo[ko][0:32, 256:512], r=[("kvo", ko)], w=())
                    for j in range(2):
                        b = bt[j]
                        for s_ in range(4):
                            P.add("pe", lambda e, b=b, s_=s_, j=j, stk=stk: e.transpose(ps[:, b, 128 * s_:128 * s_ + 128], stk[:, 256 * s_ + 128 * j:256 * s_ + 128 * j + 128], ident_f[:]),
                                  r=["stk", "ident_f"], w=[pk(b)])
                        act(kcT[:, j, :], ps[:, b, :], AF.Copy, r=[pk(b)], w=["kcT"])
                    P.add("dve", lambda e, d=Vc.rearrange("p s h c -> p (s h) c")[:, :, 0:64], s_=stv.rearrange("p (a c) -> p a c", c=64): e.tensor_copy(d, s_),
                          r=["stv"], w=["Vc"])

                def t2_(bi=bi, b2=b2, bo=bo, ro=ro, kcT=kcT, Ks=Ks, Qs=Qs, Vc=Vc, Vs=Vs, PTa=PTa, bT=bT, rec=rec, ygrp=ygrp):
                    S = ps[:, b2:b2 + 2, :].rearrange("p a b -> p (a b)")[:, 0:640].rearrange("p (h r c) -> p h r c", h=4, r=5)
                    skeys = [pk(b2), pk(b2 + 1)]
                    for hd in range(4):
                        j, pb = hd // 2, 64 * (hd % 2)
                        qv = Qs[pb:pb + 64, j, 32 * bi:32 * bi + 32]
                        for r_ in range(4):
                            mm_group(S[:, hd, r_, :], [(kcT[pb:pb + 64, j, r_ * 128:r_ * 128 + 128], qv)], r=["kcT", "Qs"], w=skeys)
                        mm_group(S[0:32, hd, 4, :], [(Ks[pb:pb + 64, j, 32 * bi:32 * bi + 32], qv)], r=["Ks", "Qs"], w=skeys)
                    dve_tt(S[:, :, 3, :], S[:, :, 3, :], bT[:, :, 0, 0:32], ALU.add, r=skeys + ["bT"], w=skeys)
                    dve_tt(S[0:32, :, 4, :], S[0:32, :, 4, :], bT[0:32, :, 1, 0:32], ALU.add, r=skeys + ["bT"], w=skeys)
                    Sx = ps[:, b2:b2 + 2, :].rearrange("p a b -> p (a b)")[:, 0:640].rearrange("p (h x) -> p h x", h=4)
                    Px = PTa.rearrange("p h r c -> p h (r c)")
                    act(Px[:, :, 0:128], Sx[:, :, 0:128], AF.Exp, r=skeys, w=["PTa"])
                    act(Px[0:32, :, 128:160], Sx[0:32, :, 128:160], AF.Exp, r=skeys, w=["PTa"])
                    for hd in range(4):
                        O = ps[:, bo, hd * 128:hd * 128 + 32]
                        pairs = [(Vc[:, r_, hd, :], PTa[:, hd, r_, :]) for r_ in range(4)] + [(Vs[0:32, hd, :], PTa[0:32, hd, 4, :])]
                        mm_group(O, pairs, r=["Vc", "Vs", "PTa"], w=[pk(bo)])
                    o4 = ps[:, bo, :].rearrange("p (h c) -> p h c", h=4)
                    r4 = rec[ro].rearrange("p (h c) -> p h c", h=4)
                    dve_recip(r4[64:128, :, 0:32], o4[64:128, :, 0:32], r=[pk(bo)], w=[("rec", ro)])
                    for par in range(2):
                        for j in range(2):
                            hd = 2 * j + par
                            dve_tt(ygrp[0][64 * par:64 * par + 64, j, 32 * bi:32 * bi + 32], o4[0:64, hd, 0:32], r4[64:128, hd, 0:32], ALU.mult,
                                   r=[pk(bo), ("rec", ro)], w=[("ygrp", 0)])
                pipe.item([t0_, t1_, t2_])
            pipe.flush()
            group_norm(l, 2, ygrp[kk], kk, ti, t0, n, sq2, sd, rstd)
            pst["single_banks"] = list(range(8))

            P.mark("M L%d" % l)
            P.barrier()
            areset()
            s_m = wload(w_in[l, :, 2048:2304])
            s_mk = wload(w_mkv[l, :, 0:256])
            s_mv = wload(w_mkv[l, :, 256:512])
            wo_cm = wout_load(l, 1)
            Q = [split3(ab(1024), 2), split3(ab(1024), 2)]
            mkT = split3(ab(512), 2)
            mva = ab(2 * 4 * 128).rearrange("p (s h c) -> p s h c", s=2, h=4)
            PTs = [split3(ab(1024), 2), split3(ab(1024), 2)]
            rec = [af(512), af(512)]
            ygrp = [split3(af(1024), 2), split3(af(1024), 2)]
            mo = [af(512), af(512)]
            sq2 = split3(ab(1024), 2)
            sd = af(512)
            rstd = af(512)
            stk = af(512)
            stv = af(512)
            mkTs = split3(ab(512), 2)
            mvs = ab(2 * 4 * 128).rearrange("p (s h c) -> p s h c", s=2, h=4)
            Qs = split3(ab(256), 2)
            PTa = ab(64)
            P.add("dve", lambda e, d=mva[:, :, :, 64:128]: e.memset(d, 1.0), w=["mva"])
            P.add("dve", lambda e, d=mvs[:, :, :, 64:128]: e.memset(d, 1.0), w=["mvs"])
            pst["single_banks"] = [4, 5, 6, 7]
            ocnt = 0
            for blk in range(2):
                b = ps1()
                mm_group(ps[:, b, 0:256], [(memT[:, kc, blk * 128:blk * 128 + 128], wslot(s_mk)[:, kc, :]) for kc in range(8)], r=[("ws", s_mk), "memT"], w=[pk(b)])
                mm_group(ps[:, b, 256:512], [(memT[:, kc, blk * 128:blk * 128 + 128], wslot(s_mv)[:, kc, :]) for kc in range(8)], r=[("ws", s_mv), "memT"], w=[pk(b)])
                act(mva[:, blk, :, 0:64], split3(ps[:, b, 256:512], 4), AF.Copy, r=[pk(b)], w=["mva"])
                ko = ocnt % 2
                ocnt += 1
                act(mo[ko][:, :], ps[:, b, :], AF.Copy, r=[pk(b)], w=[("mo", ko)])
                sp_dma(o_mk[l, blk * 128:blk * 128 + 128, :], mo[ko][:, 0:256], r=[("mo", ko)], w=())
                sp_dma(o_mv[l, blk * 128:blk * 128 + 128, :], mo[ko][:, 256:512], r=[("mo", ko)], w=())
            for j in range(2):
                b = ps1()
                mm_group(ps[:, b, 0:256], [(wslot(s_mk)[:, kc, 128 * j:128 * j + 128], memT[:, kc, :]) for kc in range(8)], r=[("ws", s_mk), "memT"], w=[pk(b)])
                act(mkT[:, j, :], ps[:, b, 0:256], AF.Copy, r=[pk(b)], w=["mkT"])
            pipe = Pipe()
            obank = 0
            for ti in range(4):
                t0, n = TILES[ti]
                kk = ti % 2
                for hd in range(4):
                    j, pb = hd // 2, 64 * (hd % 2)
                    b2 = ps2()
                    pu = hd % 2
                    bo = 4 + (obank % 3)
                    obank += 1
                    ro = hd % 2

                    def s0(hd=hd, j=j, pb=pb, b2=b2, kk=kk, ti=ti, t0=t0, n=n, Q=Q, mkT=mkT):
                        if hd == 0:
                            for jj in range(2):
                                h_mm(s_m, 128 * jj, ti, t0, n, 7)
                                act(Q[kk][:, jj, :], ps[:, 7, :], AF.Copy, r=[pk(7)], w=[("Q", kk)], scale=0.125)
                        for r_ in range(2):
                            mm_group(ps[:, b2 + r_, :], [(mkT[pb:pb + 64, j, r_ * 128:r_ * 128 + 128], Q[kk][pb:pb + 64, j, :])],
                                     r=["mkT", ("Q", kk)], w=[pk(b2 + r_)])

                    def s1(b2=b2, pu=pu, PTs=PTs):
                        act(PTs[pu][:, :, :], ps[:, b2:b2 + 2, :], AF.Exp, r=[pk(b2), pk(b2 + 1)], w=[("PTs", pu)])

                    def s2(bo=bo, pu=pu, hd=hd, mva=mva, PTs=PTs):
                        mm_group(ps[:, bo, :], [(mva[:, r_, hd, :], PTs[pu][:, r_, :]) for r_ in range(2)], r=["mva", ("PTs", pu)], w=[pk(bo)])

                    def s3(bo=bo, ro=ro, rec=rec):
                        dve_recip(rec[ro][64:128, :], ps[64:128, bo, :], r=[pk(bo)], w=[("rec", ro)])

                    def s4(bo=bo, ro=ro, rec=rec, kk=kk, j=j, pb=pb, ygrp=ygrp, hd=hd, n=n, sq2=sq2):
                        dve_tt(ygrp[kk][pb:pb + 64, j, :], ps[0:64, bo, :], rec[ro][64:128, :], ALU.mult, r=[pk(bo), ("rec", ro)], w=[("ygrp", kk)])
                        if hd == 3:
                            gn_square(ygrp[kk], kk, n, sq2)
                    stages = [s0, s1, s2, s3, s4]
                    if hd == 3:
                        def s5(l=l, kk=kk, ti=ti, t0=t0, n=n, ygrp=ygrp, sq2=sq2, sd=sd, rstd=rstd):
                            gn_rest(l, 3, ygrp[kk], kk, ti, t0, n, sq2, sd, rstd, 7)
                        stages.append(s5)
                    pipe.item(stages)
            pipe.flush()
            P.mark("Msample L%d" % l)
            ti = 4
            t0, n = TILES[4]
            kk = 0
            for j in range(2):
                b = ps1()
                h_mm(s_m, 128 * j, ti, t0, n, b)
                act(Qs[:, j, :], ps[:, b, 0:128], AF.Copy, r=[pk(b)], w=["Qs"], scale=0.125)
            for bi in range(4):
                sp_dma(split3(stk, 2), cmk[l, bi].rearrange("(s p) c -> p s c", p=128), r=(), w=["stk"])
                sp_dma(split3(stv, 2), cmv[l, bi].rearrange("(s p) c -> p s c", p=128), r=(), w=["stv"])
                b = ps1()
                for j in range(2):
                    for s_ in range(2):
                        P.add("pe", lambda e, b=b, s_=s_, j=j, stk=stk: e.transpose(ps[:, b, 256 * j + 128 * s_:256 * j + 128 * s_ + 128],
                                                                            stk[:, 256 * s_ + 128 * j:256 * s_ + 128 * j + 128], ident_f[:]),
                              r=["stk", "ident_f"], w=[pk(b)])
                act(mkTs[:, :, :], split3(ps[:, b, :], 2), AF.Copy, r=[pk(b)], w=["mkTs"])
                P.add("dve", lambda e, d=mvs.rearrange("p s h c -> p (s h) c")[:, :, 0:64], s_=stv.rearrange("p (a c) -> p a c", c=64): e.tensor_copy(d, s_),
                      r=["stv"], w=["mvs"])
                bo = ps1()
                ro = bi % 2
                for hd in range(4):
                    j, pb = hd // 2, 64 * (hd % 2)
                    b2 = ps2()
                    S = ps[:, b2, 0:64].rearrange("p (r c) -> p r c", r=2)
                    qv = Qs[pb:pb + 64, j, 32 * bi:32 * bi + 32]
                    for r_ in range(2):
                        mm_group(S[:, r_, :], [(mkTs[pb:pb + 64, j, r_ * 128:r_ * 128 + 128], qv)], r=["mkTs", "Qs"], w=[pk(b2)])
                    act(split3(PTa, 2), S, AF.Exp, r=[pk(b2)], w=["PTa"])
                    O = ps[:, bo, hd * 128:hd * 128 + 32]
                    mm_group(O, [(mvs[:, r_, hd, :], PTa[:, 32 * r_:32 * r_ + 32]) for r_ in range(2)], r=["mvs", "PTa"], w=[pk(bo)])
                o4 = ps[:, bo, :].rearrange("p (h c) -> p h c", h=4)
                r4 = rec[ro].rearrange("p (h c) -> p h c", h=4)
                dve_recip(r4[64:128, :, 0:32], o4[64:128, :, 0:32], r=[pk(bo)], w=[("rec", ro)])
                for par in range(2):
                    for j in range(2):
                        hd = 2 * j + par
                        dve_tt(ygrp[kk][64 * par:64 * par + 64, j, 32 * bi:32 * bi + 32], o4[0:64, hd, 0:32], r4[64:128, hd, 0:32], ALU.mult,
                               r=[pk(bo), ("rec", ro)], w=[("ygrp", kk)])
                wout_tile(l, 1, wo_cm, bi)
            group_norm(l, 3, ygrp[kk], kk, ti, t0, n, sq2, sd, rstd)
            pst["single_banks"] = list(range(8))

            P.mark("woutCM L%d" % l)
            wout_tile(l, 1, wo_cm, 4)

            P.mark("FFN L%d" % l)
            do_norm(l, O_N2)
            P.barrier()
            areset()
            U = [af(2 * 514).rearrange("p (a b) -> p a b", a=2) for _ in range(3)]
            fstg = [af(512), af(512)]
            tv = [af(512) for _ in range(4)]
            tg = [af(512) for _ in range(4)]
            sgf = [af(512), af(512)]
            ucnt = 0
            vcnt = 0
            fq = 0
            fpipe = Pipe()
            for (c_start, c_n) in FPARTS:
                npair = c_n // 2
                pslots = []

                def ldp(pi, c_start=c_start):
                    cc = c_start // 2 + pi
                    pslots.append((wload(w_up[l, :, 256 * cc:256 * cc + 256]), wload(w_up[l, :, 2816 + 256 * cc:2816 + 256 * cc + 256])))
                ldp(0)
                dsl = []

                def ldd(dp, c_start=c_start, c_n=c_n, dsl=dsl):
                    dsl.append(wload(w_down[l, 128 * c_start:128 * (c_start + c_n), 256 * dp:256 * dp + 256], nk=c_n))
                for pi in range(npair):
                    if pi + 1 < npair:
                        ldp(pi + 1)
                    if pi == 0:
                        for dp in range(3):
                            ldd(dp)
                    if pi == 1:
                        ldd(3)
                    sv_, sg_ = pslots[pi]
                    for sub in range(2):
                        c = c_start + 2 * pi + sub
                        cl = c - c_start
                        for ti, (t0, n) in enumerate(TILES):
                            ku = ucnt % 3
                            kprev = (ucnt - 1) % 3
                            ucnt += 1
                            u = vcnt % 4
                            vcnt += 1
                            fpipe.item(ffn_item(l, c, cl, sub, sv_, sg_, ti, t0, n, ku, kprev, u, U, tv, tg, sgf))
                if npair == 1:
                    ldd(3)
                fpipe.flush()
                for ti, (t0, n) in enumerate(TILES):
                    for dc in range(8):
                        ws_ = wslot(dsl[dc // 2])
                        dsub = dc % 2
                        b = ps1()
                        mm_group(ps[:, b, 0:n], [(ws_[:, kc, 128 * dsub:128 * dsub + 128], yg[:, kc, t0:t0 + n]) for kc in range(c_n)],
                                 r=[("ws", dsl[dc // 2]), ("yg", ti)], w=[pk(b)], arena=False)
                        dve_tt(x[:, dc, t0:t0 + n], x[:, dc, t0:t0 + n], ps[:, b, 0:n], ALU.add, r=[pk(b), xk(dc, ti)], w=[xk(dc, ti)], arena=False)
                ocf2 = o_cf[l].rearrange("s r c -> (s r) c")
                for half in range(2):
                    b = ps1()
                    for ci in range(c_n):
                        cc_ = 22 * half + c_start + ci
                        P.add("pe", lambda e, b=b, ci=ci, src=stF[:, cc_, :, :].rearrange("p s r -> p (s r)"): e.transpose(ps[0:10, b, 128 * ci:128 * ci + 128], src, ident_f[:]),
                              r=[("stF", cc_), "ident_f"], w=[pk(b)])
                    k = fq % 2
                    fq += 1
                    act(fstg[k][0:10, 0:128 * c_n], ps[0:10, b, 0:128 * c_n], AF.Copy, r=[pk(b)], w=[("sostg", k)])
                    sp_dma(ocf2[:, 2816 * half + 128 * c_start:2816 * half + 128 * (c_start + c_n)], fstg[k][0:10, 0:128 * c_n], r=[("sostg", k)], w=())

        P.mark("epilogue")
        P.barrier()
        areset()
        fg = af(1024)
        ost = [af(1024), af(1024)]
        xsq = af(512)
        ssum = [af(2), af(2)]
        sdv = [af(1), af(1)]
        rsv = [af(1), af(1)]
        sp_dma(fg, fgbc, r=(), w=["fg"])
        pipe = Pipe()
        for blk in range(17):
            ti = min(blk // 4, 4)
            k = blk % 2
            banks = [ps1(), ps1()]

            def s0(blk=blk, ti=ti, k=k, banks=banks):
                P.add("dve", lambda e, d=ssum[k]: e.memset(d, 0.0), w=[("ssum", k)])
                for half in range(2):
                    b = banks[half]
                    for ci in range(4):
                        c = half * 4 + ci
                        P.add("pe", lambda e, b=b, c=c, ci=ci, blk=blk: e.transpose(ps[:, b, 128 * ci:128 * ci + 128], x[:, c, blk * 128:blk * 128 + 128], ident_f[:]),
                              r=[xk(c, ti), "ident_f"], w=[pk(b)])
                    P.add("act", lambda e, b=b, half=half, k=k: e.activation(xsq[:, :], ps[:, b, :], AF.Square, accum_out=ssum[k][:, half:half + 1]),
                          r=[pk(b)], w=["xsq", ("ssum", k)])

            def s1(k=k):
                dve_tt(ssum[k][:, 0:1], ssum[k][:, 0:1], ssum[k][:, 1:2], ALU.add, r=[("ssum", k)], w=[("ssum", k)])
                act(sdv[k], ssum[k][:, 0:1], AF.Ln, r=[("ssum", k), "epsr"], w=[("sdv", k)], bias=epsr[:, 0:1], scale=1.0 / 1024)
                act(rsv[k], sdv[k], AF.Exp, r=[("sdv", k)], w=[("rsv", k)], scale=-0.5)

            def s2(blk=blk, k=k, banks=banks):
                for half in range(2):
                    b = banks[half]
                    dve_stt(ost[k][:, 512 * half:512 * half + 512], ps[:, b, :], rsv[k][:, 0:1], fg[:, 512 * half:512 * half + 512], ALU.mult, ALU.mult,
                            r=[pk(b), ("rsv", k), "fg"], w=[("ost", k)])
                sp_dma(y_o[blk * 128:blk * 128 + 128, :], ost[k], r=[("ost", k)], w=())
            pipe.item([s0, s1, s2])
        pipe.flush()

        P.finalize()
        _CACHE['prog'] = P
        with nc.Block() as block:
            P.emit(nc, block, sems, dsems)
    return nc


_CACHE = {}


def _prep_inputs(inp):
    f = lambda a: np.ascontiguousarray(np.asarray(a, dtype=np.float32))
    x_prompt, x_sample = f(inp["x_prompt"]), f(inp["x_sample"])
    rows = []
    for l in range(L):
        rows.append(f(inp["norm1_g"])[l].reshape(8, 128))
        rows.append(f(inp["norm2_g"])[l].reshape(8, 128))
        rows.append(f(inp["grp_norm_g"])[l].reshape(8, 128))
        caw = f(inp["conv_a_w"])[l].reshape(3, 2, 128).transpose(1, 0, 2).reshape(6, 128)
        rows.append(caw)
        cbw = f(inp["conv_b_w"])[l].reshape(31, 2, 128).transpose(1, 0, 2).reshape(62, 128)
        rows.append(cbw)
        rows.append(f(inp["conv_b_bias"])[l].reshape(2, 128))
        rows.append(f(inp["ln_b_g"])[l].reshape(2, 128))
        rows.append(f(inp["ln_b_b"])[l].reshape(2, 128))
        fcw = f(inp["ffn_conv_w"])[l].reshape(3, 44, 128).transpose(1, 0, 2).reshape(132, 128)
        rows.append(fcw)
        rows.append(f(inp["ffn_conv_b"])[l].reshape(44, 128))
    params = np.concatenate(rows, axis=0)
    assert params.shape[0] == 2 * PL
    params = np.concatenate([params, np.zeros((NPR - params.shape[0], 128), np.float32)], axis=0)
    rb = f(inp["rel_bias"])
    kk = np.arange(128)[:, None, None]
    r = np.arange(2)[None, :, None] + 3
    q = np.arange(128)[None, None, :]
    idx = np.clip(q + 128 * (4 - r) - kk, -128, 128) + 128
    biasd = np.ascontiguousarray(rb[:, :, idx].transpose(0, 2, 1, 3, 4))
    cvecd = np.ascontiguousarray(np.broadcast_to(rb[:, None, :, 256], (L, 128, 4)))
    fgbc = np.ascontiguousarray(np.broadcast_to(f(inp["final_g"])[None, :], (128, 1024)))
    shared = dict(w_in=f(inp["w_in"]), w_mkv=f(inp["w_mem_kv"]), w_out=f(inp["w_out"]), w_up=f(inp["w_up"]), w_down=f(inp["w_down"]),
                  params=params, biasd=biasd, cvecd=cvecd, fgbc=fgbc)
    cca, ccb, ccf = f(inp["cache_conv_a"]), f(inp["cache_conv_b"]), f(inp["cache_ffn_conv"])
    cak, cav = f(inp["cache_attn_k"]), f(inp["cache_attn_v"])
    cmk, cmv = f(inp["cache_mem_k"]), f(inp["cache_mem_v"])
    memp = f(inp["mem_prompt"])
    maps = []
    for i in range(8):
        sl = slice(4 * i, 4 * i + 4)
        m = dict(shared)
        m["xin"] = np.ascontiguousarray(np.concatenate([x_prompt[i], x_sample[sl].reshape(128, 1024)], axis=0))
        m["memp"] = np.ascontiguousarray(memp[i])
        m["cca"] = np.ascontiguousarray(cca[:, sl].reshape(L, 8, 256))
        m["ccb"] = np.ascontiguousarray(ccb[:, sl].reshape(L, 120, 256))
        m["ccf"] = np.ascontiguousarray(ccf[:, sl].reshape(L, 8, 5632))
        m["cak"] = np.ascontiguousarray(cak[:, sl].reshape(L, 4, 512, 256))
        m["cav"] = np.ascontiguousarray(cav[:, sl].reshape(L, 4, 512, 256))
        m["cmk"] = np.ascontiguousarray(cmk[:, sl].reshape(L, 4, 256, 256))
        m["cmv"] = np.ascontiguousarray(cmv[:, sl].reshape(L, 4, 256, 256))
        maps.append(m)
    return maps


def kernel(**inputs):
    if "nc" not in _CACHE:
        _CACHE["nc"] = build_program()
    nc = _CACHE["nc"]
    maps = _prep_inputs(inputs)
    res = run_bass_kernel_spmd(nc, maps, core_ids=list(range(8)))
    R = res.results
    y_prompt = np.stack([R[i]["y_o"][0:2048] for i in range(8)], 0)
    y_sample = np.concatenate([R[i]["y_o"][2048:].reshape(4, 32, 1024) for i in range(8)], 0)

    def gather(name, shape_tail):
        pr = np.stack([R[i][name][:, 0] for i in range(8)], 1)
        sa = np.concatenate([R[i][name][:, 1:5] for i in range(8)], 1)
        return pr, sa
    p_ca, s_ca = gather("o_ca", None)
    p_cb, s_cb = gather("o_cb", None)
    p_cf, s_cf = gather("o_cf", None)
    p_k = np.stack([R[i]["o_k"][:, 0:512].reshape(L, 512, 4, 64) for i in range(8)], 1)
    p_v = np.stack([R[i]["o_v"][:, 0:512].reshape(L, 512, 4, 64) for i in range(8)], 1)
    s_k = np.concatenate([R[i]["o_k"][:, 512:].reshape(L, 4, 32, 4, 64) for i in range(8)], 1)
    s_v = np.concatenate([R[i]["o_v"][:, 512:].reshape(L, 4, 32, 4, 64) for i in range(8)], 1)
    p_mk = np.stack([R[i]["o_mk"].reshape(L, 256, 4, 64) for i in range(8)], 1)
    p_mv = np.stack([R[i]["o_mv"].reshape(L, 256, 4, 64) for i in range(8)], 1)
    outs = (y_prompt, y_sample, p_ca, p_cb, p_cf, p_k, p_v, p_mk, p_mv, s_ca, s_cb, s_cf, s_k, s_v)
    return tuple(np.ascontiguousarray(o, dtype=np.float32) for o in outs)
```

```python
import contextlib
import numpy as np
import concourse.bass as bass
import concourse.mybir as mybir
from concourse.bass_utils import run_bass_kernel_spmd

F32 = mybir.dt.float32
BF16 = mybir.dt.bfloat16
ALU = mybir.AluOpType
AF = mybir.ActivationFunctionType

L = 2
T = 2176
TP = 2048
TILES = [(0, 512), (512, 512), (1024, 512), (1536, 512), (2048, 128)]
NPR = 640
PL = 274
O_N1, O_N2, O_GN, O_CAW, O_CBW, O_CBB, O_LNG, O_LNB, O_FCW, O_FCB = 0, 8, 16, 24, 30, 92, 94, 96, 98, 230
NSLOT = 7
NDSLOT = 8
DMA_INFLIGHT = {"sp": 3, "pool": 8}
ARENA_W = 11950
FPARTS = [(0, 4), (4, 4), (8, 4), (12, 4), (16, 4), (20, 2)]


class Prog:
    ENGS = ("sp", "pool", "pe", "dve", "act")

    def __init__(self):
        self.ops = []
        self.last_w = {}
        self.readers = {}
        self.pending_barrier = {e: set() for e in self.ENGS}
        self.last_op = {e: None for e in self.ENGS}
        self.dma_ops = {"sp": [], "pool": []}
        self.marks = []

    def add(self, eng, fn, r=(), w=(), dma=False, arena=True):
        idx = len(self.ops)
        raw = set()
        oth = set()
        for k in r:
            if k in self.last_w:
                raw.add(self.last_w[k])
        for k in w:
            if k in self.last_w:
                oth.add(self.last_w[k])
            oth.update(self.readers.get(k, ()))
        if arena:
            hard = self.pending_barrier[eng]
            self.pending_barrier[eng] = set()
        else:
            hard = set()
        for k in r:
            self.readers.setdefault(k, []).append(idx)
        for k in w:
            self.last_w[k] = idx
            self.readers[k] = []
        deps = set()
        for d in raw | oth | hard:
            o = self.ops[d]
            if (not o["dma"]) and (not dma) and o["eng"] == eng and d not in hard:
                if eng == "pe":
                    continue
                if d not in raw:
                    continue
            deps.add(d)
        op = dict(eng=eng, fn=fn, dma=dma, deps=deps, signal=False, seq=None, dslot=None, dtarget=None)
        if dma:
            q = self.dma_ops[eng]
            j = len(q)
            op["dslot"] = j % NDSLOT
            op["dtarget"] = 16 * (j // NDSLOT + 1)
            cap = DMA_INFLIGHT[eng]
            if j >= cap:
                op["deps"].add(q[j - cap])
            q.append(idx)
        self.ops.append(op)
        self.last_op[eng] = idx
        return idx

    def mark(self, name):
        self.marks.append((name, sum(1 for o in self.ops if o["eng"] == "act")))

    def barrier(self):
        s = set()
        for e in self.ENGS:
            if self.last_op[e] is not None:
                s.add(self.last_op[e])
        for q in self.dma_ops.values():
            for i in q[-NDSLOT:]:
                s.add(i)
        for e in self.ENGS:
            self.pending_barrier[e] |= s

    def finalize(self):
        for op in self.ops:
            for d in op["deps"]:
                self.ops[d]["signal"] = True
        cnt = {e: 0 for e in self.ENGS}
        for op in self.ops:
            if not op["dma"] and op["signal"]:
                cnt[op["eng"]] += 1
                op["seq"] = cnt[op["eng"]]

    def emit(self, nc, block, sems, dsems):
        attr = {"sp": "sync", "pool": "gpsimd", "pe": "tensor", "dve": "vector", "act": "scalar"}
        ops = self.ops
        final_waits = []
        for qn, q in self.dma_ops.items():
            for i in q[-NDSLOT:]:
                o = ops[i]
                final_waits.append((("d", qn, o["dslot"]), o["dtarget"]))

        def mk(eng):
            def body(e):
                known = {}

                def wait(key, val):
                    if known.get(key, 0) >= val:
                        return
                    known[key] = val
                    if key[0] == "d":
                        e.wait_ge(dsems[key[1]][key[2]], val)
                    else:
                        e.wait_ge(sems[key[1]], val)

                for op in ops:
                    if op["eng"] != eng:
                        continue
                    for d in sorted(op["deps"]):
                        o = ops[d]
                        if o["dma"]:
                            wait(("d", o["eng"], o["dslot"]), o["dtarget"])
                        else:
                            wait(("c", o["eng"]), o["seq"])
                    ins = op["fn"](e)
                    if op["dma"]:
                        ins.then_inc(dsems[eng][op["dslot"]], 16)
                    elif op["signal"]:
                        ins.then_inc(sems[eng], 1)
                if eng == "sp":
                    for key, val in final_waits:
                        wait(key, val)
            return body

        for eng in self.ENGS:
            getattr(block, attr[eng])(mk(eng))


def build_program():
    nc = bass.Bass("TRN2", target_bir_lowering=False)

    def din(name, shape):
        return nc.dram_tensor(name, list(shape), F32, kind="ExternalInput").ap()

    def dout(name, shape):
        return nc.dram_tensor(name, list(shape), F32, kind="ExternalOutput").ap()

    xin = din("xin", [T, 1024])
    memp = din("memp", [256, 1024])
    cca = din("cca", [L, 8, 256])
    ccb = din("ccb", [L, 120, 256])
    ccf = din("ccf", [L, 8, 5632])
    cak = din("cak", [L, 4, 512, 256])
    cav = din("cav", [L, 4, 512, 256])
    cmk = din("cmk", [L, 4, 256, 256])
    cmv = din("cmv", [L, 4, 256, 256])
    w_in = din("w_in", [L, 1024, 2304])
    w_mkv = din("w_mkv", [L, 1024, 512])
    w_out = din("w_out", [L, 1024, 1024])
    w_up = din("w_up", [L, 1024, 5632])
    w_down = din("w_down", [L, 2816, 1024])
    params = din("params", [NPR, 128])
    biasd = din("biasd", [L, 128, 4, 2, 128])
    cvecd = din("cvecd", [L, 128, 4])
    fgbc = din("fgbc", [128, 1024])

    y_o = dout("y_o", [T, 1024])
    o_ca = dout("o_ca", [L, 5, 2, 256])
    o_cb = dout("o_cb", [L, 5, 30, 256])
    o_cf = dout("o_cf", [L, 5, 2, 5632])
    o_k = dout("o_k", [L, 640, 256])
    o_v = dout("o_v", [L, 640, 256])
    o_mk = dout("o_mk", [L, 256, 256])
    o_mv = dout("o_mv", [L, 256, 256])

    P = Prog()
    es = contextlib.ExitStack()
    with es:
        def sb(name, shape, dt):
            return es.enter_context(nc.sbuf_tensor(name, list(shape), dt))

        x = sb("x", [128, 8, T], F32)
        h = sb("h", [128, 8, T], BF16)
        yg = sb("yg", [128, 4, T], BF16)
        wsl = sb("wsl", [128, NSLOT, 8, 256], BF16)
        ident_f = sb("ident_f", [128, 128], F32)
        ident_b = sb("ident_b", [128, 128], BF16)
        ones_b = sb("ones_b", [128, 128], BF16)
        epsr = sb("epsr", [128, 1], F32)
        epsl = sb("epsl", [128, 1], F32)
        PT = sb("PT", [128, NPR], F32)
        memT = sb("memT", [128, 8, 256], BF16)
        haloA = sb("haloA", [128, L, 2, 8], F32)
        haloB = sb("haloB", [128, L, 2, 120], F32)
        haloF = sb("haloF", [128, L, 44, 8], F32)
        stF = sb("stF", [128, 44, 5, 2], F32)
        arena = sb("arena", [128, ARENA_W], F32)
        ps = es.enter_context(nc.psum_tensor("ps", [128, 8, 512], F32))

        sems = {e: es.enter_context(nc.semaphore("c_" + e)) for e in Prog.ENGS if e != "sp"}
        sems["sp"] = sems["pool"]
        dsems = {q: [es.enter_context(nc.semaphore("d_%s%d" % (q, i))) for i in range(NDSLOT)] for q in ("sp", "pool")}

        ar = {"off": 0}

        def areset():
            ar["off"] = 0

        def af(n):
            o = ar["off"]
            ar["off"] = o + n
            assert ar["off"] <= ARENA_W, ("arena overflow", ar["off"])
            return arena[:, o:o + n]

        def ab(n):
            nw = (n + 1) // 2
            o = ar["off"]
            ar["off"] = o + nw
            assert ar["off"] <= ARENA_W, ("arena overflow", ar["off"])
            return arena[:, o:o + nw].bitcast(BF16)[:, 0:n]

        pst = {"s": 0, "p": 0, "single_banks": list(range(8))}

        def ps1():
            b = pst["single_banks"][pst["s"] % len(pst["single_banks"])]
            pst["s"] += 1
            return b

        def ps2():
            b = (pst["p"] % 2) * 2
            pst["p"] += 1
            return b

        def pk(b):
            return ("ps", b)

        wst = {"n": 0}

        def wload(src_ap, nk=8):
            s = wst["n"] % NSLOT
            wst["n"] += 1
            dst = wsl[:, s, 0:nk, :]
            P.add("pool", lambda e, dst=dst, src=src_ap: e.dma_start(out=dst, in_=src.rearrange("(kc p) n -> p kc n", p=128)),
                  r=(), w=[("ws", s)], dma=True, arena=False)
            return s

        def wslot(s):
            return wsl[:, s, :, :]

        def walloc():
            s = wst["n"] % NSLOT
            wst["n"] += 1
            return s

        def mm_group(out_ap, pairs, r, w, skip=False, arena=True):
            n = len(pairs)

            def fn(e, out_ap=out_ap, pairs=pairs):
                ins = None
                for i, (a, b) in enumerate(pairs):
                    ins = e.matmul(out_ap, a, b, start=(i == 0), stop=(i == n - 1))
                return ins
            P.add("pe", fn, r=r, w=w, arena=arena)

        def act(out, in_, func, r, w, bias=None, scale=None):
            kw = {}
            if bias is not None:
                kw["bias"] = bias
            if scale is not None:
                kw["scale"] = scale
            P.add("act", lambda e: e.activation(out, in_, func, **kw), r=r, w=w)

        def dve_tt(out, a, b, op, r, w, arena=True):
            P.add("dve", lambda e: e.tensor_tensor(out, a, b, op), r=r, w=w, arena=arena)

        def dve_stt(out, in0, scalar, in1, op0, op1, r, w):
            P.add("dve", lambda e: e.scalar_tensor_tensor(out=out, in0=in0, scalar=scalar, in1=in1, op0=op0, op1=op1), r=r, w=w)

        def dve_ts(out, in0, s1, s2, op0, op1, r, w):
            if s2 is None:
                P.add("dve", lambda e: e.tensor_scalar(out, in0, s1, None, op0), r=r, w=w)
            else:
                P.add("dve", lambda e: e.tensor_scalar(out, in0, s1, s2, op0, op1), r=r, w=w)

        def dve_recip(out, in_, r, w):
            act(out, in_, AF.Ln, r=r, w=w)
            act(out, out, AF.Exp, r=list(w), w=w, scale=-1.0)

        def sp_dma(out, in_, r, w):
            P.add("sp", lambda e: e.dma_start(out=out, in_=in_), r=r, w=w, dma=True)

        def pcol(l, off, n=1):
            c = PL * l + off
            return PT[:, c:c + n]

        def xk(c, ti):
            return ("x", c, ti)

        def xkeys(ti):
            return [("x", c, ti) for c in range(8)]

        def split3(ap, a):
            return ap.rearrange("p (a b) -> p a b", a=a)

        areset()
        NSTG = 5
        ar["off"] = ARENA_W - NSTG * 1024
        stg = [af(1024) for _ in range(NSTG)]
        P.add("pool", lambda e: e.memset(ident_f[:], 0.0), w=["ident_f"])
        P.add("pool", lambda e: e.affine_select(out=ident_f[:], in_=ident_f[:], pattern=[[-1, 128]], compare_op=ALU.not_equal,
                                                 fill=1.0, base=0, channel_multiplier=1), r=["ident_f"], w=["ident_f"])
        P.add("dve", lambda e: e.tensor_copy(ident_b[:], ident_f[:]), r=["ident_f"], w=["ident_b"])
        P.add("dve", lambda e: e.memset(ones_b[:], 1.0), w=["ones_b"])
        P.add("dve", lambda e: e.memset(epsr[:], 1e-6), w=["epsr"])
        P.add("dve", lambda e: e.memset(epsl[:], 1e-5), w=["epsl"])

        stn = {"n": 0}

        def stage():
            k = stn["n"] % NSTG
            stn["n"] += 1
            return k

        for blk in range(NPR // 128):
            k = stage()
            sp_dma(stg[k][:, 0:128], params[blk * 128:(blk + 1) * 128, :], r=(), w=[("stg", k)])
            b = ps1()
            P.add("pe", lambda e, b=b, k=k: e.transpose(ps[:, b, 0:128], stg[k][:, 0:128], ident_f[:]),
                  r=[("stg", k), "ident_f"], w=[pk(b)])
            act(PT[:, blk * 128:(blk + 1) * 128], ps[:, b, 0:128], AF.Copy, r=[pk(b)], w=["PT"])
        for l in range(L):
            k = stage()
            sp_dma(stg[k][0:8, 0:256], cca[l], r=(), w=[("stg", k)])
            b = ps1()
            for j in range(2):
                P.add("pe", lambda e, b=b, k=k, j=j: e.transpose(ps[:, b, 8 * j:8 * j + 8], stg[k][0:8, 128 * j:128 * j + 128], ident_f[0:8, 0:8]),
                      r=[("stg", k), "ident_f"], w=[pk(b)])
            act(haloA[:, l, :, :], split3(ps[:, b, 0:16], 2), AF.Copy, r=[pk(b)], w=["haloA"])
            k = stage()
            sp_dma(stg[k][0:120, 0:256], ccb[l], r=(), w=[("stg", k)])
            b = ps1()
            for j in range(2):
                P.add("pe", lambda e, b=b, k=k, j=j: e.transpose(ps[:, b, 120 * j:120 * j + 120], stg[k][0:120, 128 * j:128 * j + 128], ident_f[0:120, 0:120]),
                      r=[("stg", k), "ident_f"], w=[pk(b)])
            act(haloB[:, l, :, :], split3(ps[:, b, 0:240], 2), AF.Copy, r=[pk(b)], w=["haloB"])
            for c0 in range(0, 44, 8):
                ncn = min(8, 44 - c0)
                k = stage()
                sp_dma(stg[k][0:8, 0:128 * ncn], ccf[l, :, 128 * c0:128 * (c0 + ncn)], r=(), w=[("stg", k)])
                b = ps1()
                for ci in range(ncn):
                    P.add("pe", lambda e, b=b, k=k, ci=ci: e.transpose(ps[:, b, 8 * ci:8 * ci + 8], stg[k][0:8, 128 * ci:128 * ci + 128], ident_f[0:8, 0:8]),
                          r=[("stg", k), "ident_f"], w=[pk(b)])
                act(haloF[:, l, c0:c0 + ncn, :], split3(ps[:, b, 0:8 * ncn], ncn), AF.Copy, r=[pk(b)], w=["haloF"])
        for blk in range(2):
            k = stage()
            sp_dma(stg[k][:, :], memp[blk * 128:(blk + 1) * 128, :], r=(), w=[("stg", k)])
            for half in range(2):
                b = ps1()
                for ci in range(4):
                    c = half * 4 + ci
                    P.add("pe", lambda e, b=b, k=k, c=c, ci=ci: e.transpose(ps[:, b, 128 * ci:128 * ci + 128], stg[k][:, 128 * c:128 * c + 128], ident_f[:]),
                          r=[("stg", k), "ident_f"], w=[pk(b)])
                act(memT[:, half * 4:half * 4 + 4, blk * 128:(blk + 1) * 128], split3(ps[:, b, :], 4), AF.Copy, r=[pk(b)], w=["memT"])
        for blk in range(17):
            k = stage()
            ti = min(blk // 4, 4)
            sp_dma(stg[k][:, :], xin[blk * 128:(blk + 1) * 128, :], r=(), w=[("stg", k)])
            for half in range(2):
                b = ps1()
                for ci in range(4):
                    c = half * 4 + ci
                    P.add("pe", lambda e, b=b, k=k, c=c, ci=ci: e.transpose(ps[:, b, 128 * ci:128 * ci + 128], stg[k][:, 128 * c:128 * c + 128], ident_f[:]),
                          r=[("stg", k), "ident_f"], w=[pk(b)])
                wk = [xk(half * 4 + ci, ti) for ci in range(4)]
                if half == 0:
                    act(x[:, 0:4, blk * 128:(blk + 1) * 128], split3(ps[:, b, :], 4), AF.Copy, r=[pk(b)], w=wk)
                else:
                    P.add("dve", lambda e, b=b, blk=blk: e.tensor_copy(x[:, 4:8, blk * 128:(blk + 1) * 128], split3(ps[:, b, :], 4)), r=[pk(b)], w=wk)

        def rms_rstd(ss_bank, n, inv_n, eps_t, sd, rstd, keys_w):
            act(sd[:, 0:n], ps[:, ss_bank, 0:n], AF.Ln, r=[pk(ss_bank), "epsr", "epsl"], w=[keys_w + "_sd"], bias=eps_t[:, 0:1], scale=inv_n)
            act(rstd[:, 0:n], sd[:, 0:n], AF.Exp, r=[keys_w + "_sd"], w=[keys_w], scale=-0.5)

        def do_norm(l, goff):
            P.mark("norm L%d" % l)
            if not (l == 0 and goff == O_N1):
                P.barrier()
            areset()
            sqs = [split3(ab(8 * 512), 8), split3(ab(8 * 512), 8)]
            assert ar["off"] + 2048 <= ARENA_W - 5 * 1024
            sds = [af(512), af(512)]
            rss = [af(512), af(512)]
            pipe = Pipe()
            for ti, (t0, n) in enumerate(TILES):
                k = ti % 2
                b = ps1()

                def s0(ti=ti, t0=t0, n=n, k=k):
                    act(sqs[k][:, :, 0:n], x[:, :, t0:t0 + n], AF.Square, r=xkeys(ti), w=[("nsq", k)])

                def s1(n=n, k=k, b=b):
                    mm_group(ps[:, b, 0:n], [(ones_b[:], sqs[k][:, c, 0:n]) for c in range(8)], r=[("nsq", k), "ones_b"], w=[pk(b)])

                def s2(n=n, k=k, b=b):
                    rms_rstd(b, n, 1.0 / 1024, epsr, sds[k], rss[k], "nrs%d" % k)

                def s3(ti=ti, t0=t0, n=n, k=k):
                    for c in range(8):
                        dve_stt(h[:, c, t0:t0 + n], x[:, c, t0:t0 + n], pcol(l, goff + c), rss[k][:, 0:n], ALU.mult, ALU.mult,
                                r=[xk(c, ti), "nrs%d" % k, "PT"], w=[("h", ti)])
                pipe.item([s0, s1, s2, s3])
            pipe.flush()

        def gn_square(ygrp_k, kk, n, sq2):
            act(sq2[:, :, 0:n], ygrp_k[:, :, 0:n], AF.Square, r=[("ygrp", kk)], w=["gsq"])

        def gn_rest(l, gi, ygrp_k, kk, ti, t0, n, sq2, sd, rstd, b):
            mm_group(ps[:, b, 0:n], [(ones_b[:], sq2[:, j, 0:n]) for j in range(2)], r=["gsq", "ones_b"], w=[pk(b)])
            rms_rstd(b, n, 1.0 / 256, epsr, sd, rstd, "grs")
            for j in range(2):
                dve_stt(yg[:, (gi % 2) * 2 + j, t0:t0 + n], ygrp_k[:, j, 0:n], pcol(l, O_GN + 2 * gi + j), rstd[:, 0:n], ALU.mult, ALU.mult,
                        r=[("ygrp", kk), "grs", "PT"], w=[("yg", ti)])

        def group_norm(l, gi, ygrp_k, kk, ti, t0, n, sq2, sd, rstd):
            gn_square(ygrp_k, kk, n, sq2)
            gn_rest(l, gi, ygrp_k, kk, ti, t0, n, sq2, sd, rstd, ps1())

        def h_mm(slot, c0, ti, t0, n, b):
            ws_ = wslot(slot)
            mm_group(ps[:, b, 0:n], [(ws_[:, kc, c0:c0 + 128], h[:, kc, t0:t0 + n]) for kc in range(8)],
                     r=[("ws", slot), ("h", ti)], w=[pk(b)], arena=False)

        def state_out(l, dst, srcs, nr, stgs, keys_r):
            nch = len(srcs[0])
            cmax = stgs[0].shape[1] // 128
            q = 0
            for s in range(5):
                for c0 in range(0, nch, cmax):
                    cn = min(cmax, nch - c0)
                    b = ps1()
                    for ci in range(cn):
                        P.add("pe", lambda e, b=b, ci=ci, src=srcs[s][c0 + ci]: e.transpose(ps[0:nr, b, 128 * ci:128 * ci + 128], src, ident_f[:]),
                              r=keys_r + ["ident_f"], w=[pk(b)])
                    k = q % 2
                    q += 1
                    kname = ("sostg", k)
                    act(stgs[k][0:nr, 0:128 * cn], ps[0:nr, b, 0:128 * cn], AF.Copy, r=[pk(b)], w=[kname])
                    sp_dma(dst[l, s, :, 128 * c0:128 * (c0 + cn)], stgs[k][0:nr, 0:128 * cn], r=[kname], w=())

        def wout_load(l, pi):
            return [wload(w_out[l, 512 * pi:512 * pi + 512, 256 * dp:256 * dp + 256], nk=4) for dp in range(4)]

        def wout_tile(l, pi, slots, ti):
            t0, n = TILES[ti]
            bank_list = pst["single_banks"]
            for dc in range(8):
                ws_ = wslot(slots[dc // 2])
                dsub = dc % 2
                b = ps1()
                mm_group(ps[:, b, 0:n], [(ws_[:, kc, 128 * dsub:128 * dsub + 128], yg[:, kc, t0:t0 + n]) for kc in range(4)],
                         r=[("ws", slots[dc // 2]), ("yg", ti)], w=[pk(b)], arena=False)
                dve_tt(x[:, dc, t0:t0 + n], x[:, dc, t0:t0 + n], ps[:, b, 0:n], ALU.add, r=[pk(b), xk(dc, ti)], w=[xk(dc, ti)], arena=False)

        def wout_pair(l, pi):
            slots = wout_load(l, pi)
            for ti in range(5):
                wout_tile(l, pi, slots, ti)

        class Pipe:
            def __init__(self, first0=False):
                self.items = []
                self.first0 = first0

            def _advance(self):
                nI = len(self.items)
                for idx, it in enumerate(self.items):
                    k = nI - idx
                    if k < len(it):
                        it[k]()

            def item(self, stages):
                if self.first0:
                    stages[0]()
                    self._advance()
                else:
                    self._advance()
                    stages[0]()
                self.items.append(stages)
                self.items = self.items[-9:]

            def flush(self):
                for _ in range(9):
                    self._advance()
                    self.items.append([lambda: None])
                    self.items = self.items[-9:]
                self.items = []

        def ffn_item(l, c, cl, sub, sv_, sg_, ti, t0, n, ku, kprev, u, U, tv, tg, sgf):
            bv, bg = ps1(), ps1()
            Uk = U[ku]
            if ti < 4:
                dv, dgt = Uk[:, 0, 2:2 + n], Uk[:, 1, 2:2 + n]
                pv_, pg_ = ps[:, bv, 0:n], ps[:, bg, 0:n]
                shv = [Uk[:, 0, s:s + n] for s in range(3)]
                shg = [Uk[:, 1, s:s + n] for s in range(3)]
                tvo, tgo, sgo = tv[u][:, 0:n], tg[u][:, 0:n], sgf[u % 2][:, 0:n]
                go = yg[:, cl, t0:t0 + n]
                Us = None
            else:
                Us = Uk[:, :, 0:136].rearrange("p a (b c) -> p a b c", b=4)
                dv, dgt = Us[:, 0, :, 2:34], Us[:, 1, :, 2:34]
                pv_, pg_ = split3(ps[:, bv, 0:n], 4), split3(ps[:, bg, 0:n], 4)
                shv = [Us[:, 0, :, s:s + 32] for s in range(3)]
                shg = [Us[:, 1, :, s:s + 32] for s in range(3)]
                tvo, tgo, sgo = split3(tv[u][:, 0:n], 4), split3(tg[u][:, 0:n], 4), split3(sgf[u % 2][:, 0:n], 4)
                go = split3(yg[:, cl, t0:t0 + n], 4)

            def s1():
                h_mm(sv_, 128 * sub, ti, t0, n, bv)
                h_mm(sg_, 128 * sub, ti, t0, n, bg)
                if ti == 0:
                    P.add("dve", lambda e, d=Uk[:, :, 0:2]: e.memset(d, 0.0), w=[("U", ku)])
                elif ti < 4:
                    P.add("act", lambda e, d=Uk[:, :, 0:2], s_=U[kprev][:, :, 512:514]: e.activation(d, s_, AF.Copy),
                          r=[("U", kprev)], w=[("U", ku)])
                else:
                    P.add("act", lambda e, d=Us[:, 0, :, 0:2], s_=haloF[:, l, c, :].rearrange("p (b c) -> p b c", b=4): e.activation(d, s_, AF.Copy),
                          r=["haloF"], w=[("U", ku)])
                    P.add("act", lambda e, d=Us[:, 1, :, 0:2], s_=haloF[:, l, 22 + c, :].rearrange("p (b c) -> p b c", b=4): e.activation(d, s_, AF.Copy),
                          r=["haloF"], w=[("U", ku)])
                act(dv, pv_, AF.Copy, r=[pk(bv)], w=[("U", ku)])
                act(dgt, pg_, AF.Copy, r=[pk(bg)], w=[("U", ku)])
                act(tvo, pv_, AF.Identity, r=[pk(bv), "PT"], w=[("tv", u)], bias=pcol(l, O_FCB + c), scale=pcol(l, O_FCW + 3 * c + 2))
                act(tgo, pg_, AF.Identity, r=[pk(bg), "PT"], w=[("tg", u)], bias=pcol(l, O_FCB + 22 + c), scale=pcol(l, O_FCW + 3 * (22 + c) + 2))
                if ti == 3:
                    P.add("act", lambda e, d=stF[:, c, 0, :], s_=Uk[:, 0, 512:514]: e.activation(d, s_, AF.Copy), r=[("U", ku)], w=[("stF", c)])
                    P.add("act", lambda e, d=stF[:, 22 + c, 0, :], s_=Uk[:, 1, 512:514]: e.activation(d, s_, AF.Copy), r=[("U", ku)], w=[("stF", 22 + c)])
                if ti == 4:
                    P.add("act", lambda e, d=stF[:, c, 1:5, :], s_=Us[:, 0, :, 32:34]: e.activation(d, s_, AF.Copy), r=[("U", ku)], w=[("stF", c)])
                    P.add("act", lambda e, d=stF[:, 22 + c, 1:5, :], s_=Us[:, 1, :, 32:34]: e.activation(d, s_, AF.Copy), r=[("U", ku)], w=[("stF", 22 + c)])

            def s2():
                for (to, sh_, cc_) in ((tvo, shv, c), (tgo, shg, 22 + c)):
                    key = ("tv", u) if cc_ == c else ("tg", u)
                    dve_stt(to, sh_[1], pcol(l, O_FCW + 3 * cc_ + 1), to, ALU.mult, ALU.add, r=[("U", ku), key, "PT"], w=[key])
                    dve_stt(to, sh_[0], pcol(l, O_FCW + 3 * cc_ + 0), to, ALU.mult, ALU.add, r=[("U", ku), key, "PT"], w=[key])

            def s3():
                act(sgo, tgo, AF.Silu, r=[("tg", u)], w=[("sgf", u % 2)])

            def s4():
                P.add("pool", lambda e: e.tensor_tensor(go, tvo, sgo, ALU.mult), r=[("tv", u), ("sgf", u % 2)], w=[("yg", ti)])
            return [s1, s2, s3, s4]

        for l in range(L):
            do_norm(l, O_N1)

            P.mark("A L%d" % l)
            P.barrier()
            areset()
            s_ab = wload(w_in[l, :, 0:256])
            s_ac = wload(w_in[l, :, 256:512])
            s_ah = wload(w_in[l, :, 512:768])
            s_bu = wload(w_in[l, :, 768:1024])
            s_bg = wload(w_in[l, :, 1024:1280])
            pA = split3(af(2 * 2050), 2)
            pAs = af(2 * 4 * 34).rearrange("p (j b c) -> p j b c", j=2, b=4)
            hh = [af(512), af(512)]
            acc = [af(512), af(512)]
            ygrp = [split3(af(1024), 2), split3(af(1024), 2)]
            sq2 = split3(ab(1024), 2)
            sd = af(512)
            rstd = af(512)
            sostg = [af(256), af(256)]
            P.add("dve", lambda e, d=pA[:, :, 0:2]: e.memset(d, 0.0), w=[("pA", 0), ("pA", 1)])
            P.add("dve", lambda e, d=pAs[:, :, :, 0:2], s_=haloA[:, l, :, :].rearrange("p j (b c) -> p j b c", b=4): e.tensor_copy(d, s_),
                  r=["haloA"], w=[("pA", 0), ("pA", 1)])
            cnt = 0
            pipe = Pipe()
            for ti, (t0, n) in enumerate(TILES):
                kk = ti % 2
                for j in range(2):
                    u = cnt % 2
                    cnt += 1
                    bb, bc, bh = ps1(), ps1(), ps1()
                    if ti < 4:
                        data = pA[:, j, 2 + t0:2 + t0 + n]
                        sh = [pA[:, j, t0 + s:t0 + s + n] for s in range(3)]
                        pin, hin, ao, pbin, yo = ps[:, bc, 0:n], hh[u][:, 0:n], acc[u][:, 0:n], ps[:, bb, 0:n], ygrp[kk][:, j, 0:n]
                    else:
                        data = pAs[:, j, :, 2:34]
                        sh = [pAs[:, j, :, s:s + 32] for s in range(3)]
                        pin, hin, ao = split3(ps[:, bc, 0:n], 4), split3(hh[u][:, 0:n], 4), split3(acc[u][:, 0:n], 4)
                        pbin, yo = split3(ps[:, bb, 0:n], 4), split3(ygrp[kk][:, j, 0:n], 4)

                    def s0(j=j, ti=ti, t0=t0, n=n, bb=bb, bc=bc, bh=bh, u=u):
                        h_mm(s_ab, 128 * j, ti, t0, n, bb)
                        h_mm(s_ac, 128 * j, ti, t0, n, bc)
                        h_mm(s_ah, 128 * j, ti, t0, n, bh)
                        act(hh[u][:, 0:n], ps[:, bh, 0:n], AF.Copy, r=[pk(bh)], w=[("hh", u)])

                    def s1(j=j, n=n, u=u, kk=kk, bb=bb, bc=bc, data=data, sh=sh, pin=pin, hin=hin, ao=ao, pbin=pbin, yo=yo, l=l, sq2=sq2, ygrp=ygrp):
                        dve_tt(data, pin, hin, ALU.mult, r=[pk(bc), ("hh", u)], w=[("pA", j)])
                        dve_ts(ao, sh[2], pcol(l, O_CAW + 3 * j + 2), None, ALU.mult, None, r=[("pA", j), "PT"], w=[("acc", u)])
                        dve_stt(ao, sh[1], pcol(l, O_CAW + 3 * j + 1), ao, ALU.mult, ALU.add, r=[("pA", j), ("acc", u), "PT"], w=[("acc", u)])
                        dve_stt(ao, sh[0], pcol(l, O_CAW + 3 * j + 0), ao, ALU.mult, ALU.add, r=[("pA", j), ("acc", u), "PT"], w=[("acc", u)])
                        dve_tt(yo, pbin, ao, ALU.mult, r=[pk(bb), ("acc", u)], w=[("ygrp", kk)])
                        if j == 1:
                            gn_square(ygrp[kk], kk, n, sq2)
                    stages = [s0, s1]
                    if j == 1:
                        bgn = ps1()

                        def s2(l=l, kk=kk, ti=ti, t0=t0, n=n, bgn=bgn, ygrp=ygrp, sq2=sq2, sd=sd, rstd=rstd):
                            gn_rest(l, 0, ygrp[kk], kk, ti, t0, n, sq2, sd, rstd, bgn)
                        stages.append(s2)
                    pipe.item(stages)
            pipe.flush()
            state_out(l, o_ca, [[pA[:, j, 2048:2050] for j in range(2)]] + [[pAs[:, j, b, 32:34] for j in range(2)] for b in range(4)],
                      2, sostg, [("pA", 0), ("pA", 1)])

            P.mark("B L%d" % l)
            P.barrier()
            areset()
            pBl = split3(af(2 * 30), 2)
            pBs = af(2 * 4 * 62).rearrange("p (j b c) -> p j b c", j=2, b=4)
            pBb = split3(ab(2 * 2078), 2)
            pBbs = ab(2 * 4 * 62).rearrange("p (j b c) -> p j b c", j=2, b=4)
            sg = [af(512), af(512)]
            ygrp = [split3(af(1024), 2), split3(af(1024), 2)]
            ycb = split3(ab(1024), 2)
            sqb = split3(ab(1024), 2)
            sq2 = split3(ab(1024), 2)
            mt = af(512)
            msq = af(512)
            lsd = af(512)
            lrs = af(512)
            sd = af(512)
            rstd = af(512)
            dtmp = [af(512), msq]
            sostg = [af(256), af(256)]
            dslots = [(walloc(), walloc()) for j in range(2)]

            def dg(j, k, dslots=dslots):
                s = dslots[j][k // 16]
                return wsl[:, s, :, :].rearrange("p a b -> p (a b)")[:, 128 * (k % 16):128 * (k % 16) + 128]
            for j in range(2):
                for k in range(31):
                    P.add("dve", lambda e, d=dg(j, k), sc=pcol(l, O_CBW + 31 * j + k): e.tensor_scalar(d, ident_b[:], sc, None, ALU.mult),
                          r=["ident_b", "PT"], w=[("ws", dslots[j][k // 16])])
            P.add("dve", lambda e, d=pBb[:, :, 0:30]: e.memset(d, 0.0), w=[("pBb", 0), ("pBb", 1)])
            hB = haloB[:, l, :, :].rearrange("p j (b c) -> p j b c", b=4)
            P.add("dve", lambda e, d=pBs[:, :, :, 0:30], s_=hB: e.tensor_copy(d, s_), r=["haloB"], w=[("pB", 0), ("pB", 1)])
            P.add("dve", lambda e, d=pBbs[:, :, :, 0:30], s_=hB: e.tensor_copy(d, s_), r=["haloB"], w=[("pBb", 0), ("pBb", 1)])
            cnt = 0
            pipe = Pipe()
            for ti, (t0, n) in enumerate(TILES):
                kk = ti % 2
                for j in range(2):
                    u = cnt % 2
                    cnt += 1
                    bu, bg, bcv = ps1(), ps1(), ps1()
                    if ti < 4:
                        taps = [(dg(j, k), pBb[:, j, t0 + k:t0 + k + n]) for k in range(31)]
                        cout = ps[:, bcv, 0:n]
                    else:
                        taps = [(dg(j, k), pBbs[:, j, :, k:k + 32]) for k in range(31)]
                        cout = split3(ps[:, bcv, 0:n], 4)

                    def s0(j=j, ti=ti, t0=t0, n=n, bu=bu, bg=bg, u=u, sg=sg):
                        h_mm(s_bu, 128 * j, ti, t0, n, bu)
                        h_mm(s_bg, 128 * j, ti, t0, n, bg)
                        act(sg[u][:, 0:n], ps[:, bg, 0:n], AF.Sigmoid, r=[pk(bg)], w=[("sg", u)])

                    def s1(j=j, ti=ti, t0=t0, n=n, bu=bu, u=u, sg=sg, pBb=pBb, pBl=pBl, pBs=pBs, pBbs=pBbs):
                        if ti < 4:
                            dve_tt(pBb[:, j, 30 + t0:30 + t0 + n], ps[:, bu, 0:n], sg[u][:, 0:n], ALU.mult, r=[pk(bu), ("sg", u)], w=[("pBb", j)])
                            if ti == 3:
                                dve_tt(pBl[:, j, :], ps[:, bu, n - 30:n], sg[u][:, n - 30:n], ALU.mult, r=[pk(bu), ("sg", u)], w=[("pB", j)])
                        else:
                            dve_tt(pBs[:, j, :, 30:62], split3(ps[:, bu, 0:n], 4), split3(sg[u][:, 0:n], 4), ALU.mult, r=[pk(bu), ("sg", u)], w=[("pB", j)])
                            act(pBbs[:, j, :, 30:62], pBs[:, j, :, 30:62], AF.Copy, r=[("pB", j)], w=[("pBb", j)])

                    def s2(j=j, n=n, kk=kk, bcv=bcv, taps=taps, cout=cout, dslots=dslots, ygrp=ygrp, l=l, ycb=ycb, sqb=sqb):
                        mm_group(cout, taps, r=[("pBb", j), ("ws", dslots[j][0]), ("ws", dslots[j][1])], w=[pk(bcv)])
                        act(ygrp[kk][:, j, 0:n], ps[:, bcv, 0:n], AF.Identity, r=[pk(bcv), "PT"], w=[("ygrp", kk)], bias=pcol(l, O_CBB + j))
                        if j == 1:
                            act(ycb[:, :, 0:n], ygrp[kk][:, :, 0:n], AF.Copy, r=[("ygrp", kk)], w=["ycb"])
                            act(sqb[:, :, 0:n], ygrp[kk][:, :, 0:n], AF.Square, r=[("ygrp", kk)], w=["sqb"])
                    stages = [s0, s1, s2]
                    if j == 1:
                        b1, b2, bgn = ps1(), ps1(), ps1()

                        def s3(n=n, b1=b1, b2=b2, ycb=ycb, sqb=sqb, mt=mt, msq=msq, lsd=lsd, lrs=lrs):
                            mm_group(ps[:, b1, 0:n], [(ones_b[:], ycb[:, jj, 0:n]) for jj in range(2)], r=["ycb", "ones_b"], w=[pk(b1)])
                            mm_group(ps[:, b2, 0:n], [(ones_b[:], sqb[:, jj, 0:n]) for jj in range(2)], r=["sqb", "ones_b"], w=[pk(b2)])
                            act(mt[:, 0:n], ps[:, b1, 0:n], AF.Copy, r=[pk(b1)], w=["mt"], scale=1.0 / 256)
                            dve_tt(msq[:, 0:n], mt[:, 0:n], mt[:, 0:n], ALU.mult, r=["mt"], w=["msq"])
                            dve_stt(msq[:, 0:n], ps[:, b2, 0:n], 1.0 / 256, msq[:, 0:n], ALU.mult, ALU.subtract, r=[pk(b2), "msq"], w=["msq"])
                            act(lsd[:, 0:n], msq[:, 0:n], AF.Ln, r=["msq", "epsl"], w=["lsd"], bias=epsl[:, 0:1], scale=1.0)
                            act(lrs[:, 0:n], lsd[:, 0:n], AF.Exp, r=["lsd"], w=["lrs"], scale=-0.5)

                        def s4(n=n, kk=kk, ygrp=ygrp, mt=mt, lrs=lrs, dtmp=dtmp, l=l, sq2=sq2):
                            yk = ygrp[kk]
                            dk = ["dtmp0", "msq"]
                            for jj in range(2):
                                dve_tt(dtmp[jj][:, 0:n], yk[:, jj, 0:n], mt[:, 0:n], ALU.subtract, r=[("ygrp", kk), "mt"], w=[dk[jj]])
                                dve_tt(dtmp[jj][:, 0:n], dtmp[jj][:, 0:n], lrs[:, 0:n], ALU.mult, r=[dk[jj], "lrs"], w=[dk[jj]])
                            for jj in range(2):
                                act(yk[:, jj, 0:n], dtmp[jj][:, 0:n], AF.Silu, r=[dk[jj], "PT"], w=[("ygrp", kk)],
                                    bias=pcol(l, O_LNB + jj), scale=pcol(l, O_LNG + jj))
                            gn_square(yk, kk, n, sq2)

                        def s5(l=l, kk=kk, ti=ti, t0=t0, n=n, bgn=bgn, ygrp=ygrp, sq2=sq2, sd=sd, rstd=rstd):
                            gn_rest(l, 1, ygrp[kk], kk, ti, t0, n, sq2, sd, rstd, bgn)
                        stages += [s3, s4, s5]
                    pipe.item(stages)
            pipe.flush()
            state_out(l, o_cb, [[pBl[:, j, :] for j in range(2)]] + [[pBs[:, j, b, 32:62] for j in range(2)] for b in range(4)],
                      30, sostg, [("pB", 0), ("pB", 1)])

            P.mark("woutAB L%d" % l)
            wout_pair(l, 0)

            P.mark("C L%d" % l)
            P.barrier()
            areset()
            s_q = wload(w_in[l, :, 1280:1536])
            s_k = wload(w_in[l, :, 1536:1792])
            s_v = wload(w_in[l, :, 1792:2048])
            bT = af(4 * 2 * 128).rearrange("p (h r c) -> p h r c", h=4, r=2)
            cv = af(4)
            rec = [af(512), af(512)]
            ygrp = [split3(af(1024), 2), split3(af(1024), 2)]
            kvo = [af(512), af(512)]
            sq2 = split3(ab(1024), 2)
            sd = af(512)
            rstd = af(512)
            cmark = ar["off"]
            Q = [split3(ab(1024), 2), split3(ab(1024), 2)]
            Kr = split3(ab(2048), 2)
            Vr = ab(10 * 4 * 128).rearrange("p (s h c) -> p s h c", s=10, h=4)
            PTs = [ab(640), ab(640)]
            P.add("dve", lambda e, d=Vr[:, :, :, 64:128]: e.memset(d, 1.0), w=["Vr"])
            sp_dma(bT.rearrange("p h r c -> p (h r c)"), biasd[l].rearrange("p h r c -> p (h r c)"), r=(), w=["bT"])
            sp_dma(cv, cvecd[l], r=(), w=["cv"])
            for hd in range(4):
                dve_ts(bT[:, hd, :, :], bT[:, hd, :, :], cv[:, hd:hd + 1], None, ALU.subtract, None, r=["bT", "cv"], w=["bT"])
            pst["single_banks"] = [6, 7]
            ocnt = 0
            for ti in range(4):
                t0, n = TILES[ti]
                kk = ti % 2
                for j in range(2):
                    b = ps1()
                    h_mm(s_q, 128 * j, ti, t0, n, b)
                    act(Q[kk][:, j, :], ps[:, b, :], AF.Copy, r=[pk(b)], w=[("Q", kk)], scale=0.125)
                for j in range(2):
                    b = ps1()
                    h_mm(s_k, 128 * j, ti, t0, n, b)
                    c0 = ((4 * ti) % 8) * 128
                    act(Kr[:, j, c0:c0 + 512], ps[:, b, :], AF.Copy, r=[pk(b)], w=["Kr"])
                for blk in range(4):
                    tb = 4 * ti + blk
                    b = ps1()
                    vsl = wslot(s_v)
                    ksl = wslot(s_k)
                    if ti == 3:
                        mm_group(ps[:, b, 0:256], [(h[:, kc, tb * 128:tb * 128 + 128], ksl[:, kc, :]) for kc in range(8)],
                                 r=[("ws", s_k), ("h", ti)], w=[pk(b)])
                    mm_group(ps[:, b, 256:512], [(h[:, kc, tb * 128:tb * 128 + 128], vsl[:, kc, :]) for kc in range(8)],
                             r=[("ws", s_v), ("h", ti)], w=[pk(b)])
                    act(Vr[:, tb % 10, :, 0:64], split3(ps[:, b, 256:512], 4), AF.Copy, r=[pk(b)], w=["Vr"])
                    if ti == 3:
                        ko = ocnt % 2
                        ocnt += 1
                        act(kvo[ko][:, :], ps[:, b, :], AF.Copy, r=[pk(b)], w=[("kvo", ko)])
                        sp_dma(o_k[l, blk * 128:blk * 128 + 128, :], kvo[ko][:, 0:256], r=[("kvo", ko)], w=())
                        sp_dma(o_v[l, blk * 128:blk * 128 + 128, :], kvo[ko][:, 256:512], r=[("kvo", ko)], w=())
                if ti == 0:
                    pipe = Pipe()
                for qb in range(4):
                    i = 4 * ti + qb
                    r0 = max(0, 4 - i)
                    bo = 4 + (qb % 2)
                    ro = qb % 2
                    for hd in range(4):
                        j, pb = hd // 2, 64 * (hd % 2)
                        b2 = ps2()
                        S = ps[:, b2:b2 + 2, :].rearrange("p a b -> p (a b)")
                        qv = Q[kk][pb:pb + 64, j, qb * 128:qb * 128 + 128]
                        pu = hd % 2
                        O = ps[:, bo, hd * 128:hd * 128 + 128]
                        skeys = [pk(b2), pk(b2 + 1)]

                        def s0(S=S, qv=qv, i=i, r0=r0, j=j, pb=pb, kk=kk, skeys=skeys, Kr=Kr):
                            for r_ in range(r0, 5):
                                c0 = ((i - 4 + r_) % 8) * 128
                                mm_group(S[:, r_ * 128:r_ * 128 + 128], [(Kr[pb:pb + 64, j, c0:c0 + 128], qv)], r=["Kr", ("Q", kk)], w=skeys)

                        def s1(S=S, r0=r0, hd=hd, skeys=skeys, bT=bT):
                            r3 = max(r0, 3)
                            dve_tt(S[:, r3 * 128:640], S[:, r3 * 128:640], bT[:, hd, r3 - 3:2, :].rearrange("p r c -> p (r c)"), ALU.add,
                                   r=skeys + ["bT"], w=skeys)
                            if r0 == 0:
                                dve_ts(S[0:64, 64:128], S[0:64, 64:128], -1e30, None, ALU.add, None, r=skeys, w=skeys)
                            dve_ts(S[64:128, 512:576], S[64:128, 512:576], -1e30, None, ALU.add, None, r=skeys, w=skeys)

                        def s2(S=S, r0=r0, hd=hd, pu=pu, skeys=skeys, PTs=PTs, cv=cv):
                            act(PTs[pu][:, r0 * 128:640], S[:, r0 * 128:640], AF.Exp, r=skeys, w=[("PTs", pu)])

                        def s3(O=O, i=i, r0=r0, hd=hd, pu=pu, bo=bo, Vr=Vr, PTs=PTs):
                            mm_group(O, [(Vr[:, (i - 4 + r_) % 10, hd, :], PTs[pu][:, r_ * 128:r_ * 128 + 128]) for r_ in range(r0, 5)],
                                     r=["Vr", ("PTs", pu)], w=[pk(bo)])
                        stages = [s0, s1, s2, s3]
                        if hd == 3:
                            def s4(bo=bo, ro=ro, rec=rec):
                                dve_recip(rec[ro][64:128, :], ps[64:128, bo, :], r=[pk(bo)], w=[("rec", ro)])

                            def s5(bo=bo, ro=ro, rec=rec, kk=kk, qb=qb, ygrp=ygrp):
                                oin = ps[0:64, bo, :].rearrange("p (h c) -> p h c", h=4)
                                rin = rec[ro][64:128, :].rearrange("p (h c) -> p h c", h=4)
                                for par in range(2):
                                    for jj in range(2):
                                        hh_ = 2 * jj + par
                                        dve_tt(ygrp[kk][64 * par:64 * par + 64, jj, qb * 128:qb * 128 + 128], oin[:, hh_, :], rin[:, hh_, :], ALU.mult,
                                               r=[pk(bo), ("rec", ro)], w=[("ygrp", kk)])
                            stages += [s4, s5]
                            if qb == 3:
                                bgn = ps1()

                                def s6(kk=kk, n=n, ygrp=ygrp, sq2=sq2):
                                    gn_square(ygrp[kk], kk, n, sq2)

                                def s7(l=l, kk=kk, ti=ti, t0=t0, n=n, ygrp=ygrp, sq2=sq2, sd=sd, rstd=rstd, bgn=bgn):
                                    gn_rest(l, 2, ygrp[kk], kk, ti, t0, n, sq2, sd, rstd, bgn)
                                stages += [s6, s7]
                        pipe.item(stages)
            pipe.flush()
            P.mark("Csample L%d" % l)
            pst["single_banks"] = [4, 5, 6, 7]
            P.barrier()
            ar["off"] = cmark
            stk = af(1024)
            stv = af(1024)
            kcT = split3(ab(1024), 2)
            Vc = ab(4 * 4 * 128).rearrange("p (s h c) -> p s h c", s=4, h=4)
            Vs = ab(4 * 128).rearrange("p (h c) -> p h c", h=4)
            Ks = split3(ab(256), 2)
            Qs = split3(ab(256), 2)
            PTa = ab(640).rearrange("p (h r c) -> p h r c", h=4, r=5)
            P.add("dve", lambda e, d=Vc[:, :, :, 64:128]: e.memset(d, 1.0), w=["Vc"])
            P.add("dve", lambda e, d=Vs[:, :, 64:128]: e.memset(d, 1.0), w=["Vs"])
            ti = 4
            t0, n = TILES[4]
            kk = 0
            for j in range(2):
                b = ps1()
                h_mm(s_q, 128 * j, ti, t0, n, b)
                act(Qs[:, j, :], ps[:, b, 0:128], AF.Copy, r=[pk(b)], w=["Qs"], scale=0.125)
                b = ps1()
                h_mm(s_k, 128 * j, ti, t0, n, b)
                act(Ks[:, j, :], ps[:, b, 0:128], AF.Copy, r=[pk(b)], w=["Ks"])
            pipe = Pipe()
            for bi in range(4):
                tb0 = TP + 32 * bi
                bkv = ps1()
                bt0, bt1 = ps1(), ps1()
                b2 = ps2()
                bo = 4 + bi % 2 if False else ps1()
                ro = bi % 2
                ko = bi % 2

                def t0_(bi=bi, stk=stk, stv=stv):
                    sp_dma(split3(stk, 4), cak[l, bi].rearrange("(s p) c -> p s c", p=128), r=(), w=["stk"])
                    sp_dma(split3(stv, 4), cav[l, bi].rearrange("(s p) c -> p s c", p=128), r=(), w=["stv"])

                def t1_(bi=bi, tb0=tb0, bkv=bkv, bt=(bt0, bt1), ko=ko, stk=stk, stv=stv, kcT=kcT, Vc=Vc, Vs=Vs, kvo=kvo):
                    mm_group(ps[0:32, bkv, 0:256], [(h[:, kc, tb0:tb0 + 32], wslot(s_k)[:, kc, :]) for kc in range(8)], r=[("ws", s_k), ("h", 4)], w=[pk(bkv)])
                    mm_group(ps[0:32, bkv, 256:512], [(h[:, kc, tb0:tb0 + 32], wslot(s_v)[:, kc, :]) for kc in range(8)], r=[("ws", s_v), ("h", 4)], w=[pk(bkv)])
                    act(Vs[0:32, :, 0:64], split3(ps[0:32, bkv, 256:512], 4), AF.Copy, r=[pk(bkv)], w=["Vs"])
                    act(kvo[ko][0:32, :], ps[0:32, bkv, :], AF.Copy, r=[pk(bkv)], w=[("kvo", ko)])
                    sp_dma(o_k[l, 512 + 32 * bi:512 + 32 * bi + 32, :], kvo[ko][0:32, 0:256], r=[("kvo", ko)], w=())
                    sp_dma(o_v[l, 512 + 32 * bi:512 + 32 * bi + 32, :], kvo[ko][0:32, 256:512], r=[("kvo", ko)], w=())
                    for j in range(2):
                        b = bt[j]
                        for s_ in range(4):
                            P.add("pe", lambda e, b=b, s_=s_, j=j, stk=stk: e.transpose(ps[:, b, 128 * s_:128 * s_ + 128], stk[:, 256 * s_ + 128 * j:256 * s_ + 128 * j + 128], ident_f[:]),
                                  r=["stk", "ident_f"], w=[pk(b)])
                        act(kcT[:, j, :], ps[:, b, :], AF.Copy, r=[pk(b)], w=["kcT"])
                    P.add("dve", lambda e, d=Vc.rearrange("p s h c -> p (s h) c")[:, :, 0:64], s_=stv.rearrange("p (a c) -> p a c", c=64): e.tensor_copy(d, s_),
                          r=["stv"], w=["Vc"])

                def t2_(bi=bi, b2=b2, bo=bo, ro=ro, kcT=kcT, Ks=Ks, Qs=Qs, Vc=Vc, Vs=Vs, PTa=PTa, bT=bT, rec=rec, ygrp=ygrp):
                    S = ps[:, b2:b2 + 2, :].rearrange("p a b -> p (a b)")[:, 0:640].rearrange("p (h r c) -> p h r c", h=4, r=5)
                    skeys = [pk(b2), pk(b2 + 1)]
                    for hd in range(4):
                        j, pb = hd // 2, 64 * (hd % 2)
                        qv = Qs[pb:pb + 64, j, 32 * bi:32 * bi + 32]
                        for r_ in range(4):
                            mm_group(S[:, hd, r_, :], [(kcT[pb:pb + 64, j, r_ * 128:r_ * 128 + 128], qv)], r=["kcT", "Qs"], w=skeys)
                        mm_group(S[0:32, hd, 4, :], [(Ks[pb:pb + 64, j, 32 * bi:32 * bi + 32], qv)], r=["Ks", "Qs"], w=skeys)
                    dve_tt(S[:, :, 3, :], S[:, :, 3, :], bT[:, :, 0, 0:32], ALU.add, r=skeys + ["bT"], w=skeys)
                    dve_tt(S[0:32, :, 4, :], S[0:32, :, 4, :], bT[0:32, :, 1, 0:32], ALU.add, r=skeys + ["bT"], w=skeys)
                    Sx = ps[:, b2:b2 + 2, :].rearrange("p a b -> p (a b)")[:, 0:640].rearrange("p (h x) -> p h x", h=4)
                    Px = PTa.rearrange("p h r c -> p h (r c)")
                    act(Px[:, :, 0:128], Sx[:, :, 0:128], AF.Exp, r=skeys, w=["PTa"])
                    act(Px[0:32, :, 128:160], Sx[0:32, :, 128:160], AF.Exp, r=skeys, w=["PTa"])
                    for hd in range(4):
                        O = ps[:, bo, hd * 128:hd * 128 + 32]
                        pairs = [(Vc[:, r_, hd, :], PTa[:, hd, r_, :]) for r_ in range(4)] + [(Vs[0:32, hd, :], PTa[0:32, hd, 4, :])]
                        mm_group(O, pairs, r=["Vc", "Vs", "PTa"], w=[pk(bo)])
                    o4 = ps[:, bo, :].rearrange("p (h c) -> p h c", h=4)
                    r4 = rec[ro].rearrange("p (h c) -> p h c", h=4)
                    dve_recip(r4[64:128, :, 0:32], o4[64:128, :, 0:32], r=[pk(bo)], w=[("rec", ro)])
                    for par in range(2):
                        for j in range(2):
                            hd = 2 * j + par
                            dve_tt(ygrp[0][64 * par:64 * par + 64, j, 32 * bi:32 * bi + 32], o4[0:64, hd, 0:32], r4[64:128, hd, 0:32], ALU.mult,
                                   r=[pk(bo), ("rec", ro)], w=[("ygrp", 0)])
                pipe.item([t0_, t1_, t2_])
            pipe.flush()
            group_norm(l, 2, ygrp[kk], kk, ti, t0, n, sq2, sd, rstd)
            pst["single_banks"] = list(range(8))

            P.mark("M L%d" % l)
            P.barrier()
            areset()
            s_m = wload(w_in[l, :, 2048:2304])
            s_mk = wload(w_mkv[l, :, 0:256])
            s_mv = wload(w_mkv[l, :, 256:512])
            wo_cm = wout_load(l, 1)
            Q = [split3(ab(1024), 2), split3(ab(1024), 2)]
            mkT = split3(ab(512), 2)
            mva = ab(2 * 4 * 128).rearrange("p (s h c) -> p s h c", s=2, h=4)
            PTs = [split3(ab(1024), 2), split3(ab(1024), 2)]
            rec = [af(512), af(512)]
            ygrp = [split3(af(1024), 2), split3(af(1024), 2)]
            mo = [af(512), af(512)]
            sq2 = split3(ab(1024), 2)
            sd = af(512)
            rstd = af(512)
            stk = af(512)
            stv = af(512)
            mkTs = split3(ab(512), 2)
            mvs = ab(2 * 4 * 128).rearrange("p (s h c) -> p s h c", s=2, h=4)
            Qs = split3(ab(256), 2)
            PTa = ab(64)
            P.add("dve", lambda e, d=mva[:, :, :, 64:128]: e.memset(d, 1.0), w=["mva"])
            P.add("dve", lambda e, d=mvs[:, :, :, 64:128]: e.memset(d, 1.0), w=["mvs"])
            pst["single_banks"] = [4, 5, 6, 7]
            ocnt = 0
            for blk in range(2):
                b = ps1()
                mm_group(ps[:, b, 0:256], [(memT[:, kc, blk * 128:blk * 128 + 128], wslot(s_mk)[:, kc, :]) for kc in range(8)], r=[("ws", s_mk), "memT"], w=[pk(b)])
                mm_group(ps[:, b, 256:512], [(memT[:, kc, blk * 128:blk * 128 + 128], wslot(s_mv)[:, kc, :]) for kc in range(8)], r=[("ws", s_mv), "memT"], w=[pk(b)])
                act(mva[:, blk, :, 0:64], split3(ps[:, b, 256:512], 4), AF.Copy, r=[pk(b)], w=["mva"])
                ko = ocnt % 2
                ocnt += 1
                act(mo[ko][:, :], ps[:, b, :], AF.Copy, r=[pk(b)], w=[("mo", ko)])
                sp_dma(o_mk[l, blk * 128:blk * 128 + 128, :], mo[ko][:, 0:256], r=[("mo", ko)], w=())
                sp_dma(o_mv[l, blk * 128:blk * 128 + 128, :], mo[ko][:, 256:512], r=[("mo", ko)], w=())
            for j in range(2):
                b = ps1()
                mm_group(ps[:, b, 0:256], [(wslot(s_mk)[:, kc, 128 * j:128 * j + 128], memT[:, kc, :]) for kc in range(8)], r=[("ws", s_mk), "memT"], w=[pk(b)])
                act(mkT[:, j, :], ps[:, b, 0:256], AF.Copy, r=[pk(b)], w=["mkT"])
            pipe = Pipe()
            obank = 0
            for ti in range(4):
                t0, n = TILES[ti]
                kk = ti % 2
                for hd in range(4):
                    j, pb = hd // 2, 64 * (hd % 2)
                    b2 = ps2()
                    pu = hd % 2
                    bo = 4 + (obank % 3)
                    obank += 1
                    ro = hd % 2

                    def s0(hd=hd, j=j, pb=pb, b2=b2, kk=kk, ti=ti, t0=t0, n=n, Q=Q, mkT=mkT):
                        if hd == 0:
                            for jj in range(2):
                                h_mm(s_m, 128 * jj, ti, t0, n, 7)
                                act(Q[kk][:, jj, :], ps[:, 7, :], AF.Copy, r=[pk(7)], w=[("Q", kk)], scale=0.125)
                        for r_ in range(2):
                            mm_group(ps[:, b2 + r_, :], [(mkT[pb:pb + 64, j, r_ * 128:r_ * 128 + 128], Q[kk][pb:pb + 64, j, :])],
                                     r=["mkT", ("Q", kk)], w=[pk(b2 + r_)])

                    def s1(b2=b2, pu=pu, PTs=PTs):
                        act(PTs[pu][:, :, :], ps[:, b2:b2 + 2, :], AF.Exp, r=[pk(b2), pk(b2 + 1)], w=[("PTs", pu)])

                    def s2(bo=bo, pu=pu, hd=hd, mva=mva, PTs=PTs):
                        mm_group(ps[:, bo, :], [(mva[:, r_, hd, :], PTs[pu][:, r_, :]) for r_ in range(2)], r=["mva", ("PTs", pu)], w=[pk(bo)])

                    def s3(bo=bo, ro=ro, rec=rec):
                        dve_recip(rec[ro][64:128, :], ps[64:128, bo, :], r=[pk(bo)], w=[("rec", ro)])

                    def s4(bo=bo, ro=ro, rec=rec, kk=kk, j=j, pb=pb, ygrp=ygrp, hd=hd, n=n, sq2=sq2):
                        dve_tt(ygrp[kk][pb:pb + 64, j, :], ps[0:64, bo, :], rec[ro][64:128, :], ALU.mult, r=[pk(bo), ("rec", ro)], w=[("ygrp", kk)])
                        if hd == 3:
                            gn_square(ygrp[kk], kk, n, sq2)
                    stages = [s0, s1, s2, s3, s4]
                    if hd == 3:
                        def s5(l=l, kk=kk, ti=ti, t0=t0, n=n, ygrp=ygrp, sq2=sq2, sd=sd, rstd=rstd):
                            gn_rest(l, 3, ygrp[kk], kk, ti, t0, n, sq2, sd, rstd, 7)
                        stages.append(s5)
                    pipe.item(stages)
            pipe.flush()
            P.mark("Msample L%d" % l)
            ti = 4
            t0, n = TILES[4]
            kk = 0
            for j in range(2):
                b = ps1()
                h_mm(s_m, 128 * j, ti, t0, n, b)
                act(Qs[:, j, :], ps[:, b, 0:128], AF.Copy, r=[pk(b)], w=["Qs"], scale=0.125)
            for bi in range(4):
                sp_dma(split3(stk, 2), cmk[l, bi].rearrange("(s p) c -> p s c", p=128), r=(), w=["stk"])
                sp_dma(split3(stv, 2), cmv[l, bi].rearrange("(s p) c -> p s c", p=128), r=(), w=["stv"])
                b = ps1()
                for j in range(2):
                    for s_ in range(2):
                        P.add("pe", lambda e, b=b, s_=s_, j=j, stk=stk: e.transpose(ps[:, b, 256 * j + 128 * s_:256 * j + 128 * s_ + 128],
                                                                            stk[:, 256 * s_ + 128 * j:256 * s_ + 128 * j + 128], ident_f[:]),
                              r=["stk", "ident_f"], w=[pk(b)])
                act(mkTs[:, :, :], split3(ps[:, b, :], 2), AF.Copy, r=[pk(b)], w=["mkTs"])
                P.add("dve", lambda e, d=mvs.rearrange("p s h c -> p (s h) c")[:, :, 0:64], s_=stv.rearrange("p (a c) -> p a c", c=64): e.tensor_copy(d, s_),
                      r=["stv"], w=["mvs"])
                bo = ps1()
                ro = bi % 2
                for hd in range(4):
                    j, pb = hd // 2, 64 * (hd % 2)
                    b2 = ps2()
                    S = ps[:, b2, 0:64].rearrange("p (r c) -> p r c", r=2)
                    qv = Qs[pb:pb + 64, j, 32 * bi:32 * bi + 32]
                    for r_ in range(2):
                        mm_group(S[:, r_, :], [(mkTs[pb:pb + 64, j, r_ * 128:r_ * 128 + 128], qv)], r=["mkTs", "Qs"], w=[pk(b2)])
                    act(split3(PTa, 2), S, AF.Exp, r=[pk(b2)], w=["PTa"])
                    O = ps[:, bo, hd * 128:hd * 128 + 32]
                    mm_group(O, [(mvs[:, r_, hd, :], PTa[:, 32 * r_:32 * r_ + 32]) for r_ in range(2)], r=["mvs", "PTa"], w=[pk(bo)])
                o4 = ps[:, bo, :].rearrange("p (h c) -> p h c", h=4)
                r4 = rec[ro].rearrange("p (h c) -> p h c", h=4)
                dve_recip(r4[64:128, :, 0:32], o4[64:128, :, 0:32], r=[pk(bo)], w=[("rec", ro)])
                for par in range(2):
                    for j in range(2):
                        hd = 2 * j + par
                        dve_tt(ygrp[kk][64 * par:64 * par + 64, j, 32 * bi:32 * bi + 32], o4[0:64, hd, 0:32], r4[64:128, hd, 0:32], ALU.mult,
                               r=[pk(bo), ("rec", ro)], w=[("ygrp", kk)])
                wout_tile(l, 1, wo_cm, bi)
            group_norm(l, 3, ygrp[kk], kk, ti, t0, n, sq2, sd, rstd)
            pst["single_banks"] = list(range(8))

            P.mark("woutCM L%d" % l)
            wout_tile(l, 1, wo_cm, 4)

            P.mark("FFN L%d" % l)
            do_norm(l, O_N2)
            P.barrier()
            areset()
            U = [af(2 * 514).rearrange("p (a b) -> p a b", a=2) for _ in range(3)]
            fstg = [af(512), af(512)]
            tv = [af(512) for _ in range(4)]
            tg = [af(512) for _ in range(4)]
            sgf = [af(512), af(512)]
            ucnt = 0
            vcnt = 0
            fq = 0
            fpipe = Pipe(first0=True)
            for (c_start, c_n) in FPARTS:
                npair = c_n // 2
                pslots = []

                def ldp(pi, c_start=c_start):
                    cc = c_start // 2 + pi
                    pslots.append((wload(w_up[l, :, 256 * cc:256 * cc + 256]), wload(w_up[l, :, 2816 + 256 * cc:2816 + 256 * cc + 256])))
                ldp(0)
                dsl = []

                def ldd(dp, c_start=c_start, c_n=c_n, dsl=dsl):
                    dsl.append(wload(w_down[l, 128 * c_start:128 * (c_start + c_n), 256 * dp:256 * dp + 256], nk=c_n))
                for pi in range(npair):
                    if pi + 1 < npair:
                        ldp(pi + 1)
                    if pi == 0:
                        for dp in range(3):
                            ldd(dp)
                    if pi == 1:
                        ldd(3)
                    sv_, sg_ = pslots[pi]
                    for sub in range(2):
                        c = c_start + 2 * pi + sub
                        cl = c - c_start
                        for ti, (t0, n) in enumerate(TILES):
                            ku = ucnt % 3
                            kprev = (ucnt - 1) % 3
                            ucnt += 1
                            u = vcnt % 4
                            vcnt += 1
                            fpipe.item(ffn_item(l, c, cl, sub, sv_, sg_, ti, t0, n, ku, kprev, u, U, tv, tg, sgf))
                if npair == 1:
                    ldd(3)
                fpipe.flush()
                for ti, (t0, n) in enumerate(TILES):
                    for dc in range(8):
                        ws_ = wslot(dsl[dc // 2])
                        dsub = dc % 2
                        b = ps1()
                        mm_group(ps[:, b, 0:n], [(ws_[:, kc, 128 * dsub:128 * dsub + 128], yg[:, kc, t0:t0 + n]) for kc in range(c_n)],
                                 r=[("ws", dsl[dc // 2]), ("yg", ti)], w=[pk(b)], arena=False)
                        dve_tt(x[:, dc, t0:t0 + n], x[:, dc, t0:t0 + n], ps[:, b, 0:n], ALU.add, r=[pk(b), xk(dc, ti)], w=[xk(dc, ti)], arena=False)
                ocf2 = o_cf[l].rearrange("s r c -> (s r) c")
                for half in range(2):
                    b = ps1()
                    for ci in range(c_n):
                        cc_ = 22 * half + c_start + ci
                        P.add("pe", lambda e, b=b, ci=ci, src=stF[:, cc_, :, :].rearrange("p s r -> p (s r)"): e.transpose(ps[0:10, b, 128 * ci:128 * ci + 128], src, ident_f[:]),
                              r=[("stF", cc_), "ident_f"], w=[pk(b)])
                    k = fq % 2
                    fq += 1
                    act(fstg[k][0:10, 0:128 * c_n], ps[0:10, b, 0:128 * c_n], AF.Copy, r=[pk(b)], w=[("sostg", k)])
                    sp_dma(ocf2[:, 2816 * half + 128 * c_start:2816 * half + 128 * (c_start + c_n)], fstg[k][0:10, 0:128 * c_n], r=[("sostg", k)], w=())

        P.mark("epilogue")
        P.barrier()
        areset()
        fg = af(1024)
        ost = [af(1024), af(1024)]
        xsq = af(512)
        ssum = [af(2), af(2)]
        sdv = [af(1), af(1)]
        rsv = [af(1), af(1)]
        sp_dma(fg, fgbc, r=(), w=["fg"])
        pipe = Pipe()
        for blk in range(17):
            ti = min(blk // 4, 4)
            k = blk % 2
            banks = [ps1(), ps1()]

            def s0(blk=blk, ti=ti, k=k, banks=banks):
                P.add("dve", lambda e, d=ssum[k]: e.memset(d, 0.0), w=[("ssum", k)])
                for half in range(2):
                    b = banks[half]
                    for ci in range(4):
                        c = half * 4 + ci
                        P.add("pe", lambda e, b=b, c=c, ci=ci, blk=blk: e.transpose(ps[:, b, 128 * ci:128 * ci + 128], x[:, c, blk * 128:blk * 128 + 128], ident_f[:]),
                              r=[xk(c, ti), "ident_f"], w=[pk(b)])
                    P.add("act", lambda e, b=b, half=half, k=k: e.activation(xsq[:, :], ps[:, b, :], AF.Square, accum_out=ssum[k][:, half:half + 1]),
                          r=[pk(b)], w=["xsq", ("ssum", k)])

            def s1(k=k):
                dve_tt(ssum[k][:, 0:1], ssum[k][:, 0:1], ssum[k][:, 1:2], ALU.add, r=[("ssum", k)], w=[("ssum", k)])
                act(sdv[k], ssum[k][:, 0:1], AF.Ln, r=[("ssum", k), "epsr"], w=[("sdv", k)], bias=epsr[:, 0:1], scale=1.0 / 1024)
                act(rsv[k], sdv[k], AF.Exp, r=[("sdv", k)], w=[("rsv", k)], scale=-0.5)

            def s2(blk=blk, k=k, banks=banks):
                for half in range(2):
                    b = banks[half]
                    dve_stt(ost[k][:, 512 * half:512 * half + 512], ps[:, b, :], rsv[k][:, 0:1], fg[:, 512 * half:512 * half + 512], ALU.mult, ALU.mult,
                            r=[pk(b), ("rsv", k), "fg"], w=[("ost", k)])
                sp_dma(y_o[blk * 128:blk * 128 + 128, :], ost[k], r=[("ost", k)], w=())
            pipe.item([s0, s1, s2])
        pipe.flush()

        P.finalize()
        _CACHE['prog'] = P
        with nc.Block() as block:
            P.emit(nc, block, sems, dsems)
    return nc


_CACHE = {}


def _prep_inputs(inp):
    f = lambda a: np.ascontiguousarray(np.asarray(a, dtype=np.float32))
    x_prompt, x_sample = f(inp["x_prompt"]), f(inp["x_sample"])
    rows = []
    for l in range(L):
        rows.append(f(inp["norm1_g"])[l].reshape(8, 128))
        rows.append(f(inp["norm2_g"])[l].reshape(8, 128))
        rows.append(f(inp["grp_norm_g"])[l].reshape(8, 128))
        caw = f(inp["conv_a_w"])[l].reshape(3, 2, 128).transpose(1, 0, 2).reshape(6, 128)
        rows.append(caw)
        cbw = f(inp["conv_b_w"])[l].reshape(31, 2, 128).transpose(1, 0, 2).reshape(62, 128)
        rows.append(cbw)
        rows.append(f(inp["conv_b_bias"])[l].reshape(2, 128))
        rows.append(f(inp["ln_b_g"])[l].reshape(2, 128))
        rows.append(f(inp["ln_b_b"])[l].reshape(2, 128))
        fcw = f(inp["ffn_conv_w"])[l].reshape(3, 44, 128).transpose(1, 0, 2).reshape(132, 128)
        rows.append(fcw)
        rows.append(f(inp["ffn_conv_b"])[l].reshape(44, 128))
    params = np.concatenate(rows, axis=0)
    assert params.shape[0] == 2 * PL
    params = np.concatenate([params, np.zeros((NPR - params.shape[0], 128), np.float32)], axis=0)
    rb = f(inp["rel_bias"])
    kk = np.arange(128)[:, None, None]
    r = np.arange(2)[None, :, None] + 3
    q = np.arange(128)[None, None, :]
    idx = np.clip(q + 128 * (4 - r) - kk, -128, 128) + 128
    biasd = np.ascontiguousarray(rb[:, :, idx].transpose(0, 2, 1, 3, 4))
    cvecd = np.ascontiguousarray(np.broadcast_to(rb[:, None, :, 256], (L, 128, 4)))
    fgbc = np.ascontiguousarray(np.broadcast_to(f(inp["final_g"])[None, :], (128, 1024)))
    shared = dict(w_in=f(inp["w_in"]), w_mkv=f(inp["w_mem_kv"]), w_out=f(inp["w_out"]), w_up=f(inp["w_up"]), w_down=f(inp["w_down"]),
                  params=params, biasd=biasd, cvecd=cvecd, fgbc=fgbc)
    cca, ccb, ccf = f(inp["cache_conv_a"]), f(inp["cache_conv_b"]), f(inp["cache_ffn_conv"])
    cak, cav = f(inp["cache_attn_k"]), f(inp["cache_attn_v"])
    cmk, cmv = f(inp["cache_mem_k"]), f(inp["cache_mem_v"])
    memp = f(inp["mem_prompt"])
    maps = []
    for i in range(8):
        sl = slice(4 * i, 4 * i + 4)
        m = dict(shared)
        m["xin"] = np.ascontiguousarray(np.concatenate([x_prompt[i], x_sample[sl].reshape(128, 1024)], axis=0))
        m["memp"] = np.ascontiguousarray(memp[i])
        m["cca"] = np.ascontiguousarray(cca[:, sl].reshape(L, 8, 256))
        m["ccb"] = np.ascontiguousarray(ccb[:, sl].reshape(L, 120, 256))
        m["ccf"] = np.ascontiguousarray(ccf[:, sl].reshape(L, 8, 5632))
        m["cak"] = np.ascontiguousarray(cak[:, sl].reshape(L, 4, 512, 256))
        m["cav"] = np.ascontiguousarray(cav[:, sl].reshape(L, 4, 512, 256))
        m["cmk"] = np.ascontiguousarray(cmk[:, sl].reshape(L, 4, 256, 256))
        m["cmv"] = np.ascontiguousarray(cmv[:, sl].reshape(L, 4, 256, 256))
        maps.append(m)
    return maps


def kernel(**inputs):
    if "nc" not in _CACHE:
        _CACHE["nc"] = build_program()
    nc = _CACHE["nc"]
    maps = _prep_inputs(inputs)
    res = run_bass_kernel_spmd(nc, maps, core_ids=list(range(8)))
    R = res.results
    y_prompt = np.stack([R[i]["y_o"][0:2048] for i in range(8)], 0)
    y_sample = np.concatenate([R[i]["y_o"][2048:].reshape(4, 32, 1024) for i in range(8)], 0)

    def gather(name, shape_tail):
        pr = np.stack([R[i][name][:, 0] for i in range(8)], 1)
        sa = np.concatenate([R[i][name][:, 1:5] for i in range(8)], 1)
        return pr, sa
    p_ca, s_ca = gather("o_ca", None)
    p_cb, s_cb = gather("o_cb", None)
    p_cf, s_cf = gather("o_cf", None)
    p_k = np.stack([R[i]["o_k"][:, 0:512].reshape(L, 512, 4, 64) for i in range(8)], 1)
    p_v = np.stack([R[i]["o_v"][:, 0:512].reshape(L, 512, 4, 64) for i in range(8)], 1)
    s_k = np.concatenate([R[i]["o_k"][:, 512:].reshape(L, 4, 32, 4, 64) for i in range(8)], 1)
    s_v = np.concatenate([R[i]["o_v"][:, 512:].reshape(L, 4, 32, 4, 64) for i in range(8)], 1)
    p_mk = np.stack([R[i]["o_mk"].reshape(L, 256, 4, 64) for i in range(8)], 1)
    p_mv = np.stack([R[i]["o_mv"].reshape(L, 256, 4, 64) for i in range(8)], 1)
    outs = (y_prompt, y_sample, p_ca, p_cb, p_cf, p_k, p_v, p_mk, p_mv, s_ca, s_cb, s_cf, s_k, s_v)
    return tuple(np.ascontiguousarray(o, dtype=np.float32) for o in outs)
```
